# Optimizing a Trainium2 kernel written in Bass

```python
import jax, jax.numpy as jnp
from jax import lax
import numpy as np

D_MODEL = 2048
BATCH = 16
SEQ = 2048
DEPTH = 2

CHUNK = 128
GM_WIDTH = D_MODEL // 4
GM_HEAD_DIM = 128
GM_HEADS = GM_WIDTH // GM_HEAD_DIM
ATT_WIDTH = D_MODEL // 4
ATT_HEAD_DIM = 64
ATT_HEADS = ATT_WIDTH // ATT_HEAD_DIM
ATT_KV_HEADS = ATT_HEADS // 4
KV_WIDTH = ATT_KV_HEADS * ATT_HEAD_DIM
WINDOW = 128
SSM_WIDTH = D_MODEL // 2
SSM_HEAD_DIM = 64
SSM_HEADS = SSM_WIDTH // SSM_HEAD_DIM
SSM_GROUPS = 2
SSM_STATE = 128
CONV_WIDTH = 4
BC_WIDTH = SSM_GROUPS * SSM_STATE
CONV_CH = SSM_WIDTH + 2 * BC_WIDTH
MIX_WIDTH = GM_WIDTH + ATT_WIDTH + SSM_WIDTH
IN_SIZES = (GM_WIDTH, GM_WIDTH, ATT_WIDTH, KV_WIDTH, KV_WIDTH, SSM_WIDTH, CONV_CH, SSM_HEADS)
IN_WIDTH = 2 * GM_WIDTH + ATT_WIDTH + 2 * KV_WIDTH + SSM_WIDTH + CONV_CH + SSM_HEADS
D_FF = 4 * D_MODEL
NEG_INF = -1e30
EPS = 1e-6

kernel_name = 'hybrid_gmlp_swa_ssd_parallel_heads'


def rms_norm(x, g):
    xf = x.astype(jnp.float32)
    y = xf * lax.rsqrt(jnp.mean(xf * xf, axis=-1, keepdims=True) + EPS)
    return (y * g.astype(jnp.float32)).astype(x.dtype)


def layer_norm(x, g, b):
    xf = x.astype(jnp.float32)
    mu = jnp.mean(xf, axis=-1, keepdims=True)
    xc = xf - mu
    y = xc * lax.rsqrt(jnp.mean(xc * xc, axis=-1, keepdims=True) + 1e-5)
    return (y * g.astype(jnp.float32) + b.astype(jnp.float32)).astype(x.dtype)


def split_projection(p):
    offs = np.cumsum(np.array(IN_SIZES))[:-1].tolist()
    return jnp.split(p, offs, axis=-1)


def spatial_gating_mixer(u, v, ln_g, ln_b, w_s, b_s, out_g):
    bsz, s, _ = u.shape
    nc = s // CHUNK
    u = jax.nn.gelu(u).reshape(bsz, nc, CHUNK, GM_HEADS, GM_HEAD_DIM)
    v = jax.nn.gelu(v).reshape(bsz, s, GM_HEADS, GM_HEAD_DIM)
    v = layer_norm(v, ln_g, ln_b).reshape(bsz, nc, CHUNK, GM_HEADS, GM_HEAD_DIM)
    w = jnp.tril(w_s).astype(v.dtype)
    gate = jnp.einsum('hts,bcshe->bcthe', w, v) + b_s.T.astype(v.dtype)[None, None, :, :, None]
    y = (u * gate).reshape(bsz, s, GM_WIDTH)
    return rms_norm(y, out_g)


def sliding_window_sink_attention(q, k, v, sinks, out_g):
    bsz, s, _ = q.shape
    nb = s // WINDOW
    grp = ATT_HEADS // ATT_KV_HEADS
    qb = q.reshape(bsz, nb, WINDOW, ATT_KV_HEADS, grp, ATT_HEAD_DIM)
    pad = ((0, 0), (WINDOW, 0), (0, 0))
    kp = jnp.pad(k, pad).reshape(bsz, nb + 1, WINDOW, ATT_KV_HEADS, ATT_HEAD_DIM)
    vp = jnp.pad(v, pad).reshape(bsz, nb + 1, WINDOW, ATT_KV_HEADS, ATT_HEAD_DIM)
    kb = jnp.concatenate([kp[:, :-1], kp[:, 1:]], axis=2)
    vb = jnp.concatenate([vp[:, :-1], vp[:, 1:]], axis=2)
    scores = jnp.einsum('bnqkgd,bnjkd->bnkgqj', qb, kb).astype(jnp.float32) * (ATT_HEAD_DIM ** -0.5)
    qi = jnp.arange(WINDOW)[:, None]
    kj = jnp.arange(2 * WINDOW)[None, :]
    diff = qi + WINDOW - kj
    band = (diff >= 0) & (diff < WINDOW)
    blk = jnp.arange(nb)[:, None, None]
    valid = band[None] & ((blk * WINDOW + kj[None] - WINDOW) >= 0)
    scores = jnp.where(valid[None, :, None, None], scores, NEG_INF)
    sink = sinks.astype(jnp.float32).reshape(ATT_KV_HEADS, grp)[None, None, :, :, None, None]
    m = jnp.maximum(jnp.max(scores, axis=-1, keepdims=True), sink)
    e = jnp.exp(scores - m)
    p = e / (jnp.sum(e, axis=-1, keepdims=True) + jnp.exp(sink - m))
    o = jnp.einsum('bnkgqj,bnjkd->bnqkgd', p.astype(v.dtype), vb)
    return rms_norm(o.reshape(bsz, s, ATT_WIDTH), out_g)


def causal_depthwise_conv(x, w, b):
    kern = w[:, None, :].astype(x.dtype)
    out = lax.conv_general_dilated(x, kern, window_strides=(1,), padding=((CONV_WIDTH - 1, 0),),
                                   dimension_numbers=('NWC', 'WIO', 'NWC'),
                                   feature_group_count=x.shape[-1])
    return out + b.astype(x.dtype)


def ssd_chunked_scan(xs, dt, a_log, bm, cm, d_skip):
    bsz, s, nh, hp = xs.shape
    nc = s // CHUNK
    hg = nh // SSM_GROUPS
    a = -jnp.exp(a_log.astype(jnp.float32))
    xf = xs.astype(jnp.float32)
    xdt = (xf * dt[..., None]).reshape(bsz, nc, CHUNK, SSM_GROUPS, hg, hp)
    da = (dt * a).reshape(bsz, nc, CHUNK, SSM_GROUPS, hg).transpose(0, 1, 3, 4, 2)
    bc = bm.astype(jnp.float32).reshape(bsz, nc, CHUNK, SSM_GROUPS, SSM_STATE)
    cc = cm.astype(jnp.float32).reshape(bsz, nc, CHUNK, SSM_GROUPS, SSM_STATE)
    a_cs = jnp.cumsum(da, axis=-1)
    idx = jnp.arange(CHUNK)
    causal = idx[:, None] >= idx[None, :]
    decay = jnp.exp(jnp.where(causal, a_cs[..., :, None] - a_cs[..., None, :], -jnp.inf))
    cb = jnp.einsum('bclgn,bcsgn->bcgls', cc, bc)
    y_diag = jnp.einsum('bcghls,bcsghp->bclghp', cb[:, :, :, None] * decay, xdt)
    decay_states = jnp.exp(a_cs[..., -1:] - a_cs)
    states = jnp.einsum('bclgn,bcghl,bclghp->bcghpn', bc, decay_states, xdt)
    chunk_decay = jnp.exp(a_cs[..., -1])

    def step(carry, inp):
        st, dec = inp
        return carry * dec[..., None, None] + st, carry

    init = jnp.zeros((bsz, SSM_GROUPS, hg, hp, SSM_STATE), jnp.float32)
    _, prev = lax.scan(step, init, (jnp.moveaxis(states, 1, 0), jnp.moveaxis(chunk_decay, 1, 0)))
    prev = jnp.moveaxis(prev, 0, 1)
    y_off = jnp.einsum('bclgn,bcghpn,bcghl->bclghp', cc, prev, jnp.exp(a_cs))
    y = (y_diag + y_off).reshape(bsz, s, nh, hp)
    return y + xf * d_skip.astype(jnp.float32)[:, None]


def ssd_mixer(z, xbc, dt_raw, conv_w, conv_b, dt_bias, a_log, d_skip, norm_g):
    bsz, s, _ = z.shape
    xbc = jax.nn.silu(causal_depthwise_conv(xbc, conv_w, conv_b))
    xs, bm, cm = jnp.split(xbc, [SSM_WIDTH, SSM_WIDTH + BC_WIDTH], axis=-1)
    xs = xs.reshape(bsz, s, SSM_HEADS, SSM_HEAD_DIM)
    bm = bm.reshape(bsz, s, SSM_GROUPS, SSM_STATE)
    cm = cm.reshape(bsz, s, SSM_GROUPS, SSM_STATE)
    dt = jax.nn.softplus(dt_raw.astype(jnp.float32) + dt_bias.astype(jnp.float32))
    y = ssd_chunked_scan(xs, dt, a_log, bm, cm, d_skip)
    y = y.reshape(bsz, s, SSM_WIDTH) * jax.nn.silu(z.astype(jnp.float32))
    yg = y.reshape(bsz, s, SSM_GROUPS, SSM_WIDTH // SSM_GROUPS)
    yg = yg * lax.rsqrt(jnp.mean(yg * yg, axis=-1, keepdims=True) + EPS)
    return (yg.reshape(bsz, s, SSM_WIDTH) * norm_g.astype(jnp.float32)).astype(z.dtype)


def setup_inputs(seed: int = 0) -> dict:
    key = jax.random.key(seed)
    ks = jax.random.split(key, 26)
    L, D = DEPTH, D_MODEL

    def nrm(k, shape, scale):
        return jax.random.normal(k, shape, jnp.float32) * scale

    dt0 = jnp.exp(jax.random.uniform(ks[15], (L, SSM_HEADS), jnp.float32, np.log(1e-3), np.log(1e-1)))
    return {
        'x': nrm(ks[0], (BATCH, SEQ, D), 1.0),
        'c': nrm(ks[1], (BATCH, D), 1.0),
        'ada_w': nrm(ks[2], (L, D, 6 * D), 0.5 * D ** -0.5),
        'ada_b': nrm(ks[3], (L, 6 * D), 0.02),
        'norm1_g': 1.0 + nrm(ks[4], (L, D), 0.05),
        'w_in': nrm(ks[5], (L, D, IN_WIDTH), D ** -0.5),
        'gm_ln_g': 1.0 + nrm(ks[6], (L, GM_HEADS, GM_HEAD_DIM), 0.05),
        'gm_ln_b': nrm(ks[7], (L, GM_HEADS, GM_HEAD_DIM), 0.02),
        'gm_ws': nrm(ks[8], (L, GM_HEADS, CHUNK, CHUNK), CHUNK ** -0.5),
        'gm_bs': 1.0 + nrm(ks[9], (L, GM_HEADS, CHUNK), 0.05),
        'gm_norm_g': 1.0 + nrm(ks[10], (L, GM_WIDTH), 0.05),
        'attn_sinks': nrm(ks[11], (L, ATT_HEADS), 0.5),
        'attn_norm_g': 1.0 + nrm(ks[12], (L, ATT_WIDTH), 0.05),
        'conv_w': nrm(ks[13], (L, CONV_WIDTH, CONV_CH), CONV_WIDTH ** -0.5),
        'conv_b': nrm(ks[14], (L, CONV_CH), 0.02),
        'dt_bias': dt0 + jnp.log(-jnp.expm1(-dt0)),
        'a_log': jnp.log(jax.random.uniform(ks[16], (L, SSM_HEADS), jnp.float32, 1.0, 16.0)),
        'd_skip': 1.0 + nrm(ks[17], (L, SSM_HEADS), 0.1),
        'ssm_norm_g': 1.0 + nrm(ks[18], (L, SSM_WIDTH), 0.05),
        'w_out': nrm(ks[19], (L, MIX_WIDTH, D), MIX_WIDTH ** -0.5),
        'norm2_g': 1.0 + nrm(ks[20], (L, D), 0.05),
        'w_mlp1': nrm(ks[21], (L, D, D_FF), D ** -0.5),
        'w_mlp2': nrm(ks[22], (L, D_FF, D), D_FF ** -0.5),
        'final_norm_g': 1.0 + nrm(ks[23], (D,), 0.05),
    }


def reference(x, c, ada_w, ada_b, norm1_g, w_in, gm_ln_g, gm_ln_b, gm_ws, gm_bs, gm_norm_g,
              attn_sinks, attn_norm_g, conv_w, conv_b, dt_bias, a_log, d_skip, ssm_norm_g,
              w_out, norm2_g, w_mlp1, w_mlp2, final_norm_g):
    c_act = jax.nn.silu(c)
    for l in range(DEPTH):
        mod = c_act @ ada_w[l] + ada_b[l]
        sh1, sc1, g1, sh2, sc2, g2 = [m[:, None, :] for m in jnp.split(mod, 6, axis=-1)]
        h = rms_norm(x, norm1_g[l]) * (1.0 + sc1) + sh1
        u_a, v_a, q_b, k_b, v_b, z_c, xbc_c, dt_c = split_projection(h @ w_in[l])
        out_a = spatial_gating_mixer(u_a, v_a, gm_ln_g[l], gm_ln_b[l], gm_ws[l], gm_bs[l], gm_norm_g[l])
        out_b = sliding_window_sink_attention(q_b, k_b, v_b, attn_sinks[l], attn_norm_g[l])
        out_c = ssd_mixer(z_c, xbc_c, dt_c, conv_w[l], conv_b[l], dt_bias[l], a_log[l], d_skip[l], ssm_norm_g[l])
        mix = jnp.concatenate([out_a, out_b, out_c], axis=-1) @ w_out[l]
        x = x + g1 * mix
        h = rms_norm(x, norm2_g[l]) * (1.0 + sc2) + sh2
        x = x + g2 * (jnp.square(jax.nn.relu(h @ w_mlp1[l])) @ w_mlp2[l])
    return rms_norm(x, final_norm_g)
```

```python
import contextlib
import numpy as np
import concourse.bass as bass
import concourse.mybir as mybir
from concourse.bass_utils import run_bass_kernel_spmd

F32 = mybir.dt.float32
BF16 = mybir.dt.bfloat16
AF = mybir.ActivationFunctionType
ALU = mybir.AluOpType

ENGS = ("pe", "act", "dve", "pool", "sp")


class Reg:
    __slots__ = ("name", "w", "rd")

    def __init__(self, name):
        self.name = name
        self.w = None
        self.rd = {}


class Rec:
    __slots__ = ("eng", "fn", "inc", "deps", "dma_sem", "cnt", "idx", "dma_waits")

    def __init__(self, eng, fn, inc, dma_sem):
        self.eng = eng
        self.fn = fn
        self.inc = inc
        self.deps = []
        self.dma_waits = {}
        self.dma_sem = dma_sem
        self.cnt = None
        self.idx = None


class Sched:
    def __init__(self, nc):
        self.nc = nc
        self.recs = {e: [] for e in ENGS}
        self.dma_cnt = {}
        self.n = 0

    def _dep(self, w, d):
        if d is None or d is w:
            return
        if d.dma_sem is not None:
            k = d.dma_sem
            w.dma_waits[k] = max(w.dma_waits.get(k, 0), self.dma_cnt[k])
            return
        if d.eng == "pe" and w.eng == "pe" and w.dma_sem is None:
            return
        if not d.inc:
            lst = self.recs[d.eng]
            tgt = None
            for j in range(d.idx + 1, len(lst)):
                if lst[j].inc and lst[j].dma_sem is None:
                    tgt = lst[j]
                    break
            if tgt is None:
                d.inc = True
                tgt = d
            d = tgt
        w.deps.append(d)

    def op(self, eng, fn, reads=(), writes=(), inc=True, dma=None):
        r = Rec(eng, fn, inc if dma is None else False, dma)
        r.idx = len(self.recs[eng])
        for x in reads:
            self._dep(r, x.w)
        for x in writes:
            self._dep(r, x.w)
            for rr in x.rd.values():
                self._dep(r, rr)
        if dma is not None:
            self.dma_cnt[dma] = self.dma_cnt.get(dma, 0) + 16
            r.cnt = self.dma_cnt[dma]
        self.recs[eng].append(r)
        for x in writes:
            x.w = r
            x.rd = {}
        for x in reads:
            x.rd[eng if dma is None else (dma, r.cnt)] = r
        self.n += 1
        return r

    def emit(self, final_waits=()):
        nc = self.nc
        with contextlib.ExitStack() as es:
            sems = {}
            for e in ENGS:
                sems[e] = es.enter_context(nc.semaphore("s_" + e))
            for k in self.dma_cnt:
                sems[("dma", k)] = es.enter_context(nc.semaphore("d_" + k))
            for e in ENGS:
                c = 0
                for r in self.recs[e]:
                    if r.dma_sem is None and r.inc:
                        c += 1
                        r.cnt = c
            block = es.enter_context(nc.Block())

            def run(e, eng):
                waited = {}
                for r in self.recs[e]:
                    need = {}
                    for d in r.deps:
                        need[d.eng] = max(need.get(d.eng, 0), d.cnt)
                    for k, v in r.dma_waits.items():
                        need[("dma", k)] = max(need.get(("dma", k), 0), v)
                    for k, v in need.items():
                        if waited.get(k, 0) < v:
                            eng.wait_ge(sems[k], v)
                            waited[k] = v
                    ins = r.fn(eng)
                    if r.dma_sem is not None:
                        ins.then_inc(sems[("dma", r.dma_sem)], 16)
                    elif r.inc:
                        ins.then_inc(sems[e], 1)
                if e == "sp":
                    for k in final_waits:
                        eng.wait_ge(sems[("dma", k)], self.dma_cnt[k])

            @block.tensor
            def _(eng):
                run("pe", eng)

            @block.scalar
            def _(eng):
                run("act", eng)

            @block.vector
            def _(eng):
                run("dve", eng)

            @block.gpsimd
            def _(eng):
                run("pool", eng)

            @block.sync
            def _(eng):
                run("sp", eng)


def vw(ap, dims):
    return bass.AP(ap.tensor, ap.offset, [list(ap.ap[0])] + [list(d) for d in dims])


D = 2048
KC = 16
TB = 512
NT = 4
SEQ = 2048
NW = 2
O_U, O_V, O_Q, O_K, O_VV, O_Z, O_X, O_DT = 0, 512, 1024, 1536, 1664, 1792, 2816, 4352
C_ID, C_U, C_NU, C_SL, C_ONE = 0, 128, 256, 384, 512

PARAM_NAMES = ["ada_w", "ada_b", "norm1_g", "w_in", "gm_ln_g", "gm_ln_b", "gm_ws", "gm_bs",
               "gm_norm_g", "attn_sinks", "attn_norm_g", "conv_w", "conv_b", "dt_bias", "a_log",
               "d_skip", "ssm_norm_g", "w_out", "norm2_g", "w_mlp1", "w_mlp2", "final_norm_g"]
PARAM_SHAPES = {
    "ada_w": [2, 2048, 12288], "ada_b": [2, 12288], "norm1_g": [2, 2048], "w_in": [2, 2048, 4368],
    "gm_ln_g": [2, 4, 128], "gm_ln_b": [2, 4, 128], "gm_ws": [2, 4, 128, 128], "gm_bs": [2, 4, 128],
    "gm_norm_g": [2, 512], "attn_sinks": [2, 8], "attn_norm_g": [2, 512], "conv_w": [2, 4, 1536],
    "conv_b": [2, 1536], "dt_bias": [2, 16], "a_log": [2, 16], "d_skip": [2, 16],
    "ssm_norm_g": [2, 1024], "w_out": [2, 2048, 2048], "norm2_g": [2, 2048],
    "w_mlp1": [2, 2048, 8192], "w_mlp2": [2, 8192, 2048], "final_norm_g": [2048],
}


def slim_shapes(nlayer, do_mlp):
    sh = {k: list(v) for k, v in PARAM_SHAPES.items()}
    for k in ("ada_w", "w_in", "w_out", "w_mlp1", "w_mlp2"):
        sh[k][0] = nlayer
    if not do_mlp:
        sh["w_mlp1"] = [1, 128, 128]
        sh["w_mlp2"] = [1, 128, 128]
    return sh


def build(nseq=2, nblk=4, nlayer=2, do_a=True, do_b=True, do_c=True, do_mlp=True, slim=False):
    nc = bass.Bass("TRN2", target_bir_lowering=False)
    din = lambda n, s: nc.dram_tensor(n, s, F32, kind="ExternalInput").ap()
    x = din("x", [2, SEQ, D])
    c_in = din("c", [2, D])
    shp = slim_shapes(nlayer, do_mlp) if slim else PARAM_SHAPES
    prm = {n: din(n, shp[n]) for n in PARAM_NAMES}
    consts = din("consts", [128, 640])
    out = nc.dram_tensor("out", [2, SEQ, D], F32, kind="ExternalOutput").ap()
    mod_d = nc.dram_tensor("mod_d", [2, 2, 12288], F32).ap()

    es = contextlib.ExitStack()
    S = Sched(nc)

    def sb(name, shape, dt=F32):
        return es.enter_context(nc.sbuf_tensor(name, shape, dt))

    xT = sb("xT", [128, KC, TB]); rxT = [Reg(f"xT{k}") for k in range(KC)]
    hT = sb("hT", [128, KC, TB], BF16); rhT = [Reg(f"hT{k}") for k in range(KC)]
    mixT = sb("mixT", [128, KC, TB], BF16); rmix = [Reg(f"mix{k}") for k in range(KC)]
    wbuf = [sb(f"wbuf{i}", [128, 8192], BF16) for i in range(NW)]
    rw = [Reg(f"w{i}") for i in range(NW)]
    TT = sb("TT", [128, 6 * 1024]); rT = [Reg(f"T{i}") for i in range(6)]
    T = [TT[:, i * 1024:(i + 1) * 1024] for i in range(6)]
    xin = TT[:, 2048:4096]; rxin = [rT[2], rT[3]]
    fgb = TT[:, 0:2048]; rfgb = [rT[0], rT[1]]
    cf = sb("cf", [128, 640]); rcf = Reg("cf")
    cb = sb("cb", [128, 640], BF16); rcb = Reg("cb")
    cst = sb("cst", [128, 4]); rcst = Reg("cst")
    vrow = sb("vrow", [128, 128]); rvrow = Reg("vrow")
    vT = [sb(f"vT{l}", [128, 112]) for l in range(2)]; rvT = [Reg(f"vT{l}") for l in range(2)]
    vTf = sb("vTf", [128, 16]); rvTf = Reg("vTf")
    modT = sb("modT", [128, 4 * 96]); rmod = Reg("modT")
    gsc = sb("gsc", [128, 4 * 32]); rgsc = Reg("gsc")
    lnG = sb("lnG", [128, 512]); lnB = sb("lnB", [128, 512]); rln = Reg("ln")
    WsT = [sb(f"WsT{l}", [128, 4, 128], BF16) for l in range(2)]; rws = [Reg(f"ws{l}") for l in range(2)]
    esink = [sb(f"esink{l}", [128, 8]) for l in range(2)]
    dtb = [sb(f"dtb{l}", [128, 16]) for l in range(2)]
    aneg = [sb(f"aneg{l}", [128, 16]) for l in range(2)]
    dsk = [sb(f"dsk{l}", [128, 16]) for l in range(2)]
    rsp = Reg("smallparams")
    rstd_b = sb("rstd_b", [128, TB]); rrstd = Reg("rstd")
    tmps = [sb(f"tmp{i}", [128, TB]) for i in range(2)]; rtmp = [Reg(f"tmp{i}") for i in range(2)]
    AB = sb("AB", [128, 4096], BF16)
    qT = AB[:, 0:2048].rearrange("p (j t) -> p j t", j=4); rqT = Reg("qT")
    e_cur = AB[:, 2048:3072]; recur = Reg("ecur")
    e_prev = AB[:, 3072:4096]; reprev = Reg("eprev")
    zs = AB[:, :].rearrange("p (t f) -> p t f", t=4); rzs = [rqT, recur, reprev]
    kT = [sb(f"kT{l}", [128, 2, 640], BF16) for l in range(2)]; rkT = [Reg(f"kT{l}") for l in range(2)]
    vaug = [sb(f"vaug{l}", [128, 5, 2, 72], BF16) for l in range(2)]
    rva = [[Reg(f"va{l}_{i}") for i in range(5)] for l in range(2)]
    xs_tok = sb("xs_tok", [128, 4, 1024]); rxs = Reg("xs_tok")
    bcT = sb("bcT", [128, 4, TB], BF16); rbc = Reg("bcT")
    bm_tok = sb("bm_tok", [128, 4, 2, 128], BF16); rbmt = Reg("bm_tok")
    xraw = sb("xraw", [128, 2, 515]); rxraw = [Reg("xraw0"), Reg("xraw1")]
    hist = [sb(f"hist{l}", [128, 12, 3]) for l in range(2)]; rhist = [Reg(f"hist{l}") for l in range(2)]
    Sst = [sb(f"Sst{l}", [128, 1024]) for l in range(2)]; rS = [Reg(f"S{l}") for l in range(2)]
    Sbf = sb("Sbf", [128, 1024], BF16); rSbf = Reg("Sbf")
    M_bf = sb("M_bf", [128, 2, 8, 128], BF16); rM = [Reg("M0"), Reg("M1")]
    xdt = sb("xdt", [128, 16, 64], BF16); rxdt = Reg("xdt")
    xdt2 = sb("xdt2", [128, 16, 64], BF16); rxdt2 = Reg("xdt2")
    vn = sb("vn", [128, 512], BF16); rvn = Reg("vn")
    obf = sb("obf", [128, 1024], BF16); robf = Reg("obf")
    sm = sb("sm", [128, 256]); rsm = Reg("sm")
    dtall = sb("dtall", [128, 4, 16]); daall = sb("daall", [128, 4, 16]); rdt = Reg("dt")
    cbm = sb("cbm", [128, 2, 128]); rcbm = Reg("cbm")
    cTf = sb("cTf", [128, KC, 2]); cTb = sb("cTb", [128, KC, 2], BF16); rcT = Reg("cT")
    abt = TT[0:2, 4096:4608]; rabt = rT[4]
    mrow = TT[0:2, 5120:5632]; rmrow = rT[5]

    ps = es.enter_context(nc.psum_tensor("ps", [128, 4096], F32))
    psr = [Reg(f"ps{b}") for b in range(8)]

    class PSA:
        i = 0

        @staticmethod
        def get(n=1):
            while PSA.i % n:
                PSA.i += 1
            b = PSA.i % 8
            PSA.i += n
            return b

    def psb(b, n=512, off=0):
        return ps[:, b * 512 + off: b * 512 + off + n]

    def mm(o, lhsT, rhs, start, stop, rd, wr, inc=None):
        S.op("pe", lambda e: e.matmul(o, lhsT, rhs, start=start, stop=stop), rd, wr,
             inc=(stop if inc is None else inc))

    def tr(o, i, ident, rd, wr, inc=True):
        S.op("pe", lambda e: e.transpose(o, i, ident), rd, wr, inc=inc)

    def act(o, i, func, rd, wr, **kw):
        S.op("act", lambda e: e.activation(out=o, in_=i, func=func, **kw), rd, wr)

    def dve(meth, rd, wr, **kw):
        S.op("dve", lambda e: getattr(e, meth)(**kw), rd, wr)

    def dma(q, o, i, rd, wr, key):
        S.op(q, lambda e: e.dma_start(out=o, in_=i), rd, wr, dma=key)

    class WL:
        i = 0

    def wslot():
        i = WL.i % NW
        WL.i += 1
        return i

    def load_w(src3, k, n):
        i = wslot()
        v = wbuf[i][:, 0:k * n].rearrange("p (k n) -> p k n", k=k)
        dma("pool", v, src3, [], [rw[i]], f"w{i}")
        return v, rw[i]

    ident_f = cf[:, C_ID:C_ID + 128]
    U_f = cf[:, C_U:C_U + 128]
    negU_f = cf[:, C_NU:C_NU + 128]
    ones_f = cf[:, C_ONE:C_ONE + 128]
    ident_b = cb[:, C_ID:C_ID + 128]
    U_b = cb[:, C_U:C_U + 128]
    SL_b = cb[:, C_SL:C_SL + 128]
    ones_b = cb[:, C_ONE:C_ONE + 128]
    EPS6 = cst[:, 0:1]
    EPS5 = cst[:, 1:2]
    ONE = cst[:, 2:3]

    dma("sp", cf[:], consts, [], [rcf], "const")
    dma("pool", cb[:], consts, [], [rcb], "constb")
    dve("memset", [], [rcst], ap=cst[:, 0:1], constant=1e-6)
    dve("memset", [], [rcst], ap=cst[:, 1:2], constant=1e-5)
    dve("memset", [], [rcst], ap=cst[:, 2:3], constant=1.0)
    dve("memset", [], [rcst], ap=cst[:, 3:4], constant=0.0)
    for l in range(2):
        for i in range(5):
            dve("memset", [], [rva[l][i]], ap=vaug[l][:, i, :, 64:72], constant=0.0)
            dve("memset", [], [rva[l][i]], ap=vaug[l][:, i, :, 64:65], constant=1.0)

    def vec_rows(l):
        rows = []
        rows.append((0, 16, prm["norm1_g"][l].rearrange("(k p) -> k p", p=128)))
        rows.append((16, 16, prm["norm2_g"][l].rearrange("(k p) -> k p", p=128)))
        rows.append((32, 4, prm["gm_norm_g"][l].rearrange("(k p) -> k p", p=128)))
        rows.append((36, 4, prm["attn_norm_g"][l].rearrange("(k p) -> k p", p=128)))
        rows.append((40, 8, prm["ssm_norm_g"][l].rearrange("(k p) -> k p", p=128)))
        rows.append((48, 12, prm["conv_b"][l].rearrange("(k p) -> k p", p=128)))
        rows.append((60, 48, prm["conv_w"][l].rearrange("t (k p) -> (t k) p", p=128)))
        rows.append((108, 4, prm["gm_bs"][l]))
        return rows

    for l in range(2):
        for (r0, n, src) in vec_rows(l):
            dma("sp", vrow[r0:r0 + n, :], src, [], [rvrow], "vrow")
        b = PSA.get()
        tr(psb(b, 112), vrow[0:112, :], ident_f[0:112, 0:112], [rvrow, rcf], [psr[b]])
        act(vT[l][:], psb(b, 112), AF.Copy, [psr[b]], [rvT[l]])
    dma("sp", vrow[0:16, :], prm["final_norm_g"].rearrange("(k p) -> k p", p=128), [], [rvrow], "vrow")
    b = PSA.get()
    tr(psb(b, 16), vrow[0:16, :], ident_f[0:16, 0:16], [rvrow, rcf], [psr[b]])
    act(vTf[:], psb(b, 16), AF.Copy, [psr[b]], [rvTf])

    for l in range(2):
        dma("sp", esink[l][:], prm["attn_sinks"][l].partition_broadcast(128), [], [rsp], "sp")
        dma("sp", dtb[l][:], prm["dt_bias"][l].partition_broadcast(128), [], [rsp], "sp")
        dma("sp", aneg[l][:], prm["a_log"][l].partition_broadcast(128), [], [rsp], "sp")
        dma("sp", dsk[l][:], prm["d_skip"][l].partition_broadcast(128), [], [rsp], "sp")
    for l in range(2):
        act(esink[l][:], esink[l][:], AF.Exp, [rsp], [rsp])
        act(aneg[l][:], aneg[l][:], AF.Exp, [rsp], [rsp])
        dve("tensor_scalar", [rsp], [rsp], out=aneg[l][:], in0=aneg[l][:], scalar1=-1.0, scalar2=None,
            op0=ALU.mult)
    for l in range(2):
        wsf = T[0].rearrange("p (h s) -> p h s", h=8)[:, 0:4, :]
        dma("sp", wsf, prm["gm_ws"][l].rearrange("h t s -> t h s"), [], [rT[0]], "wsf")
        b = PSA.get()
        for h in range(4):
            tr(psb(b, 128, h * 128), wsf[:, h, :], ident_f, [rT[0], rcf], [psr[b]], inc=(h == 3))
        dve("tensor_tensor", [psr[b], rcf], [rws[l]], out=WsT[l][:],
            in0=psb(b).rearrange("p (h t) -> p h t", h=4), in1=vw(U_f, [[0, 4], [1, 128]]), op=ALU.mult)

    for s_ in range(2):
        dma("sp", cTf[:, :, s_], c_in[s_].rearrange("(k p) -> p k", p=128), [], [rcT], "cT")
    act(cTb[:], cTf[:], AF.Silu, [rcT], [rcT])
    for l in range(nlayer):
        aw3 = prm["ada_w"][l].rearrange("(k p) n -> p k n", p=128)
        for n in range(24):
            w, rwi = load_w(aw3[:, :, n * 512:(n + 1) * 512], 16, 512)
            dma("sp", abt, prm["ada_b"][l][n * 512:(n + 1) * 512].partition_broadcast(2), [], [rabt], "abt")
            b = PSA.get()
            for k in range(KC):
                mm(ps[0:2, b * 512:(b + 1) * 512], cTb[:, k, :], w[:, k, :], k == 0, k == KC - 1,
                   [rcT, rwi], [psr[b]])
            dve("tensor_tensor", [psr[b], rabt], [rmrow], out=mrow, in0=ps[0:2, b * 512:(b + 1) * 512],
                in1=abt, op=ALU.add)
            dma("sp", mod_d[l, :, n * 512:(n + 1) * 512], mrow, [rmrow], [], "mrow")
    rmodd = Reg("mod_d")
    for l in range(nlayer):
        for s in range(2):
            i = l * 2 + s
            S.op("sp", lambda e, l=l, s=s: e.dma_start(out=vrow[0:96, :],
                                                       in_=mod_d[l, s].rearrange("(r p) -> r p", p=128)),
                 [], [rmrow, rvrow], dma="mrow")
            b = PSA.get()
            tr(psb(b, 96), vrow[0:96, :], ident_f[0:96, 0:96], [rvrow, rcf], [psr[b]])
            act(modT[:, i * 96:(i + 1) * 96], psb(b, 96), AF.Copy, [psr[b]], [rmod])
            dve("scalar_tensor_tensor", [rmod, rvT[l]], [rgsc], out=gsc[:, i * 32:i * 32 + 16],
                in0=modT[:, i * 96 + 16:i * 96 + 32], scalar=1.0, in1=vT[l][:, 0:16], op0=ALU.add, op1=ALU.mult)
            dve("scalar_tensor_tensor", [rmod, rvT[l]], [rgsc], out=gsc[:, i * 32 + 16:i * 32 + 32],
                in0=modT[:, i * 96 + 64:i * 96 + 80], scalar=1.0, in1=vT[l][:, 16:32], op0=ALU.add, op1=ALU.mult)

    def norm_to_hT(gs, sh):
        for k in range(KC):
            act(hT[:, k, :], xT[:, k, :], AF.Square, [rxT[k]], [rhT[k]])
        b = PSA.get()
        for k in range(KC):
            mm(psb(b), ones_b, hT[:, k, :], k == 0, k == KC - 1, [rhT[k], rcb], [psr[b]])
        act(rstd_b[:], psb(b), AF.Sqrt, [psr[b], rcst], [rrstd], scale=1.0 / D, bias=EPS6)
        dve("reciprocal", [rrstd], [rrstd], out=rstd_b[:], in_=rstd_b[:])
        for k in range(KC):
            t_ = tmps[k % 2]
            dve("scalar_tensor_tensor", [rxT[k], rrstd, rgsc], [rtmp[k % 2]], out=t_[:], in0=xT[:, k, :],
                scalar=gs[:, k:k + 1], in1=rstd_b[:], op0=ALU.mult, op1=ALU.mult)
            act(hT[:, k, :], t_[:], AF.Identity, [rtmp[k % 2], rmod], [rhT[k]], bias=sh[:, k:k + 1], scale=1.0)

    def rms_scale(src, n, ss_col):
        act(T[3][:, 0:n], src, AF.Square, [rsm], [rT[3], rsm], accum_out=sm[:, ss_col:ss_col + 1])
        act(sm[:, ss_col + 1:ss_col + 2], sm[:, ss_col:ss_col + 1], AF.Sqrt, [rsm, rcst], [rsm],
            scale=1.0 / n, bias=EPS6)
        dve("reciprocal", [rsm], [rsm], out=sm[:, ss_col + 1:ss_col + 2], in_=sm[:, ss_col + 1:ss_col + 2])

    def to_mixT(src_bf, nchunk, k0, t, l, gcol, rsrc):
        b = PSA.get()
        pT = psb(b).bitcast(BF16)
        for j in range(nchunk):
            tr(pT[:, j * 128:(j + 1) * 128], src_bf[:, j * 128:(j + 1) * 128], ident_b, [rsrc, rcb], [psr[b]],
               inc=(j == nchunk - 1))
        for j in range(nchunk):
            act(mixT[:, k0 + j, t * 128:(t + 1) * 128], pT[:, j * 128:(j + 1) * 128], AF.Copy,
                [psr[b], rvT[l]], [rmix[k0 + j]], scale=vT[l][:, gcol + j:gcol + j + 1])

    def mixer_a(l, win3):
        wu, ru = load_w(win3[:, :, O_U:O_U + 512], 16, 512)
        wv, rv = load_w(win3[:, :, O_V:O_V + 512], 16, 512)
        for t in range(NT):
            ts = slice(t * 128, (t + 1) * 128)
            b = PSA.get(2)
            for k in range(KC):
                mm(psb(b), hT[:, k, ts], wu[:, k, :], k == 0, k == KC - 1, [rhT[k], ru], [psr[b]])
            for k in range(KC):
                mm(psb(b + 1), hT[:, k, ts], wv[:, k, :], k == 0, k == KC - 1, [rhT[k], rv], [psr[b + 1]])
            act(T[0], ps[:, b * 512:(b + 2) * 512], AF.Gelu_apprx_tanh, [psr[b], psr[b + 1]], [rT[0]])
            ug = T[0][:, 0:512]
            vg = T[0][:, 512:1024]
            st6 = sm[:, 0:24].rearrange("p (h s) -> p h s", h=4)
            mv = sm[:, 24:32].rearrange("p (h s) -> p h s", h=4)
            for h in range(4):
                dve("bn_stats", [rT[0]], [rsm], out=st6[:, h, :], in_=vg[:, h * 128:(h + 1) * 128])
            for h in range(4):
                dve("bn_aggr", [rsm], [rsm], out=mv[:, h, :], in_=st6[:, h, :])
            act(sm[:, 32:36], mv[:, :, 1], AF.Sqrt, [rsm, rcst], [rsm], bias=EPS5, scale=1.0)
            dve("reciprocal", [rsm], [rsm], out=sm[:, 32:36], in_=sm[:, 32:36])
            for h in range(4):
                dve("tensor_scalar", [rT[0], rsm], [rT[1]], out=T[1][:, h * 128:(h + 1) * 128],
                    in0=vg[:, h * 128:(h + 1) * 128], scalar1=mv[:, h, 0:1], scalar2=sm[:, 32 + h:33 + h],
                    op0=ALU.subtract, op1=ALU.mult)
            dve("tensor_tensor", [rT[1], rln], [rT[1]], out=T[1][:, 0:512], in0=T[1][:, 0:512], in1=lnG[:],
                op=ALU.mult)
            dve("tensor_tensor", [rT[1], rln], [rvn], out=vn[:], in0=T[1][:, 0:512], in1=lnB[:], op=ALU.add)
            b2 = PSA.get()
            for h in range(4):
                mm(psb(b2, 128, h * 128), WsT[l][:, h, :], vn[:, h * 128:(h + 1) * 128], True, True,
                   [rws[l], rvn], [psr[b2]], inc=(h == 3))
            for h in range(4):
                dve("scalar_tensor_tensor", [psr[b2], rT[0], rvT[l]], [rT[2]], out=T[2][:, h * 128:(h + 1) * 128],
                    in0=psb(b2, 128, h * 128), scalar=vT[l][:, 108 + h:109 + h], in1=ug[:, h * 128:(h + 1) * 128],
                    op0=ALU.add, op1=ALU.mult)
            S.op("act", lambda e: e.activation(out=T[3][:, 0:512], in_=T[2][:, 0:512], func=AF.Square,
                                               accum_out=sm[:, 40:41]), [rT[2]], [rT[3], rsm])
            act(sm[:, 41:42], sm[:, 40:41], AF.Sqrt, [rsm, rcst], [rsm], scale=1.0 / 512, bias=EPS6)
            dve("reciprocal", [rsm], [rsm], out=sm[:, 41:42], in_=sm[:, 41:42])
            act(obf[:, 0:512], T[2][:, 0:512], AF.Copy, [rT[2], rsm], [robf], scale=sm[:, 41:42])
            to_mixT(obf, 4, 0, t, l, 32, robf)

    def mixer_b(l, win3, g0):
        wq, rq = load_w(win3[:, :, O_Q:O_Q + 512], 16, 512)
        wkv, rk = load_w(win3[:, :, O_K:O_K + 256], 16, 256)
        wvv = wkv[:, :, 128:256]
        for h in range(8):
            b = PSA.get()
            for k in range(KC):
                mm(ps[0:64, b * 512:(b + 1) * 512], wq[:, k, h * 64:(h + 1) * 64], hT[:, k, :], k == 0, k == KC - 1,
                   [rhT[k], rq], [psr[b]])
            act(mixT[0:64, 8 + h, :], ps[0:64, b * 512:(b + 1) * 512], AF.Copy, [psr[b]], [rmix[8 + h]])
        for a in range(2):
            b = PSA.get()
            for k in range(KC):
                mm(ps[0:64, b * 512:(b + 1) * 512], wkv[:, k, a * 64:(a + 1) * 64], hT[:, k, :], k == 0, k == KC - 1,
                   [rhT[k], rk], [psr[b]])
            act(kT[l][0:64, a, 128:640], ps[0:64, b * 512:(b + 1) * 512], AF.Copy, [psr[b]], [rkT[l]])
        for t in range(NT):
            ts = slice(t * 128, (t + 1) * 128)
            slot = (g0 + t) % 5
            b = PSA.get()
            for k in range(KC):
                mm(psb(b, 128), hT[:, k, ts], wvv[:, k, :], k == 0, k == KC - 1, [rhT[k], rk], [psr[b]])
            act(vaug[l][:, slot, :, 0:64], psb(b, 128).rearrange("p (a d) -> p a d", a=2), AF.Copy,
                [psr[b]], [rva[l][slot]])
        for t in range(NT):
            ts = slice(t * 128, (t + 1) * 128)
            g = g0 + t
            slot = g % 5
            pslot = (g - 1) % 5
            blocks = [(e_cur, recur, 128 + t * 128, U_b, slot)]
            if g > 0:
                blocks.append((e_prev, reprev, t * 128, SL_b, pslot))
            for (ebuf, reb, kc0, mask, _) in blocks:
                b = PSA.get(2)
                for h in range(8):
                    a = h // 4
                    mm(ps[:, b * 512 + h * 128: b * 512 + (h + 1) * 128], kT[l][0:64, a, kc0:kc0 + 128],
                       mixT[0:64, 8 + h, ts], True, True, [rkT[l], rmix[8 + h]], [psr[b + h // 4]], inc=(h % 4 == 3))
                act(ebuf, ps[:, b * 512:(b + 2) * 512], AF.Exp, [psr[b], psr[b + 1]], [reb], scale=0.125)
                dve("tensor_tensor", [reb, rcb], [reb], out=ebuf.rearrange("p (h q) -> p h q", h=8),
                    in0=ebuf.rearrange("p (h q) -> p h q", h=8), in1=vw(mask, [[0, 8], [1, 128]]), op=ALU.mult)
            bo = PSA.get(2)
            for h in range(8):
                a = h // 4
                dst = ps[:, (bo + h // 4) * 512 + (h % 4) * 128:(bo + h // 4) * 512 + (h % 4) * 128 + 66]
                if g > 0:
                    mm(dst, e_prev[:, h * 128:(h + 1) * 128], vaug[l][:, pslot, a, 0:66], True, False,
                       [reprev, rva[l][pslot]], [psr[bo + h // 4]], inc=False)
                mm(dst, e_cur[:, h * 128:(h + 1) * 128], vaug[l][:, slot, a, 0:66], g == 0, True,
                   [recur, rva[l][slot]], [psr[bo + h // 4]], inc=(h % 4 == 3))
            for i2 in range(2):
                pv = psb(bo + i2, 512).rearrange("p (h c) -> p h c", c=128)
                dve("tensor_tensor", [psr[bo + i2], rsp], [rsm], out=sm[:, 48 + i2 * 4:52 + i2 * 4],
                    in0=pv[:, :, 64], in1=esink[l][:, i2 * 4:(i2 + 1) * 4], op=ALU.add)
            dve("reciprocal", [rsm], [rsm], out=sm[:, 48:56], in_=sm[:, 48:56])
            for i2 in range(2):
                pv = psb(bo + i2, 512).rearrange("p (h c) -> p h c", c=128)
                dve("tensor_tensor", [psr[bo + i2], rsm], [rT[2]],
                    out=T[2][:, i2 * 256:(i2 + 1) * 256].rearrange("p (h d) -> p h d", h=4),
                    in0=pv[:, :, 0:64], in1=vw(sm[:, 48 + i2 * 4:52 + i2 * 4], [[1, 4], [0, 64]]), op=ALU.mult)
            S.op("act", lambda e: e.activation(out=T[3][:, 0:512], in_=T[2][:, 0:512], func=AF.Square,
                                               accum_out=sm[:, 60:61]), [rT[2]], [rT[3], rsm])
            act(sm[:, 61:62], sm[:, 60:61], AF.Sqrt, [rsm, rcst], [rsm], scale=1.0 / 512, bias=EPS6)
            dve("reciprocal", [rsm], [rsm], out=sm[:, 61:62], in_=sm[:, 61:62])
            act(obf[:, 0:512], T[2][:, 0:512], AF.Copy, [rT[2], rsm], [robf], scale=sm[:, 61:62])
            to_mixT(obf, 4, 4, t, l, 36, robf)
        act(kT[l][0:64, :, 0:128], kT[l][0:64, :, 512:640], AF.Copy, [rkT[l]], [rkT[l]])

    def mixer_c(l, win3):
        wz0, rz0 = load_w(win3[:, :, O_Z:O_Z + 512], 16, 512)
        wz1, rz1 = load_w(win3[:, :, O_Z + 512:O_Z + 1024], 16, 512)
        for t in range(NT):
            ts = slice(t * 128, (t + 1) * 128)
            b = PSA.get(2)
            for k in range(KC):
                mm(psb(b), hT[:, k, ts], wz0[:, k, :], k == 0, k == KC - 1, [rhT[k], rz0], [psr[b]])
            for k in range(KC):
                mm(psb(b + 1), hT[:, k, ts], wz1[:, k, :], k == 0, k == KC - 1, [rhT[k], rz1], [psr[b + 1]])
            act(zs[:, t, :], ps[:, b * 512:(b + 2) * 512], AF.Silu, [psr[b], psr[b + 1]], rzs)
        wx = []
        for i3 in range(3):
            wx.append(load_w(win3[:, :, O_X + i3 * 512:O_X + (i3 + 1) * 512], 16, 512))
            for cc in range(4):
                c = i3 * 4 + cc
                w_, rw_ = wx[i3]
                b = PSA.get()
                for k in range(KC):
                    mm(psb(b), w_[:, k, cc * 128:(cc + 1) * 128], hT[:, k, :], k == 0, k == KC - 1,
                       [rhT[k], rw_], [psr[b]])
                xr = xraw[:, c % 2, :]
                rxr = rxraw[c % 2]
                dve("tensor_copy", [rhist[l]], [rxr], out=xr[:, 0:3], in_=hist[l][:, c, :])
                act(xr[:, 3:515], psb(b), AF.Copy, [psr[b]], [rxr])
                dve("tensor_copy", [rxr], [rhist[l]], out=hist[l][:, c, :], in_=xr[:, 512:515])
                ta = tmps[0]
                dve("tensor_scalar", [rxr, rvT[l]], [rtmp[0]], out=ta[:], in0=xr[:, 0:512],
                    scalar1=vT[l][:, 60 + c:61 + c], scalar2=vT[l][:, 48 + c:49 + c], op0=ALU.mult, op1=ALU.add)
                for tap in range(1, 4):
                    dve("scalar_tensor_tensor", [rxr, rvT[l], rtmp[0]], [rtmp[0]], out=ta[:], in0=xr[:, tap:tap + 512],
                        scalar=vT[l][:, 60 + tap * 12 + c:61 + tap * 12 + c], in1=ta[:], op0=ALU.mult, op1=ALU.add)
                if c < 8:
                    act(tmps[1][:], ta[:], AF.Silu, [rtmp[0]], [rtmp[1]])
                    bt = PSA.get()
                    for t in range(NT):
                        tr(psb(bt, 128, t * 128), tmps[1][:, t * 128:(t + 1) * 128], ident_f, [rtmp[1], rcf],
                           [psr[bt]], inc=(t == NT - 1))
                    act(xs_tok[:, :, c * 128:(c + 1) * 128], psb(bt).rearrange("p (t f) -> p t f", t=4), AF.Copy,
                        [psr[bt]], [rxs])
                else:
                    act(bcT[:, c - 8, :], ta[:], AF.Silu, [rtmp[0]], [rbc])
        b = PSA.get()
        pT = psb(b).bitcast(BF16)
        for t in range(NT):
            for g in range(2):
                tr(pT[:, (t * 2 + g) * 128:(t * 2 + g + 1) * 128], bcT[:, g, t * 128:(t + 1) * 128], ident_b,
                   [rbc, rcb], [psr[b]], inc=(t == NT - 1 and g == 1))
        act(bm_tok[:].rearrange("p t g n -> p (t g n)"), pT, AF.Copy, [psr[b]], [rbmt])
        i = wslot()
        wdt = wbuf[i][:, 0:256].rearrange("p (k n) -> p k n", k=16)
        dma("pool", wdt, win3[:, :, O_DT:O_DT + 16], [], [rw[i]], f"w{i}")
        for t in range(NT):
            ts = slice(t * 128, (t + 1) * 128)
            b = PSA.get()
            for k in range(KC):
                mm(psb(b, 16), hT[:, k, ts], wdt[:, k, :], k == 0, k == KC - 1, [rhT[k], rw[i]], [psr[b]])
            dve("tensor_tensor", [psr[b], rsp], [rdt], out=dtall[:, t, :], in0=psb(b, 16), in1=dtb[l][:], op=ALU.add)
        act(dtall[:], dtall[:], AF.Exp, [rdt], [rdt])
        act(dtall[:], dtall[:], AF.Ln, [rdt, rcst], [rdt], bias=ONE, scale=1.0)
        dve("tensor_tensor", [rdt, rsp], [rdt], out=daall[:], in0=dtall[:], in1=vw(aneg[l][:], [[0, 4], [1, 16]]),
            op=ALU.mult)
        act(Sbf[:], Sst[l][:], AF.Copy, [rS[l]], [rSbf])
        for t in range(NT):
            ts = slice(t * 128, (t + 1) * 128)
            da = daall[:, t, :]
            dt_ = dtall[:, t, :]
            b = PSA.get()
            mm(psb(b, 16), U_f, da, True, True, [rcf, rdt], [psr[b]], inc=False)
            mm(psb(b, 16, 16), ones_f, da, True, True, [rcf, rdt], [psr[b]], inc=True)
            acs = sm[:, 64:96]
            act(acs, psb(b, 32), AF.Copy, [psr[b]], [rsm])
            ea = sm[:, 96:112]
            dsd = sm[:, 112:128]
            cd = sm[:, 128:144]
            w2 = sm[:, 144:160]
            act(ea, sm[:, 64:80], AF.Exp, [rsm], [rsm])
            act(cd, sm[:, 80:96], AF.Exp, [rsm], [rsm])
            dve("tensor_tensor", [rsm], [rsm], out=dsd, in0=sm[:, 80:96], in1=sm[:, 64:80], op=ALU.subtract)
            act(dsd, dsd, AF.Exp, [rsm], [rsm])
            dve("tensor_tensor", [rsm, rdt], [rsm], out=w2, in0=dsd, in1=dt_, op=ALU.mult)
            xs3 = xs_tok[:, t, :].rearrange("p (h d) -> p h d", h=16)
            dve("tensor_tensor", [rxs, rdt], [rxdt], out=xdt[:], in0=xs3, in1=vw(dt_, [[1, 16], [0, 64]]), op=ALU.mult)
            dve("tensor_tensor", [rxs, rsm], [rxdt2], out=xdt2[:], in0=xs3, in1=vw(w2, [[1, 16], [0, 64]]), op=ALU.mult)
            b = PSA.get()
            for g in range(2):
                mm(psb(b, 128, g * 128), bcT[:, g, ts], bcT[:, 2 + g, ts], True, True, [rbc], [psr[b]], inc=(g == 1))
            dve("tensor_tensor", [psr[b], rcf], [rcbm], out=cbm[:], in0=psb(b, 256).rearrange("p (g l) -> p g l", g=2),
                in1=vw(U_f, [[0, 2], [1, 128]]), op=ALU.mult)
            for g in range(2):
                dag = daall[:, t, g * 8:(g + 1) * 8]
                dve("tensor_tensor", [rdt, rcf], [rT[0]], out=T[0].rearrange("p (h l) -> p h l", h=8),
                    in0=vw(dag, [[1, 8], [0, 128]]), in1=vw(U_f, [[0, 8], [1, 128]]), op=ALU.mult)
                dve("tensor_copy", [rdt], [rT[1]], out=T[1].rearrange("p (h l) -> p h l", h=8),
                    in_=vw(dag, [[1, 8], [0, 128]]))
                b = PSA.get(2)
                for hf in range(2):
                    mm(psb(b + hf), ones_f, T[0][:, hf * 512:(hf + 1) * 512], True, False, [rcf, rT[0]], [psr[b + hf]],
                       inc=False)
                    mm(psb(b + hf), negU_f, T[1][:, hf * 512:(hf + 1) * 512], False, True, [rcf, rT[1]], [psr[b + hf]],
                       inc=True)
                dve("tensor_scalar", [psr[b], psr[b + 1]], [rT[2]], out=T[2], in0=ps[:, b * 512:(b + 2) * 512],
                    scalar1=0.0, scalar2=None, op0=ALU.min)
                act(T[3], T[2], AF.Exp, [rT[2]], [rT[3]])
                dve("tensor_tensor", [rT[3], rcbm], [rM[g]], out=M_bf[:, g, :, :],
                    in0=T[3].rearrange("p (h l) -> p h l", h=8), in1=vw(cbm[:, g, :], [[0, 8], [1, 128]]), op=ALU.mult)
            b = PSA.get(2)
            for h in range(16):
                mm(ps[:, b * 512 + h * 64: b * 512 + (h + 1) * 64], M_bf[:, h // 8, h % 8, :], xdt[:, h, :], True, True,
                   [rM[h // 8], rxdt], [psr[b + h // 8]], inc=(h % 8 == 7))
            b2 = PSA.get(2)
            for g in range(2):
                mm(psb(b2 + g), bcT[:, 2 + g, ts], Sbf[:, g * 512:(g + 1) * 512], True, True, [rbc, rSbf], [psr[b2 + g]])
            dve("tensor_tensor", [psr[b2], psr[b2 + 1], rsm], [rT[4]], out=T[4].rearrange("p (h d) -> p h d", h=16),
                in0=ps[:, b2 * 512:(b2 + 2) * 512].rearrange("p (h d) -> p h d", h=16),
                in1=vw(ea, [[1, 16], [0, 64]]), op=ALU.mult)
            dve("tensor_tensor", [psr[b], psr[b + 1], rT[4]], [rT[4]], out=T[4], in0=ps[:, b * 512:(b + 2) * 512],
                in1=T[4], op=ALU.add)
            dve("tensor_tensor", [rxs, rsp], [rT[5]], out=T[5].rearrange("p (h d) -> p h d", h=16), in0=xs3,
                in1=vw(dsk[l][:], [[1, 16], [0, 64]]), op=ALU.mult)
            dve("tensor_tensor", [rT[4], rT[5]], [rT[4]], out=T[4], in0=T[4], in1=T[5], op=ALU.add)
            b3 = PSA.get(2)
            for g in range(2):
                mm(psb(b3 + g), bm_tok[:, t, g, :], xdt2[:, g * 8:(g + 1) * 8, :].rearrange("p h d -> p (h d)"), True,
                   True, [rbmt, rxdt2], [psr[b3 + g]])
            dve("tensor_tensor", [rS[l], rsm], [rS[l]], out=Sst[l][:].rearrange("p (h d) -> p h d", h=16),
                in0=Sst[l][:].rearrange("p (h d) -> p h d", h=16), in1=vw(cd, [[1, 16], [0, 64]]), op=ALU.mult)
            dve("tensor_tensor", [psr[b3], psr[b3 + 1], rS[l]], [rS[l]], out=Sst[l][:], in0=ps[:, b3 * 512:(b3 + 2) * 512],
                in1=Sst[l][:], op=ALU.add)
            act(Sbf[:], Sst[l][:], AF.Copy, [rS[l]], [rSbf])
            dve("tensor_tensor", [rT[4]] + rzs, [rT[5]], out=T[5], in0=T[4], in1=zs[:, t, :], op=ALU.mult)
            for g in range(2):
                S.op("act", lambda e, g=g: e.activation(out=T[3][:, g * 512:(g + 1) * 512],
                                                        in_=T[5][:, g * 512:(g + 1) * 512], func=AF.Square,
                                                        accum_out=sm[:, 160 + g:161 + g]), [rT[5]], [rT[3], rsm])
            act(sm[:, 162:164], sm[:, 160:162], AF.Sqrt, [rsm, rcst], [rsm], scale=1.0 / 512, bias=EPS6)
            dve("reciprocal", [rsm], [rsm], out=sm[:, 162:164], in_=sm[:, 162:164])
            for g in range(2):
                act(obf[:, g * 512:(g + 1) * 512], T[5][:, g * 512:(g + 1) * 512], AF.Copy, [rT[5], rsm], [robf],
                    scale=sm[:, 162 + g:163 + g])
            to_mixT(obf, 8, 8, t, l, 40, robf)

    def out_proj(l, gate):
        wo3 = prm["w_out"][l].rearrange("(k p) n -> p k n", p=128)
        for gi in range(4):
            wo, ro = load_w(wo3[:, :, gi * 512:(gi + 1) * 512], 16, 512)
            for mi in range(4):
                m = gi * 4 + mi
                b = PSA.get()
                for k in range(KC):
                    mm(psb(b), wo[:, k, mi * 128:(mi + 1) * 128], mixT[:, k, :], k == 0, k == KC - 1, [rmix[k], ro],
                       [psr[b]])
                dve("scalar_tensor_tensor", [psr[b], rxT[m], rmod], [rxT[m]], out=xT[:, m, :], in0=psb(b),
                    scalar=gate[:, m:m + 1], in1=xT[:, m, :], op0=ALU.mult, op1=ALU.add)

    def mlp(l, gate):
        w13 = prm["w_mlp1"][l].rearrange("(k p) n -> p k n", p=128)
        for f in range(16):
            w1, r1 = load_w(w13[:, :, f * 512:(f + 1) * 512], 16, 512)
            w2, r2 = load_w(prm["w_mlp2"][l][f * 512:(f + 1) * 512, :].rearrange("(c p) n -> p c n", p=128), 4, 2048)
            hb = (f % 2) * 4
            for cc in range(4):
                b = PSA.get()
                for k in range(KC):
                    mm(psb(b), w1[:, k, cc * 128:(cc + 1) * 128], hT[:, k, :], k == 0, k == KC - 1, [rhT[k], r1],
                       [psr[b]])
                act(tmps[cc % 2][:], psb(b), AF.Relu, [psr[b]], [rtmp[cc % 2]])
                dve("tensor_tensor", [rtmp[cc % 2]], [rmix[hb + cc]], out=mixT[:, hb + cc, :], in0=tmps[cc % 2][:],
                    in1=tmps[cc % 2][:], op=ALU.mult)
            for m in range(KC):
                b = PSA.get()
                for cc in range(4):
                    mm(psb(b), w2[:, cc, m * 128:(m + 1) * 128], mixT[:, hb + cc, :], cc == 0, cc == 3,
                       [rmix[hb + cc], r2], [psr[b]])
                dve("scalar_tensor_tensor", [psr[b], rxT[m], rmod], [rxT[m]], out=xT[:, m, :], in0=psb(b),
                    scalar=gate[:, m:m + 1], in1=xT[:, m, :], op0=ALU.mult, op1=ALU.add)

    for s in range(nseq):
        for l in range(2):
            dve("memset", [], [rS[l]], ap=Sst[l][:], constant=0.0)
            dve("memset", [], [rhist[l]], ap=hist[l][:], constant=0.0)
        for blk in range(nblk):
            tok0 = blk * TB
            for t in range(NT):
                dma("sp", xin, x[s, tok0 + t * 128: tok0 + (t + 1) * 128, :], [], rxin, "xin")
                for j in range(4):
                    b = PSA.get()
                    for i4 in range(4):
                        k = j * 4 + i4
                        tr(psb(b, 128, i4 * 128), xin[:, k * 128:(k + 1) * 128], ident_f, rxin + [rcf], [psr[b]],
                           inc=(i4 == 3))
                    act(xT[:, j * 4:(j + 1) * 4, t * 128:(t + 1) * 128], psb(b).rearrange("p (i f) -> p i f", i=4),
                        AF.Copy, [psr[b]], [rxT[j * 4 + i4] for i4 in range(4)])
            for l in range(nlayer):
                i = l * 2 + s
                mo = modT[:, i * 96:(i + 1) * 96]
                win3 = prm["w_in"][l].rearrange("(k p) n -> p k n", p=128)
                dma("sp", lnG[:], prm["gm_ln_g"][l].rearrange("h e -> (h e)").partition_broadcast(128), [], [rln], "ln")
                dma("sp", lnB[:], prm["gm_ln_b"][l].rearrange("h e -> (h e)").partition_broadcast(128), [], [rln], "ln")
                norm_to_hT(gsc[:, i * 32:i * 32 + 16], mo[:, 0:16])
                if do_a:
                    mixer_a(l, win3)
                else:
                    for k in range(0, 4):
                        dve("memset", [], [rmix[k]], ap=mixT[:, k, :], constant=0.0)
                if do_b:
                    mixer_b(l, win3, blk * NT)
                else:
                    for k in range(4, 8):
                        dve("memset", [], [rmix[k]], ap=mixT[:, k, :], constant=0.0)
                if do_c:
                    mixer_c(l, win3)
                else:
                    for k in range(8, 16):
                        dve("memset", [], [rmix[k]], ap=mixT[:, k, :], constant=0.0)
                out_proj(l, mo[:, 32:48])
                if do_mlp:
                    norm_to_hT(gsc[:, i * 32 + 16:i * 32 + 32], mo[:, 48:64])
                    mlp(l, mo[:, 80:96])
            dma("sp", fgb, prm["final_norm_g"].partition_broadcast(128), [], rfgb, "fgb")
            for t in range(NT):
                b = PSA.get(4)
                for k in range(KC):
                    tr(ps[:, b * 512 + k * 128: b * 512 + (k + 1) * 128], xT[:, k, t * 128:(t + 1) * 128], ident_f,
                       [rxT[k], rcf], [psr[b + k // 4]], inc=(k % 4 == 3))
                pr4 = [psr[b + q] for q in range(4)]
                S.op("act", lambda e, b=b: e.activation(out=xin, in_=ps[:, b * 512:(b + 4) * 512], func=AF.Square,
                                                        accum_out=sm[:, 170:171]), pr4, rxin + [rsm])
                act(sm[:, 171:172], sm[:, 170:171], AF.Sqrt, [rsm, rcst], [rsm], scale=1.0 / D, bias=EPS6)
                dve("reciprocal", [rsm], [rsm], out=sm[:, 171:172], in_=sm[:, 171:172])
                dve("scalar_tensor_tensor", pr4 + [rsm] + rfgb, rxin, out=xin, in0=ps[:, b * 512:(b + 4) * 512],
                    scalar=sm[:, 171:172], in1=fgb, op0=ALU.mult, op1=ALU.mult)
                dma("sp", out[s, tok0 + t * 128: tok0 + (t + 1) * 128, :], xin, rxin, [], "xin")

    with nc.allow_non_contiguous_dma(reason="small strided parameter loads"):
        S.emit(final_waits=["xin"])
    es.close()
    return nc


def make_consts():
    c = np.zeros((128, 640), np.float32)
    i = np.arange(128)
    c[:, C_ID:C_ID + 128] = np.eye(128, dtype=np.float32)
    u = (i[:, None] <= i[None, :]).astype(np.float32)
    c[:, C_U:C_U + 128] = u
    c[:, C_NU:C_NU + 128] = -u
    c[:, C_SL:C_SL + 128] = 1.0 - u
    c[:, C_ONE:C_ONE + 128] = 1.0
    return c


_NC_CACHE = {}


def kernel(**inputs):
    n = 8
    key = "full"
    if key not in _NC_CACHE:
        _NC_CACHE[key] = build()
    nc = _NC_CACHE[key]
    x = np.ascontiguousarray(inputs["x"], dtype=np.float32)
    c = np.ascontiguousarray(inputs["c"], dtype=np.float32)
    consts = make_consts()
    params = {k: np.ascontiguousarray(inputs[k], dtype=np.float32) for k in PARAM_NAMES}
    in_maps = []
    for i in range(n):
        m = {"x": x[2 * i:2 * i + 2], "c": c[2 * i:2 * i + 2], "consts": consts}
        m.update(params)
        in_maps.append(m)
    res = run_bass_kernel_spmd(nc, in_maps, core_ids=list(range(n)))
    return np.concatenate([r["out"] for r in res.results], axis=0)
```

```python
import contextlib
import numpy as np
import concourse.bass as bass
import concourse.mybir as mybir
from concourse.bass_utils import run_bass_kernel_spmd

F32 = mybir.dt.float32
BF16 = mybir.dt.bfloat16
AF = mybir.ActivationFunctionType
ALU = mybir.AluOpType

ENGS = ("pe", "act", "dve", "pool", "sp")


class Reg:
    __slots__ = ("name", "w", "rd")

    def __init__(self, name):
        self.name = name
        self.w = None
        self.rd = {}


class Rec:
    __slots__ = ("eng", "fn", "inc", "deps", "dma_sem", "cnt", "idx", "dma_waits")

    def __init__(self, eng, fn, inc, dma_sem):
        self.eng = eng
        self.fn = fn
        self.inc = inc
        self.deps = []
        self.dma_waits = {}
        self.dma_sem = dma_sem
        self.cnt = None
        self.idx = None


class Sched:
    def __init__(self, nc):
        self.nc = nc
        self.recs = {e: [] for e in ENGS}
        self.dma_cnt = {}
        self.n = 0

    def _dep(self, w, d):
        if d is None or d is w:
            return
        if d.dma_sem is not None:
            k = d.dma_sem
            w.dma_waits[k] = max(w.dma_waits.get(k, 0), self.dma_cnt[k])
            return
        if d.eng == "pe" and w.eng == "pe" and w.dma_sem is None:
            return
        if not d.inc:
            lst = self.recs[d.eng]
            tgt = None
            for j in range(d.idx + 1, len(lst)):
                if lst[j].inc and lst[j].dma_sem is None:
                    tgt = lst[j]
                    break
            if tgt is None:
                d.inc = True
                tgt = d
            d = tgt
        w.deps.append(d)

    def op(self, eng, fn, reads=(), writes=(), inc=True, dma=None):
        r = Rec(eng, fn, inc if dma is None else False, dma)
        r.idx = len(self.recs[eng])
        for x in reads:
            self._dep(r, x.w)
        for x in writes:
            self._dep(r, x.w)
            for rr in x.rd.values():
                self._dep(r, rr)
        if dma is not None:
            self.dma_cnt[dma] = self.dma_cnt.get(dma, 0) + 16
            r.cnt = self.dma_cnt[dma]
        self.recs[eng].append(r)
        for x in writes:
            x.w = r
            x.rd = {}
        for x in reads:
            x.rd[eng if dma is None else (dma, r.cnt)] = r
        self.n += 1
        return r

    def emit(self, final_waits=()):
        nc = self.nc
        with contextlib.ExitStack() as es:
            sems = {}
            for e in ENGS:
                sems[e] = es.enter_context(nc.semaphore("s_" + e))
            for k in self.dma_cnt:
                sems[("dma", k)] = es.enter_context(nc.semaphore("d_" + k))
            for e in ENGS:
                c = 0
                for r in self.recs[e]:
                    if r.dma_sem is None and r.inc:
                        c += 1
                        r.cnt = c
            block = es.enter_context(nc.Block())

            def run(e, eng):
                waited = {}
                for r in self.recs[e]:
                    need = {}
                    for d in r.deps:
                        need[d.eng] = max(need.get(d.eng, 0), d.cnt)
                    for k, v in r.dma_waits.items():
                        need[("dma", k)] = max(need.get(("dma", k), 0), v)
                    for k, v in need.items():
                        if waited.get(k, 0) < v:
                            eng.wait_ge(sems[k], v)
                            waited[k] = v
                    ins = r.fn(eng)
                    if r.dma_sem is not None:
                        ins.then_inc(sems[("dma", r.dma_sem)], 16)
                    elif r.inc:
                        ins.then_inc(sems[e], 1)
                if e == "sp":
                    for k in final_waits:
                        eng.wait_ge(sems[("dma", k)], self.dma_cnt[k])

            @block.tensor
            def _(eng):
                run("pe", eng)

            @block.scalar
            def _(eng):
                run("act", eng)

            @block.vector
            def _(eng):
                run("dve", eng)

            @block.gpsimd
            def _(eng):
                run("pool", eng)

            @block.sync
            def _(eng):
                run("sp", eng)


def vw(ap, dims):
    return bass.AP(ap.tensor, ap.offset, [list(ap.ap[0])] + [list(d) for d in dims])


D = 2048
KC = 16
TB = 512
NT = 4
SEQ = 2048
NW = 2
O_U, O_V, O_Q, O_K, O_VV, O_Z, O_X, O_DT = 0, 512, 1024, 1536, 1664, 1792, 2816, 4352
C_ID, C_U, C_NU, C_SL, C_ONE = 0, 128, 256, 384, 512

PARAM_NAMES = ["ada_w", "ada_b", "norm1_g", "w_in", "gm_ln_g", "gm_ln_b", "gm_ws", "gm_bs",
               "gm_norm_g", "attn_sinks", "attn_norm_g", "conv_w", "conv_b", "dt_bias", "a_log",
               "d_skip", "ssm_norm_g", "w_out", "norm2_g", "w_mlp1", "w_mlp2", "final_norm_g"]
PARAM_SHAPES = {
    "ada_w": [2, 2048, 12288], "ada_b": [2, 12288], "norm1_g": [2, 2048], "w_in": [2, 2048, 4368],
    "gm_ln_g": [2, 4, 128], "gm_ln_b": [2, 4, 128], "gm_ws": [2, 4, 128, 128], "gm_bs": [2, 4, 128],
    "gm_norm_g": [2, 512], "attn_sinks": [2, 8], "attn_norm_g": [2, 512], "conv_w": [2, 4, 1536],
    "conv_b": [2, 1536], "dt_bias": [2, 16], "a_log": [2, 16], "d_skip": [2, 16],
    "ssm_norm_g": [2, 1024], "w_out": [2, 2048, 2048], "norm2_g": [2, 2048],
    "w_mlp1": [2, 2048, 8192], "w_mlp2": [2, 8192, 2048], "final_norm_g": [2048],
}


def slim_shapes(nlayer, do_mlp):
    sh = {k: list(v) for k, v in PARAM_SHAPES.items()}
    for k in ("ada_w", "w_in", "w_out", "w_mlp1", "w_mlp2"):
        sh[k][0] = nlayer
    if not do_mlp:
        sh["w_mlp1"] = [1, 128, 128]
        sh["w_mlp2"] = [1, 128, 128]
    return sh


def build(nseq=2, nblk=4, nlayer=2, do_a=True, do_b=True, do_c=True, do_mlp=True, slim=False):
    nc = bass.Bass("TRN2", target_bir_lowering=False)
    din = lambda n, s: nc.dram_tensor(n, s, F32, kind="ExternalInput").ap()
    x = din("x", [2, SEQ, D])
    c_in = din("c", [2, D])
    shp = slim_shapes(nlayer, do_mlp) if slim else PARAM_SHAPES
    prm = {n: din(n, shp[n]) for n in PARAM_NAMES}
    consts = din("consts", [128, 640])
    out = nc.dram_tensor("out", [2, SEQ, D], F32, kind="ExternalOutput").ap()
    mod_d = nc.dram_tensor("mod_d", [2, 2, 12288], F32).ap()

    es = contextlib.ExitStack()
    S = Sched(nc)

    def sb(name, shape, dt=F32):
        return es.enter_context(nc.sbuf_tensor(name, shape, dt))

    xT = sb("xT", [128, KC, TB]); rxT = [Reg(f"xT{k}") for k in range(KC)]
    hT = sb("hT", [128, KC, TB], BF16); rhT = [Reg(f"hT{k}") for k in range(KC)]
    mixT = sb("mixT", [128, KC, TB], BF16); rmix = [Reg(f"mix{k}") for k in range(KC)]
    wbuf = [sb(f"wbuf{i}", [128, 8192], BF16) for i in range(NW)]
    rw = [Reg(f"w{i}") for i in range(NW)]
    TT = sb("TT", [128, 6 * 1024]); rT = [Reg(f"T{i}") for i in range(6)]
    T = [TT[:, i * 1024:(i + 1) * 1024] for i in range(6)]
    xin = TT[:, 2048:4096]; rxin = [rT[2], rT[3]]
    fgb = TT[:, 0:2048]; rfgb = [rT[0], rT[1]]
    cf = sb("cf", [128, 640]); rcf = Reg("cf")
    cb = sb("cb", [128, 640], BF16); rcb = Reg("cb")
    cst = sb("cst", [128, 4]); rcst = Reg("cst")
    vrow = sb("vrow", [128, 128]); rvrow = Reg("vrow")
    vT = [sb(f"vT{l}", [128, 112]) for l in range(2)]; rvT = [Reg(f"vT{l}") for l in range(2)]
    vTf = sb("vTf", [128, 16]); rvTf = Reg("vTf")
    modT = sb("modT", [128, 4 * 96]); rmod = Reg("modT")
    gsc = sb("gsc", [128, 4 * 32]); rgsc = Reg("gsc")
    lnG = sb("lnG", [128, 512]); lnB = sb("lnB", [128, 512]); rln = Reg("ln")
    WsT = [sb(f"WsT{l}", [128, 4, 128], BF16) for l in range(2)]; rws = [Reg(f"ws{l}") for l in range(2)]
    esink = [sb(f"esink{l}", [128, 8]) for l in range(2)]
    dtb = [sb(f"dtb{l}", [128, 16]) for l in range(2)]
    aneg = [sb(f"aneg{l}", [128, 16]) for l in range(2)]
    dsk = [sb(f"dsk{l}", [128, 16]) for l in range(2)]
    rsp = Reg("smallparams")
    rstd_b = sb("rstd_b", [128, TB]); rrstd = Reg("rstd")
    tmps = [sb(f"tmp{i}", [128, TB]) for i in range(2)]; rtmp = [Reg(f"tmp{i}") for i in range(2)]
    AB = sb("AB", [128, 4096], BF16)
    qT = AB[:, 0:2048].rearrange("p (j t) -> p j t", j=4); rqT = Reg("qT")
    e_cur = AB[:, 2048:3072]; recur = Reg("ecur")
    e_prev = AB[:, 3072:4096]; reprev = Reg("eprev")
    zs = AB[:, :].rearrange("p (t f) -> p t f", t=4); rzs = [rqT, recur, reprev]
    kT = [sb(f"kT{l}", [128, 2, 640], BF16) for l in range(2)]; rkT = [Reg(f"kT{l}") for l in range(2)]
    vaug = [sb(f"vaug{l}", [128, 5, 2, 72], BF16) for l in range(2)]
    rva = [[Reg(f"va{l}_{i}") for i in range(5)] for l in range(2)]
    xs_tok = sb("xs_tok", [128, 4, 1024], BF16); rxs = Reg("xs_tok")
    bcT = sb("bcT", [128, 4, TB], BF16); rbc = Reg("bcT")
    bm_tok = sb("bm_tok", [128, 4, 2, 128], BF16); rbmt = Reg("bm_tok")
    xraw = sb("xraw", [128, 2, 515]); rxraw = [Reg("xraw0"), Reg("xraw1")]
    hist = [sb(f"hist{l}", [128, 12, 3]) for l in range(2)]; rhist = [Reg(f"hist{l}") for l in range(2)]
    Sst = [sb(f"Sst{l}", [128, 1024]) for l in range(2)]; rS = [Reg(f"S{l}") for l in range(2)]
    Sbf = sb("Sbf", [128, 1024], BF16); rSbf = Reg("Sbf")
    M_bf2 = [sb(f"M_bf{i}", [128, 2, 8, 128], BF16) for i in range(2)]; rM2 = [[Reg(f"M{i}{g}") for g in range(2)] for i in range(2)]
    xdtb = [sb(f"xdt{i}", [128, 16, 64], BF16) for i in range(2)]; rxdtb = [Reg(f"xdt{i}") for i in range(2)]
    xdt2b = [sb(f"xdt2{i}", [128, 16, 64], BF16) for i in range(2)]; rxdt2b = [Reg(f"xdt2{i}") for i in range(2)]
    vn = sb("vn", [128, 512], BF16); rvn = Reg("vn")
    obf = sb("obf", [128, 1024], BF16); robf = Reg("obf")
    sm = sb("sm", [128, 256]); rsm = Reg("sm")
    rsmc = [Reg("smc0"), Reg("smc1")]; rsmg = Reg("smg"); rsmf = Reg("smf")
    dtall = sb("dtall", [128, 4, 16]); daall = sb("daall", [128, 4, 16]); rdt = Reg("dt")
    cbm = sb("cbm", [128, 2, 128]); rcbm = Reg("cbm")
    cTf = sb("cTf", [128, KC, 2]); cTb = sb("cTb", [128, KC, 2], BF16); rcT = Reg("cT")
    abt = TT[0:2, 4096:4608]; rabt = rT[4]
    mrow = TT[0:2, 5120:5632]; rmrow = rT[5]

    ps = es.enter_context(nc.psum_tensor("ps", [128, 4096], F32))
    psr = [Reg(f"ps{b}") for b in range(8)]

    class PSA:
        i = 0

        @staticmethod
        def get(n=1):
            while PSA.i % n:
                PSA.i += 1
            b = PSA.i % 8
            PSA.i += n
            return b

    def psb(b, n=512, off=0):
        return ps[:, b * 512 + off: b * 512 + off + n]

    def mm(o, lhsT, rhs, start, stop, rd, wr, inc=None):
        S.op("pe", lambda e: e.matmul(o, lhsT, rhs, start=start, stop=stop), rd, wr,
             inc=(stop if inc is None else inc))

    def tr(o, i, ident, rd, wr, inc=True):
        S.op("pe", lambda e: e.transpose(o, i, ident), rd, wr, inc=inc)

    def act(o, i, func, rd, wr, **kw):
        S.op("act", lambda e: e.activation(out=o, in_=i, func=func, **kw), rd, wr)

    def dve(meth, rd, wr, **kw):
        S.op("dve", lambda e: getattr(e, meth)(**kw), rd, wr)

    def dma(q, o, i, rd, wr, key):
        S.op(q, lambda e: e.dma_start(out=o, in_=i), rd, wr, dma=key)

    class WL:
        i = 0

    def wslot():
        i = WL.i % NW
        WL.i += 1
        return i

    def load_w(src3, k, n):
        i = wslot()
        v = wbuf[i][:, 0:k * n].rearrange("p (k n) -> p k n", k=k)
        dma("pool", v, src3, [], [rw[i]], f"w{i}")
        return v, rw[i]

    ident_f = cf[:, C_ID:C_ID + 128]
    U_f = cf[:, C_U:C_U + 128]
    negU_f = cf[:, C_NU:C_NU + 128]
    ones_f = cf[:, C_ONE:C_ONE + 128]
    ident_b = cb[:, C_ID:C_ID + 128]
    U_b = cb[:, C_U:C_U + 128]
    SL_b = cb[:, C_SL:C_SL + 128]
    ones_b = cb[:, C_ONE:C_ONE + 128]
    EPS6 = cst[:, 0:1]
    EPS5 = cst[:, 1:2]
    ONE = cst[:, 2:3]

    dma("sp", cf[:], consts, [], [rcf], "const")
    dma("pool", cb[:], consts, [], [rcb], "constb")
    dve("memset", [], [rcst], ap=cst[:, 0:1], constant=1e-6)
    dve("memset", [], [rcst], ap=cst[:, 1:2], constant=1e-5)
    dve("memset", [], [rcst], ap=cst[:, 2:3], constant=1.0)
    dve("memset", [], [rcst], ap=cst[:, 3:4], constant=0.0)
    for l in range(2):
        for i in range(5):
            dve("memset", [], [rva[l][i]], ap=vaug[l][:, i, :, 64:72], constant=0.0)
            dve("memset", [], [rva[l][i]], ap=vaug[l][:, i, :, 64:65], constant=1.0)

    def vec_rows(l):
        rows = []
        rows.append((0, 16, prm["norm1_g"][l].rearrange("(k p) -> k p", p=128)))
        rows.append((16, 16, prm["norm2_g"][l].rearrange("(k p) -> k p", p=128)))
        rows.append((32, 4, prm["gm_norm_g"][l].rearrange("(k p) -> k p", p=128)))
        rows.append((36, 4, prm["attn_norm_g"][l].rearrange("(k p) -> k p", p=128)))
        rows.append((40, 8, prm["ssm_norm_g"][l].rearrange("(k p) -> k p", p=128)))
        rows.append((48, 12, prm["conv_b"][l].rearrange("(k p) -> k p", p=128)))
        rows.append((60, 48, prm["conv_w"][l].rearrange("t (k p) -> (t k) p", p=128)))
        rows.append((108, 4, prm["gm_bs"][l]))
        return rows

    for l in range(2):
        for (r0, n, src) in vec_rows(l):
            dma("sp", vrow[r0:r0 + n, :], src, [], [rvrow], "vrow")
        b = PSA.get()
        tr(psb(b, 112), vrow[0:112, :], ident_f[0:112, 0:112], [rvrow, rcf], [psr[b]])
        act(vT[l][:], psb(b, 112), AF.Copy, [psr[b]], [rvT[l]])
    dma("sp", vrow[0:16, :], prm["final_norm_g"].rearrange("(k p) -> k p", p=128), [], [rvrow], "vrow")
    b = PSA.get()
    tr(psb(b, 16), vrow[0:16, :], ident_f[0:16, 0:16], [rvrow, rcf], [psr[b]])
    act(vTf[:], psb(b, 16), AF.Copy, [psr[b]], [rvTf])

    for l in range(2):
        dma("sp", esink[l][:], prm["attn_sinks"][l].partition_broadcast(128), [], [rsp], "sp")
        dma("sp", dtb[l][:], prm["dt_bias"][l].partition_broadcast(128), [], [rsp], "sp")
        dma("sp", aneg[l][:], prm["a_log"][l].partition_broadcast(128), [], [rsp], "sp")
        dma("sp", dsk[l][:], prm["d_skip"][l].partition_broadcast(128), [], [rsp], "sp")
    for l in range(2):
        act(esink[l][:], esink[l][:], AF.Exp, [rsp], [rsp])
        act(aneg[l][:], aneg[l][:], AF.Exp, [rsp], [rsp])
        dve("tensor_scalar", [rsp], [rsp], out=aneg[l][:], in0=aneg[l][:], scalar1=-1.0, scalar2=None,
            op0=ALU.mult)
    for l in range(2):
        wsf = T[0].rearrange("p (h s) -> p h s", h=8)[:, 0:4, :]
        dma("sp", wsf, prm["gm_ws"][l].rearrange("h t s -> t h s"), [], [rT[0]], "wsf")
        b = PSA.get()
        for h in range(4):
            tr(psb(b, 128, h * 128), wsf[:, h, :], ident_f, [rT[0], rcf], [psr[b]], inc=(h == 3))
        dve("tensor_tensor", [psr[b], rcf], [rws[l]], out=WsT[l][:],
            in0=psb(b).rearrange("p (h t) -> p h t", h=4), in1=vw(U_f, [[0, 4], [1, 128]]), op=ALU.mult)

    for s_ in range(2):
        dma("sp", cTf[:, :, s_], c_in[s_].rearrange("(k p) -> p k", p=128), [], [rcT], "cT")
    act(cTb[:], cTf[:], AF.Silu, [rcT], [rcT])
    for l in range(nlayer):
        aw3 = prm["ada_w"][l].rearrange("(k p) n -> p k n", p=128)
        for n in range(24):
            w, rwi = load_w(aw3[:, :, n * 512:(n + 1) * 512], 16, 512)
            dma("sp", abt, prm["ada_b"][l][n * 512:(n + 1) * 512].partition_broadcast(2), [], [rabt], "abt")
            b = PSA.get()
            for k in range(KC):
                mm(ps[0:2, b * 512:(b + 1) * 512], cTb[:, k, :], w[:, k, :], k == 0, k == KC - 1,
                   [rcT, rwi], [psr[b]])
            dve("tensor_tensor", [psr[b], rabt], [rmrow], out=mrow, in0=ps[0:2, b * 512:(b + 1) * 512],
                in1=abt, op=ALU.add)
            dma("sp", mod_d[l, :, n * 512:(n + 1) * 512], mrow, [rmrow], [], "mrow")
    rmodd = Reg("mod_d")
    for l in range(nlayer):
        for s in range(2):
            i = l * 2 + s
            S.op("sp", lambda e, l=l, s=s: e.dma_start(out=vrow[0:96, :],
                                                       in_=mod_d[l, s].rearrange("(r p) -> r p", p=128)),
                 [], [rmrow, rvrow], dma="mrow")
            b = PSA.get()
            tr(psb(b, 96), vrow[0:96, :], ident_f[0:96, 0:96], [rvrow, rcf], [psr[b]])
            act(modT[:, i * 96:(i + 1) * 96], psb(b, 96), AF.Copy, [psr[b]], [rmod])
            dve("scalar_tensor_tensor", [rmod, rvT[l]], [rgsc], out=gsc[:, i * 32:i * 32 + 16],
                in0=modT[:, i * 96 + 16:i * 96 + 32], scalar=1.0, in1=vT[l][:, 0:16], op0=ALU.add, op1=ALU.mult)
            dve("scalar_tensor_tensor", [rmod, rvT[l]], [rgsc], out=gsc[:, i * 32 + 16:i * 32 + 32],
                in0=modT[:, i * 96 + 64:i * 96 + 80], scalar=1.0, in1=vT[l][:, 16:32], op0=ALU.add, op1=ALU.mult)

    def norm_to_hT(gs, sh):
        for k in range(KC):
            act(hT[:, k, :], xT[:, k, :], AF.Square, [rxT[k]], [rhT[k]])
        b = PSA.get()
        for k in range(KC):
            mm(psb(b), ones_b, hT[:, k, :], k == 0, k == KC - 1, [rhT[k], rcb], [psr[b]])
        act(rstd_b[:], psb(b), AF.Sqrt, [psr[b], rcst], [rrstd], scale=1.0 / D, bias=EPS6)
        dve("reciprocal", [rrstd], [rrstd], out=rstd_b[:], in_=rstd_b[:])
        for k in range(KC):
            t_ = tmps[k % 2]
            dve("scalar_tensor_tensor", [rxT[k], rrstd, rgsc], [rtmp[k % 2]], out=t_[:], in0=xT[:, k, :],
                scalar=gs[:, k:k + 1], in1=rstd_b[:], op0=ALU.mult, op1=ALU.mult)
            act(hT[:, k, :], t_[:], AF.Identity, [rtmp[k % 2], rmod], [rhT[k]], bias=sh[:, k:k + 1], scale=1.0)

    def rms_scale(src, n, ss_col):
        act(T[3][:, 0:n], src, AF.Square, [rsm], [rT[3], rsm], accum_out=sm[:, ss_col:ss_col + 1])
        act(sm[:, ss_col + 1:ss_col + 2], sm[:, ss_col:ss_col + 1], AF.Sqrt, [rsm, rcst], [rsm],
            scale=1.0 / n, bias=EPS6)
        dve("reciprocal", [rsm], [rsm], out=sm[:, ss_col + 1:ss_col + 2], in_=sm[:, ss_col + 1:ss_col + 2])

    def to_mixT(src_bf, nchunk, k0, t, l, gcol, rsrc):
        b = PSA.get()
        pT = psb(b).bitcast(BF16)
        for j in range(nchunk):
            tr(pT[:, j * 128:(j + 1) * 128], src_bf[:, j * 128:(j + 1) * 128], ident_b, [rsrc, rcb], [psr[b]],
               inc=(j == nchunk - 1))
        for j in range(nchunk):
            act(mixT[:, k0 + j, t * 128:(t + 1) * 128], pT[:, j * 128:(j + 1) * 128], AF.Copy,
                [psr[b], rvT[l]], [rmix[k0 + j]], scale=vT[l][:, gcol + j:gcol + j + 1])

    def mixer_a(l, win3):
        wu, ru = load_w(win3[:, :, O_U:O_U + 512], 16, 512)
        wv, rv = load_w(win3[:, :, O_V:O_V + 512], 16, 512)
        def proj(t):
            ts = slice(t * 128, (t + 1) * 128)
            b = PSA.get(2)
            for k in range(KC):
                mm(psb(b), hT[:, k, ts], wu[:, k, :], k == 0, k == KC - 1, [rhT[k], ru], [psr[b]])
            for k in range(KC):
                mm(psb(b + 1), hT[:, k, ts], wv[:, k, :], k == 0, k == KC - 1, [rhT[k], rv], [psr[b + 1]])
            return b

        bnext = proj(0)
        for t in range(NT):
            b = bnext
            if t + 1 < NT:
                bnext = proj(t + 1)
            act(T[0], ps[:, b * 512:(b + 2) * 512], AF.Gelu_apprx_tanh, [psr[b], psr[b + 1]], [rT[0]])
            ug = T[0][:, 0:512]
            vg = T[0][:, 512:1024]
            st6 = sm[:, 0:24].rearrange("p (h s) -> p h s", h=4)
            mv = sm[:, 24:32].rearrange("p (h s) -> p h s", h=4)
            for h in range(4):
                dve("bn_stats", [rT[0]], [rsm], out=st6[:, h, :], in_=vg[:, h * 128:(h + 1) * 128])
            for h in range(4):
                dve("bn_aggr", [rsm], [rsm], out=mv[:, h, :], in_=st6[:, h, :])
            act(sm[:, 32:36], mv[:, :, 1], AF.Sqrt, [rsm, rcst], [rsm], bias=EPS5, scale=1.0)
            dve("reciprocal", [rsm], [rsm], out=sm[:, 32:36], in_=sm[:, 32:36])
            for h in range(4):
                dve("tensor_scalar", [rT[0], rsm], [rT[1]], out=T[1][:, h * 128:(h + 1) * 128],
                    in0=vg[:, h * 128:(h + 1) * 128], scalar1=mv[:, h, 0:1], scalar2=sm[:, 32 + h:33 + h],
                    op0=ALU.subtract, op1=ALU.mult)
            dve("tensor_tensor", [rT[1], rln], [rT[1]], out=T[1][:, 0:512], in0=T[1][:, 0:512], in1=lnG[:],
                op=ALU.mult)
            dve("tensor_tensor", [rT[1], rln], [rvn], out=vn[:], in0=T[1][:, 0:512], in1=lnB[:], op=ALU.add)
            b2 = PSA.get()
            for h in range(4):
                mm(psb(b2, 128, h * 128), WsT[l][:, h, :], vn[:, h * 128:(h + 1) * 128], True, True,
                   [rws[l], rvn], [psr[b2]], inc=(h == 3))
            for h in range(4):
                dve("scalar_tensor_tensor", [psr[b2], rT[0], rvT[l]], [rT[2]], out=T[2][:, h * 128:(h + 1) * 128],
                    in0=psb(b2, 128, h * 128), scalar=vT[l][:, 108 + h:109 + h], in1=ug[:, h * 128:(h + 1) * 128],
                    op0=ALU.add, op1=ALU.mult)
            S.op("act", lambda e: e.activation(out=T[3][:, 0:512], in_=T[2][:, 0:512], func=AF.Square,
                                               accum_out=sm[:, 40:41]), [rT[2]], [rT[3], rsm])
            act(sm[:, 41:42], sm[:, 40:41], AF.Sqrt, [rsm, rcst], [rsm], scale=1.0 / 512, bias=EPS6)
            dve("reciprocal", [rsm], [rsm], out=sm[:, 41:42], in_=sm[:, 41:42])
            act(obf[:, 0:512], T[2][:, 0:512], AF.Copy, [rT[2], rsm], [robf], scale=sm[:, 41:42])
            to_mixT(obf, 4, 0, t, l, 32, robf)

    def mixer_b(l, win3, g0):
        wq, rq = load_w(win3[:, :, O_Q:O_Q + 512], 16, 512)
        wkv, rk = load_w(win3[:, :, O_K:O_K + 256], 16, 256)
        wvv = wkv[:, :, 128:256]
        for h in range(8):
            b = PSA.get()
            for k in range(KC):
                mm(ps[0:64, b * 512:(b + 1) * 512], wq[:, k, h * 64:(h + 1) * 64], hT[:, k, :], k == 0, k == KC - 1,
                   [rhT[k], rq], [psr[b]])
            act(mixT[0:64, 8 + h, :], ps[0:64, b * 512:(b + 1) * 512], AF.Copy, [psr[b]], [rmix[8 + h]])
        for a in range(2):
            b = PSA.get()
            for k in range(KC):
                mm(ps[0:64, b * 512:(b + 1) * 512], wkv[:, k, a * 64:(a + 1) * 64], hT[:, k, :], k == 0, k == KC - 1,
                   [rhT[k], rk], [psr[b]])
            act(kT[l][0:64, a, 128:640], ps[0:64, b * 512:(b + 1) * 512], AF.Copy, [psr[b]], [rkT[l]])
        for t in range(NT):
            ts = slice(t * 128, (t + 1) * 128)
            slot = (g0 + t) % 5
            b = PSA.get()
            for k in range(KC):
                mm(psb(b, 128), hT[:, k, ts], wvv[:, k, :], k == 0, k == KC - 1, [rhT[k], rk], [psr[b]])
            act(vaug[l][:, slot, :, 0:64], psb(b, 128).rearrange("p (a d) -> p a d", a=2), AF.Copy,
                [psr[b]], [rva[l][slot]])
        for t in range(NT):
            ts = slice(t * 128, (t + 1) * 128)
            g = g0 + t
            slot = g % 5
            pslot = (g - 1) % 5
            blocks = [(e_cur, recur, 128 + t * 128, U_b, slot)]
            if g > 0:
                blocks.append((e_prev, reprev, t * 128, SL_b, pslot))
            for (ebuf, reb, kc0, mask, _) in blocks:
                b = PSA.get(2)
                for h in range(8):
                    a = h // 4
                    mm(ps[:, b * 512 + h * 128: b * 512 + (h + 1) * 128], kT[l][0:64, a, kc0:kc0 + 128],
                       mixT[0:64, 8 + h, ts], True, True, [rkT[l], rmix[8 + h]], [psr[b + h // 4]], inc=(h % 4 == 3))
                act(ebuf, ps[:, b * 512:(b + 2) * 512], AF.Exp, [psr[b], psr[b + 1]], [reb], scale=0.125)
                dve("tensor_tensor", [reb, rcb], [reb], out=ebuf.rearrange("p (h q) -> p h q", h=8),
                    in0=ebuf.rearrange("p (h q) -> p h q", h=8), in1=vw(mask, [[0, 8], [1, 128]]), op=ALU.mult)
            bo = PSA.get(2)
            for h in range(8):
                a = h // 4
                dst = ps[:, (bo + h // 4) * 512 + (h % 4) * 128:(bo + h // 4) * 512 + (h % 4) * 128 + 66]
                if g > 0:
                    mm(dst, e_prev[:, h * 128:(h + 1) * 128], vaug[l][:, pslot, a, 0:66], True, False,
                       [reprev, rva[l][pslot]], [psr[bo + h // 4]], inc=False)
                mm(dst, e_cur[:, h * 128:(h + 1) * 128], vaug[l][:, slot, a, 0:66], g == 0, True,
                   [recur, rva[l][slot]], [psr[bo + h // 4]], inc=(h % 4 == 3))
            for i2 in range(2):
                pv = psb(bo + i2, 512).rearrange("p (h c) -> p h c", c=128)
                dve("tensor_tensor", [psr[bo + i2], rsp], [rsm], out=sm[:, 48 + i2 * 4:52 + i2 * 4],
                    in0=pv[:, :, 64], in1=esink[l][:, i2 * 4:(i2 + 1) * 4], op=ALU.add)
            dve("reciprocal", [rsm], [rsm], out=sm[:, 48:56], in_=sm[:, 48:56])
            for i2 in range(2):
                pv = psb(bo + i2, 512).rearrange("p (h c) -> p h c", c=128)
                dve("tensor_tensor", [psr[bo + i2], rsm], [rT[2]],
                    out=T[2][:, i2 * 256:(i2 + 1) * 256].rearrange("p (h d) -> p h d", h=4),
                    in0=pv[:, :, 0:64], in1=vw(sm[:, 48 + i2 * 4:52 + i2 * 4], [[1, 4], [0, 64]]), op=ALU.mult)
            S.op("act", lambda e: e.activation(out=T[3][:, 0:512], in_=T[2][:, 0:512], func=AF.Square,
                                               accum_out=sm[:, 60:61]), [rT[2]], [rT[3], rsm])
            act(sm[:, 61:62], sm[:, 60:61], AF.Sqrt, [rsm, rcst], [rsm], scale=1.0 / 512, bias=EPS6)
            dve("reciprocal", [rsm], [rsm], out=sm[:, 61:62], in_=sm[:, 61:62])
            act(obf[:, 0:512], T[2][:, 0:512], AF.Copy, [rT[2], rsm], [robf], scale=sm[:, 61:62])
            to_mixT(obf, 4, 4, t, l, 36, robf)
        act(kT[l][0:64, :, 0:128], kT[l][0:64, :, 512:640], AF.Copy, [rkT[l]], [rkT[l]])

    def mixer_c(l, win3):
        wz0, rz0 = load_w(win3[:, :, O_Z:O_Z + 512], 16, 512)
        wz1, rz1 = load_w(win3[:, :, O_Z + 512:O_Z + 1024], 16, 512)
        for t in range(NT):
            ts = slice(t * 128, (t + 1) * 128)
            b = PSA.get(2)
            for k in range(KC):
                mm(psb(b), hT[:, k, ts], wz0[:, k, :], k == 0, k == KC - 1, [rhT[k], rz0], [psr[b]])
            for k in range(KC):
                mm(psb(b + 1), hT[:, k, ts], wz1[:, k, :], k == 0, k == KC - 1, [rhT[k], rz1], [psr[b + 1]])
            act(zs[:, t, :], ps[:, b * 512:(b + 2) * 512], AF.Silu, [psr[b], psr[b + 1]], rzs)
        wx = []
        for i3 in range(3):
            wx.append(load_w(win3[:, :, O_X + i3 * 512:O_X + (i3 + 1) * 512], 16, 512))
            for cc in range(4):
                c = i3 * 4 + cc
                w_, rw_ = wx[i3]
                b = PSA.get()
                for k in range(KC):
                    mm(psb(b), w_[:, k, cc * 128:(cc + 1) * 128], hT[:, k, :], k == 0, k == KC - 1,
                       [rhT[k], rw_], [psr[b]])
                xr = xraw[:, c % 2, :]
                rxr = rxraw[c % 2]
                dve("tensor_copy", [rhist[l]], [rxr], out=xr[:, 0:3], in_=hist[l][:, c, :])
                act(xr[:, 3:515], psb(b), AF.Copy, [psr[b]], [rxr])
                dve("tensor_copy", [rxr], [rhist[l]], out=hist[l][:, c, :], in_=xr[:, 512:515])
                ta = tmps[0]
                dve("tensor_scalar", [rxr, rvT[l]], [rtmp[0]], out=ta[:], in0=xr[:, 0:512],
                    scalar1=vT[l][:, 60 + c:61 + c], scalar2=vT[l][:, 48 + c:49 + c], op0=ALU.mult, op1=ALU.add)
                for tap in range(1, 4):
                    dve("scalar_tensor_tensor", [rxr, rvT[l], rtmp[0]], [rtmp[0]], out=ta[:], in0=xr[:, tap:tap + 512],
                        scalar=vT[l][:, 60 + tap * 12 + c:61 + tap * 12 + c], in1=ta[:], op0=ALU.mult, op1=ALU.add)
                if c < 8:
                    act(obf[:, 0:512], ta[:], AF.Silu, [rtmp[0]], [robf])
                    bt = PSA.get()
                    pTx = psb(bt).bitcast(BF16)
                    for t in range(NT):
                        tr(pTx[:, t * 128:(t + 1) * 128], obf[:, t * 128:(t + 1) * 128], ident_b, [robf, rcb],
                           [psr[bt]], inc=(t == NT - 1))
                    act(xs_tok[:, :, c * 128:(c + 1) * 128], pTx[:, 0:512].rearrange("p (t f) -> p t f", t=4), AF.Copy,
                        [psr[bt]], [rxs])
                else:
                    act(bcT[:, c - 8, :], ta[:], AF.Silu, [rtmp[0]], [rbc])
        b = PSA.get()
        pT = psb(b).bitcast(BF16)
        for t in range(NT):
            for g in range(2):
                tr(pT[:, (t * 2 + g) * 128:(t * 2 + g + 1) * 128], bcT[:, g, t * 128:(t + 1) * 128], ident_b,
                   [rbc, rcb], [psr[b]], inc=(t == NT - 1 and g == 1))
        act(bm_tok[:].rearrange("p t g n -> p (t g n)"), pT, AF.Copy, [psr[b]], [rbmt])
        i = wslot()
        wdt = wbuf[i][:, 0:256].rearrange("p (k n) -> p k n", k=16)
        dma("pool", wdt, win3[:, :, O_DT:O_DT + 16], [], [rw[i]], f"w{i}")
        for t in range(NT):
            ts = slice(t * 128, (t + 1) * 128)
            b = PSA.get()
            for k in range(KC):
                mm(psb(b, 16), hT[:, k, ts], wdt[:, k, :], k == 0, k == KC - 1, [rhT[k], rw[i]], [psr[b]])
            dve("tensor_tensor", [psr[b], rsp], [rdt], out=dtall[:, t, :], in0=psb(b, 16), in1=dtb[l][:], op=ALU.add)
        act(dtall[:], dtall[:], AF.Exp, [rdt], [rdt])
        act(dtall[:], dtall[:], AF.Ln, [rdt, rcst], [rdt], bias=ONE, scale=1.0)
        dve("tensor_tensor", [rdt, rsp], [rdt], out=daall[:], in0=dtall[:], in1=vw(aneg[l][:], [[0, 4], [1, 16]]),
            op=ALU.mult)
        act(Sbf[:], Sst[l][:], AF.Copy, [rS[l]], [rSbf])
        def P1(t):
            par = t % 2
            ts = slice(t * 128, (t + 1) * 128)
            q0 = 64 + par * 96
            rs = rsmc[par]
            xdt, rxdt, xdt2, rxdt2, M_bf, rM = xdtb[par], rxdtb[par], xdt2b[par], rxdt2b[par], M_bf2[par], rM2[par]
            da = daall[:, t, :]
            dt_ = dtall[:, t, :]
            b = PSA.get()
            mm(psb(b, 16), U_f, da, True, True, [rcf, rdt], [psr[b]], inc=False)
            mm(psb(b, 16, 16), ones_f, da, True, True, [rcf, rdt], [psr[b]], inc=True)
            act(sm[:, q0:q0 + 32], psb(b, 32), AF.Copy, [psr[b]], [rs])
            ea = sm[:, q0 + 32:q0 + 48]
            dsd = sm[:, q0 + 48:q0 + 64]
            cd = sm[:, q0 + 64:q0 + 80]
            w2 = sm[:, q0 + 80:q0 + 96]
            act(ea, sm[:, q0:q0 + 16], AF.Exp, [rs], [rs])
            act(cd, sm[:, q0 + 16:q0 + 32], AF.Exp, [rs], [rs])
            dve("tensor_tensor", [rs], [rs], out=dsd, in0=sm[:, q0 + 16:q0 + 32], in1=sm[:, q0:q0 + 16], op=ALU.subtract)
            act(dsd, dsd, AF.Exp, [rs], [rs])
            dve("tensor_tensor", [rs, rdt], [rs], out=w2, in0=dsd, in1=dt_, op=ALU.mult)
            xs3 = xs_tok[:, t, :].rearrange("p (h d) -> p h d", h=16)
            dve("tensor_tensor", [rxs, rdt], [rxdt], out=xdt[:], in0=xs3, in1=vw(dt_, [[1, 16], [0, 64]]), op=ALU.mult)
            dve("tensor_tensor", [rxs, rs], [rxdt2], out=xdt2[:], in0=xs3, in1=vw(w2, [[1, 16], [0, 64]]), op=ALU.mult)
            b = PSA.get()
            for g in range(2):
                mm(psb(b, 128, g * 128), bcT[:, g, ts], bcT[:, 2 + g, ts], True, True, [rbc], [psr[b]], inc=(g == 1))
            dve("tensor_tensor", [psr[b], rcf], [rcbm], out=cbm[:], in0=psb(b, 256).rearrange("p (g l) -> p g l", g=2),
                in1=vw(U_f, [[0, 2], [1, 128]]), op=ALU.mult)
            yield
            for g in range(2):
                dag = daall[:, t, g * 8:(g + 1) * 8]
                dve("tensor_tensor", [rdt, rcf], [rT[0]], out=T[0].rearrange("p (h l) -> p h l", h=8),
                    in0=vw(dag, [[1, 8], [0, 128]]), in1=vw(U_f, [[0, 8], [1, 128]]), op=ALU.mult)
                dve("tensor_copy", [rdt], [rT[1]], out=T[1].rearrange("p (h l) -> p h l", h=8),
                    in_=vw(dag, [[1, 8], [0, 128]]))
                b = PSA.get(2)
                for hf in range(2):
                    mm(psb(b + hf), ones_f, T[0][:, hf * 512:(hf + 1) * 512], True, False, [rcf, rT[0]], [psr[b + hf]],
                       inc=False)
                    mm(psb(b + hf), negU_f, T[1][:, hf * 512:(hf + 1) * 512], False, True, [rcf, rT[1]], [psr[b + hf]],
                       inc=True)
                yield
                dve("tensor_scalar", [psr[b], psr[b + 1]], [rT[2]], out=T[2], in0=ps[:, b * 512:(b + 2) * 512],
                    scalar1=0.0, scalar2=None, op0=ALU.min)
                act(T[3], T[2], AF.Exp, [rT[2]], [rT[3]])
                dve("tensor_tensor", [rT[3], rcbm], [rM[g]], out=M_bf[:, g, :, :],
                    in0=T[3].rearrange("p (h l) -> p h l", h=8), in1=vw(cbm[:, g, :], [[0, 8], [1, 128]]), op=ALU.mult)
                yield

        def P2(t):
            par = t % 2
            ts = slice(t * 128, (t + 1) * 128)
            q0 = 64 + par * 96
            rs = rsmc[par]
            xdt, rxdt, xdt2, rxdt2, M_bf, rM = xdtb[par], rxdtb[par], xdt2b[par], rxdt2b[par], M_bf2[par], rM2[par]
            ea = sm[:, q0 + 32:q0 + 48]
            cd = sm[:, q0 + 64:q0 + 80]
            xs3 = xs_tok[:, t, :].rearrange("p (h d) -> p h d", h=16)
            b = PSA.get(2)
            for h in range(16):
                mm(ps[:, b * 512 + h * 64: b * 512 + (h + 1) * 64], M_bf[:, h // 8, h % 8, :], xdt[:, h, :], True, True,
                   [rM[h // 8], rxdt], [psr[b + h // 8]], inc=(h % 8 == 7))
            b2 = PSA.get(2)
            for g in range(2):
                mm(psb(b2 + g), bcT[:, 2 + g, ts], Sbf[:, g * 512:(g + 1) * 512], True, True, [rbc, rSbf], [psr[b2 + g]])
            b3 = PSA.get(2)
            for g in range(2):
                mm(psb(b3 + g), bm_tok[:, t, g, :], xdt2[:, g * 8:(g + 1) * 8, :].rearrange("p h d -> p (h d)"), True,
                   True, [rbmt, rxdt2], [psr[b3 + g]])
            yield
            dve("tensor_tensor", [psr[b2], psr[b2 + 1], rs], [rT[4]], out=T[4].rearrange("p (h d) -> p h d", h=16),
                in0=ps[:, b2 * 512:(b2 + 2) * 512].rearrange("p (h d) -> p h d", h=16),
                in1=vw(ea, [[1, 16], [0, 64]]), op=ALU.mult)
            dve("tensor_tensor", [psr[b], psr[b + 1], rT[4]], [rT[4]], out=T[4], in0=ps[:, b * 512:(b + 2) * 512],
                in1=T[4], op=ALU.add)
            dve("tensor_tensor", [rxs, rsp], [rT[5]], out=T[5].rearrange("p (h d) -> p h d", h=16), in0=xs3,
                in1=vw(dsk[l][:], [[1, 16], [0, 64]]), op=ALU.mult)
            dve("tensor_tensor", [rT[4], rT[5]], [rT[4]], out=T[4], in0=T[4], in1=T[5], op=ALU.add)
            dve("tensor_tensor", [rS[l], rs], [rS[l]], out=Sst[l][:].rearrange("p (h d) -> p h d", h=16),
                in0=Sst[l][:].rearrange("p (h d) -> p h d", h=16), in1=vw(cd, [[1, 16], [0, 64]]), op=ALU.mult)
            dve("tensor_tensor", [psr[b3], psr[b3 + 1], rS[l]], [rS[l]], out=Sst[l][:], in0=ps[:, b3 * 512:(b3 + 2) * 512],
                in1=Sst[l][:], op=ALU.add)
            act(Sbf[:], Sst[l][:], AF.Copy, [rS[l]], [rSbf])
            yield
            dve("tensor_tensor", [rT[4]] + rzs, [rT[5]], out=T[5], in0=T[4], in1=zs[:, t, :], op=ALU.mult)
            for g in range(2):
                S.op("act", lambda e, g=g: e.activation(out=obf[:, g * 512:(g + 1) * 512],
                                                        in_=T[5][:, g * 512:(g + 1) * 512], func=AF.Square,
                                                        accum_out=sm[:, 44 + g:45 + g]), [rT[5]], [robf, rsmg])
            act(sm[:, 46:48], sm[:, 44:46], AF.Sqrt, [rsmg, rcst], [rsmg], scale=1.0 / 512, bias=EPS6)
            dve("reciprocal", [rsmg], [rsmg], out=sm[:, 46:48], in_=sm[:, 46:48])
            for g in range(2):
                act(obf[:, g * 512:(g + 1) * 512], T[5][:, g * 512:(g + 1) * 512], AF.Copy, [rT[5], rsmg], [robf],
                    scale=sm[:, 46 + g:47 + g])
            yield
            to_mixT(obf, 8, 8, t, l, 40, robf)
            yield

        def drain(*gens):
            gens = [g for g in gens if g is not None]
            while gens:
                for g in list(gens):
                    try:
                        next(g)
                    except StopIteration:
                        gens.remove(g)

        drain(P1(0))
        for t in range(NT):
            drain(P1(t + 1) if t + 1 < NT else None, P2(t))

    def out_proj(l, gate):
        wo3 = prm["w_out"][l].rearrange("(k p) n -> p k n", p=128)
        for gi in range(4):
            wo, ro = load_w(wo3[:, :, gi * 512:(gi + 1) * 512], 16, 512)
            for mi in range(4):
                m = gi * 4 + mi
                b = PSA.get()
                for k in range(KC):
                    mm(psb(b), wo[:, k, mi * 128:(mi + 1) * 128], mixT[:, k, :], k == 0, k == KC - 1, [rmix[k], ro],
                       [psr[b]])
                dve("scalar_tensor_tensor", [psr[b], rxT[m], rmod], [rxT[m]], out=xT[:, m, :], in0=psb(b),
                    scalar=gate[:, m:m + 1], in1=xT[:, m, :], op0=ALU.mult, op1=ALU.add)

    def mlp(l, gate):
        w13 = prm["w_mlp1"][l].rearrange("(k p) n -> p k n", p=128)
        for f in range(16):
            w1, r1 = load_w(w13[:, :, f * 512:(f + 1) * 512], 16, 512)
            w2, r2 = load_w(prm["w_mlp2"][l][f * 512:(f + 1) * 512, :].rearrange("(c p) n -> p c n", p=128), 4, 2048)
            hb = (f % 2) * 4
            for cc in range(4):
                b = PSA.get()
                for k in range(KC):
                    mm(psb(b), w1[:, k, cc * 128:(cc + 1) * 128], hT[:, k, :], k == 0, k == KC - 1, [rhT[k], r1],
                       [psr[b]])
                act(tmps[cc % 2][:], psb(b), AF.Relu, [psr[b]], [rtmp[cc % 2]])
                dve("tensor_tensor", [rtmp[cc % 2]], [rmix[hb + cc]], out=mixT[:, hb + cc, :], in0=tmps[cc % 2][:],
                    in1=tmps[cc % 2][:], op=ALU.mult)
            for m in range(KC):
                b = PSA.get()
                for cc in range(4):
                    mm(psb(b), w2[:, cc, m * 128:(m + 1) * 128], mixT[:, hb + cc, :], cc == 0, cc == 3,
                       [rmix[hb + cc], r2], [psr[b]])
                dve("scalar_tensor_tensor", [psr[b], rxT[m], rmod], [rxT[m]], out=xT[:, m, :], in0=psb(b),
                    scalar=gate[:, m:m + 1], in1=xT[:, m, :], op0=ALU.mult, op1=ALU.add)

    for s in range(nseq):
        for l in range(2):
            dve("memset", [], [rS[l]], ap=Sst[l][:], constant=0.0)
            dve("memset", [], [rhist[l]], ap=hist[l][:], constant=0.0)
        for blk in range(nblk):
            tok0 = blk * TB
            for t in range(NT):
                dma("sp", xin, x[s, tok0 + t * 128: tok0 + (t + 1) * 128, :], [], rxin, "xin")
                for j in range(4):
                    b = PSA.get()
                    for i4 in range(4):
                        k = j * 4 + i4
                        tr(psb(b, 128, i4 * 128), xin[:, k * 128:(k + 1) * 128], ident_f, rxin + [rcf], [psr[b]],
                           inc=(i4 == 3))
                    act(xT[:, j * 4:(j + 1) * 4, t * 128:(t + 1) * 128], psb(b).rearrange("p (i f) -> p i f", i=4),
                        AF.Copy, [psr[b]], [rxT[j * 4 + i4] for i4 in range(4)])
            for l in range(nlayer):
                i = l * 2 + s
                mo = modT[:, i * 96:(i + 1) * 96]
                win3 = prm["w_in"][l].rearrange("(k p) n -> p k n", p=128)
                dma("sp", lnG[:], prm["gm_ln_g"][l].rearrange("h e -> (h e)").partition_broadcast(128), [], [rln], "ln")
                dma("sp", lnB[:], prm["gm_ln_b"][l].rearrange("h e -> (h e)").partition_broadcast(128), [], [rln], "ln")
                norm_to_hT(gsc[:, i * 32:i * 32 + 16], mo[:, 0:16])
                if do_a:
                    mixer_a(l, win3)
                else:
                    for k in range(0, 4):
                        dve("memset", [], [rmix[k]], ap=mixT[:, k, :], constant=0.0)
                if do_b:
                    mixer_b(l, win3, blk * NT)
                else:
                    for k in range(4, 8):
                        dve("memset", [], [rmix[k]], ap=mixT[:, k, :], constant=0.0)
                if do_c:
                    mixer_c(l, win3)
                else:
                    for k in range(8, 16):
                        dve("memset", [], [rmix[k]], ap=mixT[:, k, :], constant=0.0)
                out_proj(l, mo[:, 32:48])
                if do_mlp:
                    norm_to_hT(gsc[:, i * 32 + 16:i * 32 + 32], mo[:, 48:64])
                    mlp(l, mo[:, 80:96])
            dma("sp", fgb, prm["final_norm_g"].partition_broadcast(128), [], rfgb, "fgb")
            for t in range(NT):
                b = PSA.get(4)
                for k in range(KC):
                    tr(ps[:, b * 512 + k * 128: b * 512 + (k + 1) * 128], xT[:, k, t * 128:(t + 1) * 128], ident_f,
                       [rxT[k], rcf], [psr[b + k // 4]], inc=(k % 4 == 3))
                pr4 = [psr[b + q] for q in range(4)]
                S.op("act", lambda e, b=b: e.activation(out=xin, in_=ps[:, b * 512:(b + 4) * 512], func=AF.Square,
                                                        accum_out=sm[:, 36:37]), pr4, rxin + [rsmf])
                act(sm[:, 37:38], sm[:, 36:37], AF.Sqrt, [rsmf, rcst], [rsmf], scale=1.0 / D, bias=EPS6)
                dve("reciprocal", [rsmf], [rsmf], out=sm[:, 37:38], in_=sm[:, 37:38])
                dve("scalar_tensor_tensor", pr4 + [rsmf] + rfgb, rxin, out=xin, in0=ps[:, b * 512:(b + 4) * 512],
                    scalar=sm[:, 37:38], in1=fgb, op0=ALU.mult, op1=ALU.mult)
                dma("sp", out[s, tok0 + t * 128: tok0 + (t + 1) * 128, :], xin, rxin, [], "xin")

    with nc.allow_non_contiguous_dma(reason="small strided parameter loads"):
        S.emit(final_waits=["xin"])
    es.close()
    return nc


def make_consts():
    c = np.zeros((128, 640), np.float32)
    i = np.arange(128)
    c[:, C_ID:C_ID + 128] = np.eye(128, dtype=np.float32)
    u = (i[:, None] <= i[None, :]).astype(np.float32)
    c[:, C_U:C_U + 128] = u
    c[:, C_NU:C_NU + 128] = -u
    c[:, C_SL:C_SL + 128] = 1.0 - u
    c[:, C_ONE:C_ONE + 128] = 1.0
    return c


_NC_CACHE = {}


def kernel(**inputs):
    n = 8
    key = "full"
    if key not in _NC_CACHE:
        _NC_CACHE[key] = build()
    nc = _NC_CACHE[key]
    x = np.ascontiguousarray(inputs["x"], dtype=np.float32)
    c = np.ascontiguousarray(inputs["c"], dtype=np.float32)
    consts = make_consts()
    params = {k: np.ascontiguousarray(inputs[k], dtype=np.float32) for k in PARAM_NAMES}
    in_maps = []
    for i in range(n):
        m = {"x": x[2 * i:2 * i + 2], "c": c[2 * i:2 * i + 2], "consts": consts}
        m.update(params)
        in_maps.append(m)
    res = run_bass_kernel_spmd(nc, in_maps, core_ids=list(range(n)))
    return np.concatenate([r["out"] for r in res.results], axis=0)
```

```python
import contextlib
import numpy as np
import concourse.bass as bass
import concourse.mybir as mybir
from concourse.bass_utils import run_bass_kernel_spmd

F32 = mybir.dt.float32
BF16 = mybir.dt.bfloat16
AF = mybir.ActivationFunctionType
ALU = mybir.AluOpType

ENGS = ("pe", "act", "dve", "pool", "sp")


class Reg:
    __slots__ = ("name", "w", "rd")

    def __init__(self, name):
        self.name = name
        self.w = None
        self.rd = {}


class Rec:
    __slots__ = ("eng", "fn", "inc", "deps", "dma_sem", "cnt", "idx", "dma_waits")

    def __init__(self, eng, fn, inc, dma_sem):
        self.eng = eng
        self.fn = fn
        self.inc = inc
        self.deps = []
        self.dma_waits = {}
        self.dma_sem = dma_sem
        self.cnt = None
        self.idx = None


class Sched:
    def __init__(self, nc):
        self.nc = nc
        self.recs = {e: [] for e in ENGS}
        self.dma_cnt = {}
        self.n = 0

    def _dep(self, w, d):
        if d is None or d is w:
            return
        if d.dma_sem is not None:
            k = d.dma_sem
            w.dma_waits[k] = max(w.dma_waits.get(k, 0), self.dma_cnt[k])
            return
        if d.eng == "pe" and w.eng == "pe" and w.dma_sem is None:
            return
        if not d.inc:
            lst = self.recs[d.eng]
            tgt = None
            for j in range(d.idx + 1, len(lst)):
                if lst[j].inc and lst[j].dma_sem is None:
                    tgt = lst[j]
                    break
            if tgt is None:
                d.inc = True
                tgt = d
            d = tgt
        w.deps.append(d)

    def op(self, eng, fn, reads=(), writes=(), inc=True, dma=None):
        r = Rec(eng, fn, inc if dma is None else False, dma)
        r.idx = len(self.recs[eng])
        for x in reads:
            self._dep(r, x.w)
        for x in writes:
            self._dep(r, x.w)
            for rr in x.rd.values():
                self._dep(r, rr)
        if dma is not None:
            self.dma_cnt[dma] = self.dma_cnt.get(dma, 0) + 16
            r.cnt = self.dma_cnt[dma]
        self.recs[eng].append(r)
        for x in writes:
            x.w = r
            x.rd = {}
        for x in reads:
            x.rd[eng if dma is None else (dma, r.cnt)] = r
        self.n += 1
        return r

    def emit(self, final_waits=()):
        nc = self.nc
        with contextlib.ExitStack() as es:
            sems = {}
            for e in ENGS:
                sems[e] = es.enter_context(nc.semaphore("s_" + e))
            for k in self.dma_cnt:
                sems[("dma", k)] = es.enter_context(nc.semaphore("d_" + k))
            for e in ENGS:
                c = 0
                for r in self.recs[e]:
                    if r.dma_sem is None and r.inc:
                        c += 1
                        r.cnt = c
            block = es.enter_context(nc.Block())

            def run(e, eng):
                waited = {}
                for r in self.recs[e]:
                    need = {}
                    for d in r.deps:
                        need[d.eng] = max(need.get(d.eng, 0), d.cnt)
                    for k, v in r.dma_waits.items():
                        need[("dma", k)] = max(need.get(("dma", k), 0), v)
                    for k, v in need.items():
                        if waited.get(k, 0) < v:
                            eng.wait_ge(sems[k], v)
                            waited[k] = v
                    ins = r.fn(eng)
                    if r.dma_sem is not None:
                        ins.then_inc(sems[("dma", r.dma_sem)], 16)
                    elif r.inc:
                        ins.then_inc(sems[e], 1)
                if e == "sp":
                    for k in final_waits:
                        eng.wait_ge(sems[("dma", k)], self.dma_cnt[k])

            @block.tensor
            def _(eng):
                run("pe", eng)

            @block.scalar
            def _(eng):
                run("act", eng)

            @block.vector
            def _(eng):
                run("dve", eng)

            @block.gpsimd
            def _(eng):
                run("pool", eng)

            @block.sync
            def _(eng):
                run("sp", eng)


def vw(ap, dims):
    return bass.AP(ap.tensor, ap.offset, [list(ap.ap[0])] + [list(d) for d in dims])


D = 2048
KC = 16
TB = 512
NT = 4
SEQ = 2048
NW = 2
O_U, O_V, O_Q, O_K, O_VV, O_Z, O_X, O_DT = 0, 512, 1024, 1536, 1664, 1792, 2816, 4352
C_ID, C_U, C_NU, C_SL, C_ONE = 0, 128, 256, 384, 512

PARAM_NAMES = ["ada_w", "ada_b", "norm1_g", "w_in", "gm_ln_g", "gm_ln_b", "gm_ws", "gm_bs",
               "gm_norm_g", "attn_sinks", "attn_norm_g", "conv_w", "conv_b", "dt_bias", "a_log",
               "d_skip", "ssm_norm_g", "w_out", "norm2_g", "w_mlp1", "w_mlp2", "final_norm_g"]
PARAM_SHAPES = {
    "ada_w": [2, 2048, 12288], "ada_b": [2, 12288], "norm1_g": [2, 2048], "w_in": [2, 2048, 4368],
    "gm_ln_g": [2, 4, 128], "gm_ln_b": [2, 4, 128], "gm_ws": [2, 4, 128, 128], "gm_bs": [2, 4, 128],
    "gm_norm_g": [2, 512], "attn_sinks": [2, 8], "attn_norm_g": [2, 512], "conv_w": [2, 4, 1536],
    "conv_b": [2, 1536], "dt_bias": [2, 16], "a_log": [2, 16], "d_skip": [2, 16],
    "ssm_norm_g": [2, 1024], "w_out": [2, 2048, 2048], "norm2_g": [2, 2048],
    "w_mlp1": [2, 2048, 8192], "w_mlp2": [2, 8192, 2048], "final_norm_g": [2048],
}


def slim_shapes(nlayer, do_mlp):
    sh = {k: list(v) for k, v in PARAM_SHAPES.items()}
    for k in ("ada_w", "w_in", "w_out", "w_mlp1", "w_mlp2"):
        sh[k][0] = nlayer
    if not do_mlp:
        sh["w_mlp1"] = [1, 128, 128]
        sh["w_mlp2"] = [1, 128, 128]
    return sh


def build(nseq=2, nblk=4, nlayer=2, do_a=True, do_b=True, do_c=True, do_mlp=True, slim=False):
    nc = bass.Bass("TRN2", target_bir_lowering=False)
    din = lambda n, s: nc.dram_tensor(n, s, F32, kind="ExternalInput").ap()
    x = din("x", [2, SEQ, D])
    c_in = din("c", [2, D])
    shp = slim_shapes(nlayer, do_mlp) if slim else PARAM_SHAPES
    prm = {n: din(n, shp[n]) for n in PARAM_NAMES}
    consts = din("consts", [128, 640])
    out = nc.dram_tensor("out", [2, SEQ, D], F32, kind="ExternalOutput").ap()
    mod_d = nc.dram_tensor("mod_d", [2, 2, 12288], F32).ap()
    NTILE = 48
    wscr = nc.dram_tensor("wscr", [2, NTILE, 128, 8192], BF16).ap()

    es = contextlib.ExitStack()
    S = Sched(nc)

    def sb(name, shape, dt=F32):
        return es.enter_context(nc.sbuf_tensor(name, shape, dt))

    xT = sb("xT", [128, KC, TB]); rxT = [Reg(f"xT{k}") for k in range(KC)]
    hT = sb("hT", [128, KC, TB], BF16); rhT = [Reg(f"hT{k}") for k in range(KC)]
    mixT = sb("mixT", [128, KC, TB], BF16); rmix = [Reg(f"mix{k}") for k in range(KC)]
    wbuf = [sb(f"wbuf{i}", [128, 8192], BF16) for i in range(NW)]
    rw = [Reg(f"w{i}") for i in range(NW)]
    TT = sb("TT", [128, 6 * 1024]); rT = [Reg(f"T{i}") for i in range(6)]
    T = [TT[:, i * 1024:(i + 1) * 1024] for i in range(6)]
    xin = TT[:, 2048:4096]; rxin = [rT[2], rT[3]]
    fgb = TT[:, 0:2048]; rfgb = [rT[0], rT[1]]
    cf = sb("cf", [128, 640]); rcf = Reg("cf")
    cb = sb("cb", [128, 640], BF16); rcb = Reg("cb")
    cst = sb("cst", [128, 4]); rcst = Reg("cst")
    vrow = sb("vrow", [128, 128]); rvrow = Reg("vrow")
    vT = [sb(f"vT{l}", [128, 112]) for l in range(2)]; rvT = [Reg(f"vT{l}") for l in range(2)]
    vTf = sb("vTf", [128, 16]); rvTf = Reg("vTf")
    modT = sb("modT", [128, 4 * 96]); rmod = Reg("modT")
    gsc = sb("gsc", [128, 4 * 32]); rgsc = Reg("gsc")
    lnG = sb("lnG", [128, 512]); lnB = sb("lnB", [128, 512]); rln = Reg("ln")
    WsT = [sb(f"WsT{l}", [128, 4, 128], BF16) for l in range(2)]; rws = [Reg(f"ws{l}") for l in range(2)]
    esink = [sb(f"esink{l}", [128, 8]) for l in range(2)]
    dtb = [sb(f"dtb{l}", [128, 16]) for l in range(2)]
    aneg = [sb(f"aneg{l}", [128, 16]) for l in range(2)]
    dsk = [sb(f"dsk{l}", [128, 16]) for l in range(2)]
    rsp = Reg("smallparams")
    rstd_b = sb("rstd_b", [128, TB]); rrstd = Reg("rstd")
    tmps = [sb(f"tmp{i}", [128, TB]) for i in range(2)]; rtmp = [Reg(f"tmp{i}") for i in range(2)]
    AB = sb("AB", [128, 4096], BF16)
    qT = AB[:, 0:2048].rearrange("p (j t) -> p j t", j=4); rqT = Reg("qT")
    e_cur = AB[:, 2048:3072]; recur = Reg("ecur")
    e_prev = AB[:, 3072:4096]; reprev = Reg("eprev")
    zs = AB[:, :].rearrange("p (t f) -> p t f", t=4); rzs = [rqT, recur, reprev]
    kT = [sb(f"kT{l}", [128, 2, 640], BF16) for l in range(2)]; rkT = [Reg(f"kT{l}") for l in range(2)]
    vaug = [sb(f"vaug{l}", [128, 5, 2, 72], BF16) for l in range(2)]
    rva = [[Reg(f"va{l}_{i}") for i in range(5)] for l in range(2)]
    xs_tok = sb("xs_tok", [128, 4, 1024], BF16); rxs = Reg("xs_tok")
    bcT = sb("bcT", [128, 4, TB], BF16); rbc = Reg("bcT")
    bm_tok = sb("bm_tok", [128, 4, 2, 128], BF16); rbmt = Reg("bm_tok")
    xraw = sb("xraw", [128, 2, 515]); rxraw = [Reg("xraw0"), Reg("xraw1")]
    hist = [sb(f"hist{l}", [128, 12, 3]) for l in range(2)]; rhist = [Reg(f"hist{l}") for l in range(2)]
    Sst = [sb(f"Sst{l}", [128, 1024]) for l in range(2)]; rS = [Reg(f"S{l}") for l in range(2)]
    Sbf = sb("Sbf", [128, 1024], BF16); rSbf = Reg("Sbf")
    M_bf2 = [sb(f"M_bf{i}", [128, 2, 8, 128], BF16) for i in range(2)]; rM2 = [[Reg(f"M{i}{g}") for g in range(2)] for i in range(2)]
    xdtb = [sb(f"xdt{i}", [128, 16, 64], BF16) for i in range(2)]; rxdtb = [Reg(f"xdt{i}") for i in range(2)]
    xdt2b = [sb(f"xdt2{i}", [128, 16, 64], BF16) for i in range(2)]; rxdt2b = [Reg(f"xdt2{i}") for i in range(2)]
    vn = sb("vn", [128, 512], BF16); rvn = Reg("vn")
    obf = sb("obf", [128, 1024], BF16); robf = Reg("obf")
    sm = sb("sm", [128, 256]); rsm = Reg("sm")
    rsmc = [Reg("smc0"), Reg("smc1")]; rsmg = Reg("smg"); rsmf = Reg("smf")
    dtall = sb("dtall", [128, 4, 16]); daall = sb("daall", [128, 4, 16]); rdt = Reg("dt")
    cbm = sb("cbm", [128, 2, 128]); rcbm = Reg("cbm")
    cTf = sb("cTf", [128, KC, 2]); cTb = sb("cTb", [128, KC, 2], BF16); rcT = Reg("cT")
    abt = TT[0:2, 4096:4608]; rabt = rT[4]
    mrow = TT[0:2, 5120:5632]; rmrow = rT[5]

    ps = es.enter_context(nc.psum_tensor("ps", [128, 4096], F32))
    psr = [Reg(f"ps{b}") for b in range(8)]

    class PSA:
        i = 0

        @staticmethod
        def get(n=1):
            while PSA.i % n:
                PSA.i += 1
            b = PSA.i % 8
            PSA.i += n
            return b

    def psb(b, n=512, off=0):
        return ps[:, b * 512 + off: b * 512 + off + n]

    def mm(o, lhsT, rhs, start, stop, rd, wr, inc=None):
        S.op("pe", lambda e: e.matmul(o, lhsT, rhs, start=start, stop=stop), rd, wr,
             inc=(stop if inc is None else inc))

    def tr(o, i, ident, rd, wr, inc=True):
        S.op("pe", lambda e: e.transpose(o, i, ident), rd, wr, inc=inc)

    def act(o, i, func, rd, wr, **kw):
        S.op("act", lambda e: e.activation(out=o, in_=i, func=func, **kw), rd, wr)

    def dve(meth, rd, wr, **kw):
        S.op("dve", lambda e: getattr(e, meth)(**kw), rd, wr)

    def dma(q, o, i, rd, wr, key):
        S.op(q, lambda e: e.dma_start(out=o, in_=i), rd, wr, dma=key)

    class WL:
        i = 0

    def wslot():
        i = WL.i % NW
        WL.i += 1
        return i

    class WS:
        first = True
        l = 0
        ti = 0
    rwscr = [Reg("wscr0"), Reg("wscr1")]

    def load_w(src3, k, n, cache=True):
        i = wslot()
        v = wbuf[i][:, 0:k * n].rearrange("p (k n) -> p k n", k=k)
        if not cache:
            dma("pool", v, src3, [], [rw[i]], f"w{i}")
            return v, rw[i]
        l_, ti = WS.l, WS.ti
        WS.ti += 1
        flat = wbuf[i][:, 0:k * n]
        if WS.first:
            dma("pool", v, src3, [], [rw[i]], f"w{i}")
            dma("sp", wscr[l_, ti, :, 0:k * n], flat, [rw[i]], [rwscr[l_]], f"ws{l_}")
        else:
            dma("sp", flat, wscr[l_, ti, :, 0:k * n], [rwscr[l_]], [rw[i]], f"wh{i}")
        return v, rw[i]

    ident_f = cf[:, C_ID:C_ID + 128]
    U_f = cf[:, C_U:C_U + 128]
    negU_f = cf[:, C_NU:C_NU + 128]
    ones_f = cf[:, C_ONE:C_ONE + 128]
    ident_b = cb[:, C_ID:C_ID + 128]
    U_b = cb[:, C_U:C_U + 128]
    SL_b = cb[:, C_SL:C_SL + 128]
    ones_b = cb[:, C_ONE:C_ONE + 128]
    EPS6 = cst[:, 0:1]
    EPS5 = cst[:, 1:2]
    ONE = cst[:, 2:3]

    dma("sp", cf[:], consts, [], [rcf], "const")
    dma("pool", cb[:], consts, [], [rcb], "constb")
    dve("memset", [], [rcst], ap=cst[:, 0:1], constant=1e-6)
    dve("memset", [], [rcst], ap=cst[:, 1:2], constant=1e-5)
    dve("memset", [], [rcst], ap=cst[:, 2:3], constant=1.0)
    dve("memset", [], [rcst], ap=cst[:, 3:4], constant=0.0)
    for l in range(2):
        for i in range(5):
            dve("memset", [], [rva[l][i]], ap=vaug[l][:, i, :, 64:72], constant=0.0)
            dve("memset", [], [rva[l][i]], ap=vaug[l][:, i, :, 64:65], constant=1.0)

    def vec_rows(l):
        rows = []
        rows.append((0, 16, prm["norm1_g"][l].rearrange("(k p) -> k p", p=128)))
        rows.append((16, 16, prm["norm2_g"][l].rearrange("(k p) -> k p", p=128)))
        rows.append((32, 4, prm["gm_norm_g"][l].rearrange("(k p) -> k p", p=128)))
        rows.append((36, 4, prm["attn_norm_g"][l].rearrange("(k p) -> k p", p=128)))
        rows.append((40, 8, prm["ssm_norm_g"][l].rearrange("(k p) -> k p", p=128)))
        rows.append((48, 12, prm["conv_b"][l].rearrange("(k p) -> k p", p=128)))
        rows.append((60, 48, prm["conv_w"][l].rearrange("t (k p) -> (t k) p", p=128)))
        rows.append((108, 4, prm["gm_bs"][l]))
        return rows

    for l in range(2):
        for (r0, n, src) in vec_rows(l):
            dma("sp", vrow[r0:r0 + n, :], src, [], [rvrow], "vrow")
        b = PSA.get()
        tr(psb(b, 112), vrow[0:112, :], ident_f[0:112, 0:112], [rvrow, rcf], [psr[b]])
        act(vT[l][:], psb(b, 112), AF.Copy, [psr[b]], [rvT[l]])
    dma("sp", vrow[0:16, :], prm["final_norm_g"].rearrange("(k p) -> k p", p=128), [], [rvrow], "vrow")
    b = PSA.get()
    tr(psb(b, 16), vrow[0:16, :], ident_f[0:16, 0:16], [rvrow, rcf], [psr[b]])
    act(vTf[:], psb(b, 16), AF.Copy, [psr[b]], [rvTf])

    for l in range(2):
        dma("sp", esink[l][:], prm["attn_sinks"][l].partition_broadcast(128), [], [rsp], "sp")
        dma("sp", dtb[l][:], prm["dt_bias"][l].partition_broadcast(128), [], [rsp], "sp")
        dma("sp", aneg[l][:], prm["a_log"][l].partition_broadcast(128), [], [rsp], "sp")
        dma("sp", dsk[l][:], prm["d_skip"][l].partition_broadcast(128), [], [rsp], "sp")
    for l in range(2):
        act(esink[l][:], esink[l][:], AF.Exp, [rsp], [rsp])
        act(aneg[l][:], aneg[l][:], AF.Exp, [rsp], [rsp])
        dve("tensor_scalar", [rsp], [rsp], out=aneg[l][:], in0=aneg[l][:], scalar1=-1.0, scalar2=None,
            op0=ALU.mult)
    for l in range(2):
        wsf = T[0].rearrange("p (h s) -> p h s", h=8)[:, 0:4, :]
        dma("sp", wsf, prm["gm_ws"][l].rearrange("h t s -> t h s"), [], [rT[0]], "wsf")
        b = PSA.get()
        for h in range(4):
            tr(psb(b, 128, h * 128), wsf[:, h, :], ident_f, [rT[0], rcf], [psr[b]], inc=(h == 3))
        dve("tensor_tensor", [psr[b], rcf], [rws[l]], out=WsT[l][:],
            in0=psb(b).rearrange("p (h t) -> p h t", h=4), in1=vw(U_f, [[0, 4], [1, 128]]), op=ALU.mult)

    for s_ in range(2):
        dma("sp", cTf[:, :, s_], c_in[s_].rearrange("(k p) -> p k", p=128), [], [rcT], "cT")
    act(cTb[:], cTf[:], AF.Silu, [rcT], [rcT])
    for l in range(nlayer):
        aw3 = prm["ada_w"][l].rearrange("(k p) n -> p k n", p=128)
        for n in range(24):
            w, rwi = load_w(aw3[:, :, n * 512:(n + 1) * 512], 16, 512, cache=False)
            dma("sp", abt, prm["ada_b"][l][n * 512:(n + 1) * 512].partition_broadcast(2), [], [rabt], "abt")
            b = PSA.get()
            for k in range(KC):
                mm(ps[0:2, b * 512:(b + 1) * 512], cTb[:, k, :], w[:, k, :], k == 0, k == KC - 1,
                   [rcT, rwi], [psr[b]])
            dve("tensor_tensor", [psr[b], rabt], [rmrow], out=mrow, in0=ps[0:2, b * 512:(b + 1) * 512],
                in1=abt, op=ALU.add)
            dma("sp", mod_d[l, :, n * 512:(n + 1) * 512], mrow, [rmrow], [], "mrow")
    rmodd = Reg("mod_d")
    for l in range(nlayer):
        for s in range(2):
            i = l * 2 + s
            S.op("sp", lambda e, l=l, s=s: e.dma_start(out=vrow[0:96, :],
                                                       in_=mod_d[l, s].rearrange("(r p) -> r p", p=128)),
                 [], [rmrow, rvrow], dma="mrow")
            b = PSA.get()
            tr(psb(b, 96), vrow[0:96, :], ident_f[0:96, 0:96], [rvrow, rcf], [psr[b]])
            act(modT[:, i * 96:(i + 1) * 96], psb(b, 96), AF.Copy, [psr[b]], [rmod])
            dve("scalar_tensor_tensor", [rmod, rvT[l]], [rgsc], out=gsc[:, i * 32:i * 32 + 16],
                in0=modT[:, i * 96 + 16:i * 96 + 32], scalar=1.0, in1=vT[l][:, 0:16], op0=ALU.add, op1=ALU.mult)
            dve("scalar_tensor_tensor", [rmod, rvT[l]], [rgsc], out=gsc[:, i * 32 + 16:i * 32 + 32],
                in0=modT[:, i * 96 + 64:i * 96 + 80], scalar=1.0, in1=vT[l][:, 16:32], op0=ALU.add, op1=ALU.mult)

    def norm_to_hT(gs, sh):
        for k in range(KC):
            act(hT[:, k, :], xT[:, k, :], AF.Square, [rxT[k]], [rhT[k]])
        b = PSA.get()
        for k in range(KC):
            mm(psb(b), ones_b, hT[:, k, :], k == 0, k == KC - 1, [rhT[k], rcb], [psr[b]])
        act(rstd_b[:], psb(b), AF.Sqrt, [psr[b], rcst], [rrstd], scale=1.0 / D, bias=EPS6)
        dve("reciprocal", [rrstd], [rrstd], out=rstd_b[:], in_=rstd_b[:])
        for k in range(KC):
            t_ = tmps[k % 2]
            dve("scalar_tensor_tensor", [rxT[k], rrstd, rgsc], [rtmp[k % 2]], out=t_[:], in0=xT[:, k, :],
                scalar=gs[:, k:k + 1], in1=rstd_b[:], op0=ALU.mult, op1=ALU.mult)
            act(hT[:, k, :], t_[:], AF.Identity, [rtmp[k % 2], rmod], [rhT[k]], bias=sh[:, k:k + 1], scale=1.0)

    def rms_scale(src, n, ss_col):
        act(T[3][:, 0:n], src, AF.Square, [rsm], [rT[3], rsm], accum_out=sm[:, ss_col:ss_col + 1])
        act(sm[:, ss_col + 1:ss_col + 2], sm[:, ss_col:ss_col + 1], AF.Sqrt, [rsm, rcst], [rsm],
            scale=1.0 / n, bias=EPS6)
        dve("reciprocal", [rsm], [rsm], out=sm[:, ss_col + 1:ss_col + 2], in_=sm[:, ss_col + 1:ss_col + 2])

    def to_mixT(src_bf, nchunk, k0, t, l, gcol, rsrc):
        b = PSA.get()
        pT = psb(b).bitcast(BF16)
        for j in range(nchunk):
            tr(pT[:, j * 128:(j + 1) * 128], src_bf[:, j * 128:(j + 1) * 128], ident_b, [rsrc, rcb], [psr[b]],
               inc=(j == nchunk - 1))
        for j in range(nchunk):
            act(mixT[:, k0 + j, t * 128:(t + 1) * 128], pT[:, j * 128:(j + 1) * 128], AF.Copy,
                [psr[b], rvT[l]], [rmix[k0 + j]], scale=vT[l][:, gcol + j:gcol + j + 1])

    def mixer_a(l, win3):
        wu, ru = load_w(win3[:, :, O_U:O_U + 512], 16, 512)
        wv, rv = load_w(win3[:, :, O_V:O_V + 512], 16, 512)
        def proj(t):
            ts = slice(t * 128, (t + 1) * 128)
            b = PSA.get(2)
            for k in range(KC):
                mm(psb(b), hT[:, k, ts], wu[:, k, :], k == 0, k == KC - 1, [rhT[k], ru], [psr[b]])
            for k in range(KC):
                mm(psb(b + 1), hT[:, k, ts], wv[:, k, :], k == 0, k == KC - 1, [rhT[k], rv], [psr[b + 1]])
            return b

        bnext = proj(0)
        for t in range(NT):
            b = bnext
            if t + 1 < NT:
                bnext = proj(t + 1)
            act(T[0], ps[:, b * 512:(b + 2) * 512], AF.Gelu_apprx_tanh, [psr[b], psr[b + 1]], [rT[0]])
            ug = T[0][:, 0:512]
            vg = T[0][:, 512:1024]
            st6 = sm[:, 0:24].rearrange("p (h s) -> p h s", h=4)
            mv = sm[:, 24:32].rearrange("p (h s) -> p h s", h=4)
            for h in range(4):
                dve("bn_stats", [rT[0]], [rsm], out=st6[:, h, :], in_=vg[:, h * 128:(h + 1) * 128])
            for h in range(4):
                dve("bn_aggr", [rsm], [rsm], out=mv[:, h, :], in_=st6[:, h, :])
            act(sm[:, 32:36], mv[:, :, 1], AF.Sqrt, [rsm, rcst], [rsm], bias=EPS5, scale=1.0)
            dve("reciprocal", [rsm], [rsm], out=sm[:, 32:36], in_=sm[:, 32:36])
            for h in range(4):
                dve("tensor_scalar", [rT[0], rsm], [rT[1]], out=T[1][:, h * 128:(h + 1) * 128],
                    in0=vg[:, h * 128:(h + 1) * 128], scalar1=mv[:, h, 0:1], scalar2=sm[:, 32 + h:33 + h],
                    op0=ALU.subtract, op1=ALU.mult)
            dve("tensor_tensor", [rT[1], rln], [rT[1]], out=T[1][:, 0:512], in0=T[1][:, 0:512], in1=lnG[:],
                op=ALU.mult)
            dve("tensor_tensor", [rT[1], rln], [rvn], out=vn[:], in0=T[1][:, 0:512], in1=lnB[:], op=ALU.add)
            b2 = PSA.get()
            for h in range(4):
                mm(psb(b2, 128, h * 128), WsT[l][:, h, :], vn[:, h * 128:(h + 1) * 128], True, True,
                   [rws[l], rvn], [psr[b2]], inc=(h == 3))
            for h in range(4):
                dve("scalar_tensor_tensor", [psr[b2], rT[0], rvT[l]], [rT[2]], out=T[2][:, h * 128:(h + 1) * 128],
                    in0=psb(b2, 128, h * 128), scalar=vT[l][:, 108 + h:109 + h], in1=ug[:, h * 128:(h + 1) * 128],
                    op0=ALU.add, op1=ALU.mult)
            S.op("act", lambda e: e.activation(out=T[3][:, 0:512], in_=T[2][:, 0:512], func=AF.Square,
                                               accum_out=sm[:, 40:41]), [rT[2]], [rT[3], rsm])
            act(sm[:, 41:42], sm[:, 40:41], AF.Sqrt, [rsm, rcst], [rsm], scale=1.0 / 512, bias=EPS6)
            dve("reciprocal", [rsm], [rsm], out=sm[:, 41:42], in_=sm[:, 41:42])
            act(obf[:, 0:512], T[2][:, 0:512], AF.Copy, [rT[2], rsm], [robf], scale=sm[:, 41:42])
            to_mixT(obf, 4, 0, t, l, 32, robf)

    def mixer_b(l, win3, g0):
        wq, rq = load_w(win3[:, :, O_Q:O_Q + 512], 16, 512)
        wkv, rk = load_w(win3[:, :, O_K:O_K + 256], 16, 256)
        wvv = wkv[:, :, 128:256]
        for h in range(8):
            b = PSA.get()
            for k in range(KC):
                mm(ps[0:64, b * 512:(b + 1) * 512], wq[:, k, h * 64:(h + 1) * 64], hT[:, k, :], k == 0, k == KC - 1,
                   [rhT[k], rq], [psr[b]])
            act(mixT[0:64, 8 + h, :], ps[0:64, b * 512:(b + 1) * 512], AF.Copy, [psr[b]], [rmix[8 + h]])
        for a in range(2):
            b = PSA.get()
            for k in range(KC):
                mm(ps[0:64, b * 512:(b + 1) * 512], wkv[:, k, a * 64:(a + 1) * 64], hT[:, k, :], k == 0, k == KC - 1,
                   [rhT[k], rk], [psr[b]])
            act(kT[l][0:64, a, 128:640], ps[0:64, b * 512:(b + 1) * 512], AF.Copy, [psr[b]], [rkT[l]])
        for t in range(NT):
            ts = slice(t * 128, (t + 1) * 128)
            slot = (g0 + t) % 5
            b = PSA.get()
            for k in range(KC):
                mm(psb(b, 128), hT[:, k, ts], wvv[:, k, :], k == 0, k == KC - 1, [rhT[k], rk], [psr[b]])
            act(vaug[l][:, slot, :, 0:64], psb(b, 128).rearrange("p (a d) -> p a d", a=2), AF.Copy,
                [psr[b]], [rva[l][slot]])
        for t in range(NT):
            ts = slice(t * 128, (t + 1) * 128)
            g = g0 + t
            slot = g % 5
            pslot = (g - 1) % 5
            blocks = [(e_cur, recur, 128 + t * 128, U_b, slot)]
            if g > 0:
                blocks.append((e_prev, reprev, t * 128, SL_b, pslot))
            for (ebuf, reb, kc0, mask, _) in blocks:
                b = PSA.get(2)
                for h in range(8):
                    a = h // 4
                    mm(ps[:, b * 512 + h * 128: b * 512 + (h + 1) * 128], kT[l][0:64, a, kc0:kc0 + 128],
                       mixT[0:64, 8 + h, ts], True, True, [rkT[l], rmix[8 + h]], [psr[b + h // 4]], inc=(h % 4 == 3))
                act(ebuf, ps[:, b * 512:(b + 2) * 512], AF.Exp, [psr[b], psr[b + 1]], [reb], scale=0.125)
                dve("tensor_tensor", [reb, rcb], [reb], out=ebuf.rearrange("p (h q) -> p h q", h=8),
                    in0=ebuf.rearrange("p (h q) -> p h q", h=8), in1=vw(mask, [[0, 8], [1, 128]]), op=ALU.mult)
            bo = PSA.get(2)
            for h in range(8):
                a = h // 4
                dst = ps[:, (bo + h // 4) * 512 + (h % 4) * 128:(bo + h // 4) * 512 + (h % 4) * 128 + 66]
                if g > 0:
                    mm(dst, e_prev[:, h * 128:(h + 1) * 128], vaug[l][:, pslot, a, 0:66], True, False,
                       [reprev, rva[l][pslot]], [psr[bo + h // 4]], inc=False)
                mm(dst, e_cur[:, h * 128:(h + 1) * 128], vaug[l][:, slot, a, 0:66], g == 0, True,
                   [recur, rva[l][slot]], [psr[bo + h // 4]], inc=(h % 4 == 3))
            for i2 in range(2):
                pv = psb(bo + i2, 512).rearrange("p (h c) -> p h c", c=128)
                dve("tensor_tensor", [psr[bo + i2], rsp], [rsm], out=sm[:, 48 + i2 * 4:52 + i2 * 4],
                    in0=pv[:, :, 64], in1=esink[l][:, i2 * 4:(i2 + 1) * 4], op=ALU.add)
            dve("reciprocal", [rsm], [rsm], out=sm[:, 48:56], in_=sm[:, 48:56])
            for i2 in range(2):
                pv = psb(bo + i2, 512).rearrange("p (h c) -> p h c", c=128)
                dve("tensor_tensor", [psr[bo + i2], rsm], [rT[2]],
                    out=T[2][:, i2 * 256:(i2 + 1) * 256].rearrange("p (h d) -> p h d", h=4),
                    in0=pv[:, :, 0:64], in1=vw(sm[:, 48 + i2 * 4:52 + i2 * 4], [[1, 4], [0, 64]]), op=ALU.mult)
            S.op("act", lambda e: e.activation(out=T[3][:, 0:512], in_=T[2][:, 0:512], func=AF.Square,
                                               accum_out=sm[:, 60:61]), [rT[2]], [rT[3], rsm])
            act(sm[:, 61:62], sm[:, 60:61], AF.Sqrt, [rsm, rcst], [rsm], scale=1.0 / 512, bias=EPS6)
            dve("reciprocal", [rsm], [rsm], out=sm[:, 61:62], in_=sm[:, 61:62])
            act(obf[:, 0:512], T[2][:, 0:512], AF.Copy, [rT[2], rsm], [robf], scale=sm[:, 61:62])
            to_mixT(obf, 4, 4, t, l, 36, robf)
        act(kT[l][0:64, :, 0:128], kT[l][0:64, :, 512:640], AF.Copy, [rkT[l]], [rkT[l]])

    def mixer_c(l, win3):
        wz0, rz0 = load_w(win3[:, :, O_Z:O_Z + 512], 16, 512)
        wz1, rz1 = load_w(win3[:, :, O_Z + 512:O_Z + 1024], 16, 512)
        for t in range(NT):
            ts = slice(t * 128, (t + 1) * 128)
            b = PSA.get(2)
            for k in range(KC):
                mm(psb(b), hT[:, k, ts], wz0[:, k, :], k == 0, k == KC - 1, [rhT[k], rz0], [psr[b]])
            for k in range(KC):
                mm(psb(b + 1), hT[:, k, ts], wz1[:, k, :], k == 0, k == KC - 1, [rhT[k], rz1], [psr[b + 1]])
            act(zs[:, t, :], ps[:, b * 512:(b + 2) * 512], AF.Silu, [psr[b], psr[b + 1]], rzs)
        wx = [None, None, None]

        def xproj(c):
            i3, cc = c // 4, c % 4
            if cc == 0:
                wx[i3] = load_w(win3[:, :, O_X + i3 * 512:O_X + (i3 + 1) * 512], 16, 512)
            w_, rw_ = wx[i3]
            b = PSA.get()
            for k in range(KC):
                mm(psb(b), w_[:, k, cc * 128:(cc + 1) * 128], hT[:, k, :], k == 0, k == KC - 1,
                   [rhT[k], rw_], [psr[b]])
            return b

        bnx = xproj(0)
        for c in range(12):
            b = bnx
            xr = xraw[:, c % 2, :]
            rxr = rxraw[c % 2]
            dve("tensor_copy", [rhist[l]], [rxr], out=xr[:, 0:3], in_=hist[l][:, c, :])
            act(xr[:, 3:515], psb(b), AF.Copy, [psr[b]], [rxr])
            if c + 1 < 12:
                bnx = xproj(c + 1)
            dve("tensor_copy", [rxr], [rhist[l]], out=hist[l][:, c, :], in_=xr[:, 512:515])
            ta = tmps[0]
            dve("tensor_scalar", [rxr, rvT[l]], [rtmp[0]], out=ta[:], in0=xr[:, 0:512],
                scalar1=vT[l][:, 60 + c:61 + c], scalar2=vT[l][:, 48 + c:49 + c], op0=ALU.mult, op1=ALU.add)
            for tap in range(1, 4):
                dve("scalar_tensor_tensor", [rxr, rvT[l], rtmp[0]], [rtmp[0]], out=ta[:], in0=xr[:, tap:tap + 512],
                    scalar=vT[l][:, 60 + tap * 12 + c:61 + tap * 12 + c], in1=ta[:], op0=ALU.mult, op1=ALU.add)
            if c < 8:
                act(obf[:, 0:512], ta[:], AF.Silu, [rtmp[0]], [robf])
                bt = PSA.get()
                pTx = psb(bt).bitcast(BF16)
                for t in range(NT):
                    tr(pTx[:, t * 128:(t + 1) * 128], obf[:, t * 128:(t + 1) * 128], ident_b, [robf, rcb],
                       [psr[bt]], inc=(t == NT - 1))
                act(xs_tok[:, :, c * 128:(c + 1) * 128], pTx[:, 0:512].rearrange("p (t f) -> p t f", t=4), AF.Copy,
                    [psr[bt]], [rxs])
            else:
                act(bcT[:, c - 8, :], ta[:], AF.Silu, [rtmp[0]], [rbc])
        b = PSA.get()
        pT = psb(b).bitcast(BF16)
        for t in range(NT):
            for g in range(2):
                tr(pT[:, (t * 2 + g) * 128:(t * 2 + g + 1) * 128], bcT[:, g, t * 128:(t + 1) * 128], ident_b,
                   [rbc, rcb], [psr[b]], inc=(t == NT - 1 and g == 1))
        act(bm_tok[:].rearrange("p t g n -> p (t g n)"), pT, AF.Copy, [psr[b]], [rbmt])
        i = wslot()
        wdt = wbuf[i][:, 0:256].rearrange("p (k n) -> p k n", k=16)
        dma("pool", wdt, win3[:, :, O_DT:O_DT + 16], [], [rw[i]], f"w{i}")
        for t in range(NT):
            ts = slice(t * 128, (t + 1) * 128)
            b = PSA.get()
            for k in range(KC):
                mm(psb(b, 16), hT[:, k, ts], wdt[:, k, :], k == 0, k == KC - 1, [rhT[k], rw[i]], [psr[b]])
            dve("tensor_tensor", [psr[b], rsp], [rdt], out=dtall[:, t, :], in0=psb(b, 16), in1=dtb[l][:], op=ALU.add)
        act(dtall[:], dtall[:], AF.Exp, [rdt], [rdt])
        act(dtall[:], dtall[:], AF.Ln, [rdt, rcst], [rdt], bias=ONE, scale=1.0)
        dve("tensor_tensor", [rdt, rsp], [rdt], out=daall[:], in0=dtall[:], in1=vw(aneg[l][:], [[0, 4], [1, 16]]),
            op=ALU.mult)
        act(Sbf[:], Sst[l][:], AF.Copy, [rS[l]], [rSbf])
        def P1(t):
            par = t % 2
            ts = slice(t * 128, (t + 1) * 128)
            q0 = 64 + par * 96
            rs = rsmc[par]
            xdt, rxdt, xdt2, rxdt2, M_bf, rM = xdtb[par], rxdtb[par], xdt2b[par], rxdt2b[par], M_bf2[par], rM2[par]
            da = daall[:, t, :]
            dt_ = dtall[:, t, :]
            b = PSA.get()
            mm(psb(b, 16), U_f, da, True, True, [rcf, rdt], [psr[b]], inc=False)
            mm(psb(b, 16, 16), ones_f, da, True, True, [rcf, rdt], [psr[b]], inc=True)
            act(sm[:, q0:q0 + 32], psb(b, 32), AF.Copy, [psr[b]], [rs])
            ea = sm[:, q0 + 32:q0 + 48]
            dsd = sm[:, q0 + 48:q0 + 64]
            cd = sm[:, q0 + 64:q0 + 80]
            w2 = sm[:, q0 + 80:q0 + 96]
            act(ea, sm[:, q0:q0 + 16], AF.Exp, [rs], [rs])
            act(cd, sm[:, q0 + 16:q0 + 32], AF.Exp, [rs], [rs])
            dve("tensor_tensor", [rs], [rs], out=dsd, in0=sm[:, q0 + 16:q0 + 32], in1=sm[:, q0:q0 + 16], op=ALU.subtract)
            act(dsd, dsd, AF.Exp, [rs], [rs])
            dve("tensor_tensor", [rs, rdt], [rs], out=w2, in0=dsd, in1=dt_, op=ALU.mult)
            xs3 = xs_tok[:, t, :].rearrange("p (h d) -> p h d", h=16)
            dve("tensor_tensor", [rxs, rdt], [rxdt], out=xdt[:], in0=xs3, in1=vw(dt_, [[1, 16], [0, 64]]), op=ALU.mult)
            dve("tensor_tensor", [rxs, rs], [rxdt2], out=xdt2[:], in0=xs3, in1=vw(w2, [[1, 16], [0, 64]]), op=ALU.mult)
            b = PSA.get()
            for g in range(2):
                mm(psb(b, 128, g * 128), bcT[:, g, ts], bcT[:, 2 + g, ts], True, True, [rbc], [psr[b]], inc=(g == 1))
            dve("tensor_tensor", [psr[b], rcf], [rcbm], out=cbm[:], in0=psb(b, 256).rearrange("p (g l) -> p g l", g=2),
                in1=vw(U_f, [[0, 2], [1, 128]]), op=ALU.mult)
            yield
            for g in range(2):
                dag = daall[:, t, g * 8:(g + 1) * 8]
                dve("tensor_tensor", [rdt, rcf], [rT[0]], out=T[0].rearrange("p (h l) -> p h l", h=8),
                    in0=vw(dag, [[1, 8], [0, 128]]), in1=vw(U_f, [[0, 8], [1, 128]]), op=ALU.mult)
                dve("tensor_copy", [rdt], [rT[1]], out=T[1].rearrange("p (h l) -> p h l", h=8),
                    in_=vw(dag, [[1, 8], [0, 128]]))
                b = PSA.get(2)
                for hf in range(2):
                    mm(psb(b + hf), ones_f, T[0][:, hf * 512:(hf + 1) * 512], True, False, [rcf, rT[0]], [psr[b + hf]],
                       inc=False)
                    mm(psb(b + hf), negU_f, T[1][:, hf * 512:(hf + 1) * 512], False, True, [rcf, rT[1]], [psr[b + hf]],
                       inc=True)
                yield
                dve("tensor_scalar", [psr[b], psr[b + 1]], [rT[2]], out=T[2], in0=ps[:, b * 512:(b + 2) * 512],
                    scalar1=0.0, scalar2=None, op0=ALU.min)
                act(T[3], T[2], AF.Exp, [rT[2]], [rT[3]])
                dve("tensor_tensor", [rT[3], rcbm], [rM[g]], out=M_bf[:, g, :, :],
                    in0=T[3].rearrange("p (h l) -> p h l", h=8), in1=vw(cbm[:, g, :], [[0, 8], [1, 128]]), op=ALU.mult)
                yield

        def P2(t):
            par = t % 2
            ts = slice(t * 128, (t + 1) * 128)
            q0 = 64 + par * 96
            rs = rsmc[par]
            xdt, rxdt, xdt2, rxdt2, M_bf, rM = xdtb[par], rxdtb[par], xdt2b[par], rxdt2b[par], M_bf2[par], rM2[par]
            ea = sm[:, q0 + 32:q0 + 48]
            cd = sm[:, q0 + 64:q0 + 80]
            xs3 = xs_tok[:, t, :].rearrange("p (h d) -> p h d", h=16)
            b = PSA.get(2)
            for h in range(16):
                mm(ps[:, b * 512 + h * 64: b * 512 + (h + 1) * 64], M_bf[:, h // 8, h % 8, :], xdt[:, h, :], True, True,
                   [rM[h // 8], rxdt], [psr[b + h // 8]], inc=(h % 8 == 7))
            b2 = PSA.get(2)
            for g in range(2):
                mm(psb(b2 + g), bcT[:, 2 + g, ts], Sbf[:, g * 512:(g + 1) * 512], True, True, [rbc, rSbf], [psr[b2 + g]])
            b3 = PSA.get(2)
            for g in range(2):
                mm(psb(b3 + g), bm_tok[:, t, g, :], xdt2[:, g * 8:(g + 1) * 8, :].rearrange("p h d -> p (h d)"), True,
                   True, [rbmt, rxdt2], [psr[b3 + g]])
            yield
            dve("tensor_tensor", [psr[b2], psr[b2 + 1], rs], [rT[4]], out=T[4].rearrange("p (h d) -> p h d", h=16),
                in0=ps[:, b2 * 512:(b2 + 2) * 512].rearrange("p (h d) -> p h d", h=16),
                in1=vw(ea, [[1, 16], [0, 64]]), op=ALU.mult)
            dve("tensor_tensor", [psr[b], psr[b + 1], rT[4]], [rT[4]], out=T[4], in0=ps[:, b * 512:(b + 2) * 512],
                in1=T[4], op=ALU.add)
            dve("tensor_tensor", [rxs, rsp], [rT[5]], out=T[5].rearrange("p (h d) -> p h d", h=16), in0=xs3,
                in1=vw(dsk[l][:], [[1, 16], [0, 64]]), op=ALU.mult)
            dve("tensor_tensor", [rT[4], rT[5]], [rT[4]], out=T[4], in0=T[4], in1=T[5], op=ALU.add)
            dve("tensor_tensor", [rS[l], rs], [rS[l]], out=Sst[l][:].rearrange("p (h d) -> p h d", h=16),
                in0=Sst[l][:].rearrange("p (h d) -> p h d", h=16), in1=vw(cd, [[1, 16], [0, 64]]), op=ALU.mult)
            dve("tensor_tensor", [psr[b3], psr[b3 + 1], rS[l]], [rS[l]], out=Sst[l][:], in0=ps[:, b3 * 512:(b3 + 2) * 512],
                in1=Sst[l][:], op=ALU.add)
            act(Sbf[:], Sst[l][:], AF.Copy, [rS[l]], [rSbf])
            yield
            dve("tensor_tensor", [rT[4]] + rzs, [rT[5]], out=T[5], in0=T[4], in1=zs[:, t, :], op=ALU.mult)
            for g in range(2):
                S.op("act", lambda e, g=g: e.activation(out=obf[:, g * 512:(g + 1) * 512],
                                                        in_=T[5][:, g * 512:(g + 1) * 512], func=AF.Square,
                                                        accum_out=sm[:, 44 + g:45 + g]), [rT[5]], [robf, rsmg])
            act(sm[:, 46:48], sm[:, 44:46], AF.Sqrt, [rsmg, rcst], [rsmg], scale=1.0 / 512, bias=EPS6)
            dve("reciprocal", [rsmg], [rsmg], out=sm[:, 46:48], in_=sm[:, 46:48])
            for g in range(2):
                act(obf[:, g * 512:(g + 1) * 512], T[5][:, g * 512:(g + 1) * 512], AF.Copy, [rT[5], rsmg], [robf],
                    scale=sm[:, 46 + g:47 + g])
            yield
            to_mixT(obf, 8, 8, t, l, 40, robf)
            yield

        def drain(*gens):
            gens = [g for g in gens if g is not None]
            while gens:
                for g in list(gens):
                    try:
                        next(g)
                    except StopIteration:
                        gens.remove(g)

        drain(P1(0))
        for t in range(NT):
            drain(P1(t + 1) if t + 1 < NT else None, P2(t))

    def out_proj(l, gate):
        wo3 = prm["w_out"][l].rearrange("(k p) n -> p k n", p=128)
        for gi in range(4):
            wo, ro = load_w(wo3[:, :, gi * 512:(gi + 1) * 512], 16, 512)
            for mi in range(4):
                m = gi * 4 + mi
                b = PSA.get()
                for k in range(KC):
                    mm(psb(b), wo[:, k, mi * 128:(mi + 1) * 128], mixT[:, k, :], k == 0, k == KC - 1, [rmix[k], ro],
                       [psr[b]])
                dve("scalar_tensor_tensor", [psr[b], rxT[m], rmod], [rxT[m]], out=xT[:, m, :], in0=psb(b),
                    scalar=gate[:, m:m + 1], in1=xT[:, m, :], op0=ALU.mult, op1=ALU.add)

    def mlp(l, gate):
        w13 = prm["w_mlp1"][l].rearrange("(k p) n -> p k n", p=128)
        for f in range(16):
            w1, r1 = load_w(w13[:, :, f * 512:(f + 1) * 512], 16, 512)
            w2, r2 = load_w(prm["w_mlp2"][l][f * 512:(f + 1) * 512, :].rearrange("(c p) n -> p c n", p=128), 4, 2048)
            hb = (f % 2) * 4
            for cc in range(4):
                b = PSA.get()
                for k in range(KC):
                    mm(psb(b), w1[:, k, cc * 128:(cc + 1) * 128], hT[:, k, :], k == 0, k == KC - 1, [rhT[k], r1],
                       [psr[b]])
                act(tmps[cc % 2][:], psb(b), AF.Relu, [psr[b]], [rtmp[cc % 2]])
                dve("tensor_tensor", [rtmp[cc % 2]], [rmix[hb + cc]], out=mixT[:, hb + cc, :], in0=tmps[cc % 2][:],
                    in1=tmps[cc % 2][:], op=ALU.mult)
            for m in range(KC):
                b = PSA.get()
                for cc in range(4):
                    mm(psb(b), w2[:, cc, m * 128:(m + 1) * 128], mixT[:, hb + cc, :], cc == 0, cc == 3,
                       [rmix[hb + cc], r2], [psr[b]])
                dve("scalar_tensor_tensor", [psr[b], rxT[m], rmod], [rxT[m]], out=xT[:, m, :], in0=psb(b),
                    scalar=gate[:, m:m + 1], in1=xT[:, m, :], op0=ALU.mult, op1=ALU.add)

    for s in range(nseq):
        for l in range(2):
            dve("memset", [], [rS[l]], ap=Sst[l][:], constant=0.0)
            dve("memset", [], [rhist[l]], ap=hist[l][:], constant=0.0)
        for blk in range(nblk):
            tok0 = blk * TB
            for t in range(NT):
                dma("sp", xin, x[s, tok0 + t * 128: tok0 + (t + 1) * 128, :], [], rxin, "xin")
                for j in range(4):
                    b = PSA.get()
                    for i4 in range(4):
                        k = j * 4 + i4
                        tr(psb(b, 128, i4 * 128), xin[:, k * 128:(k + 1) * 128], ident_f, rxin + [rcf], [psr[b]],
                           inc=(i4 == 3))
                    act(xT[:, j * 4:(j + 1) * 4, t * 128:(t + 1) * 128], psb(b).rearrange("p (i f) -> p i f", i=4),
                        AF.Copy, [psr[b]], [rxT[j * 4 + i4] for i4 in range(4)])
            for l in range(nlayer):
                i = l * 2 + s
                WS.l = l
                WS.ti = 0
                WS.first = (s == 0 and blk == 0)
                mo = modT[:, i * 96:(i + 1) * 96]
                win3 = prm["w_in"][l].rearrange("(k p) n -> p k n", p=128)
                dma("sp", lnG[:], prm["gm_ln_g"][l].rearrange("h e -> (h e)").partition_broadcast(128), [], [rln], "ln")
                dma("sp", lnB[:], prm["gm_ln_b"][l].rearrange("h e -> (h e)").partition_broadcast(128), [], [rln], "ln")
                norm_to_hT(gsc[:, i * 32:i * 32 + 16], mo[:, 0:16])
                if do_a:
                    mixer_a(l, win3)
                else:
                    for k in range(0, 4):
                        dve("memset", [], [rmix[k]], ap=mixT[:, k, :], constant=0.0)
                if do_b:
                    mixer_b(l, win3, blk * NT)
                else:
                    for k in range(4, 8):
                        dve("memset", [], [rmix[k]], ap=mixT[:, k, :], constant=0.0)
                if do_c:
                    mixer_c(l, win3)
                else:
                    for k in range(8, 16):
                        dve("memset", [], [rmix[k]], ap=mixT[:, k, :], constant=0.0)
                out_proj(l, mo[:, 32:48])
                if do_mlp:
                    norm_to_hT(gsc[:, i * 32 + 16:i * 32 + 32], mo[:, 48:64])
                    mlp(l, mo[:, 80:96])
            dma("sp", fgb, prm["final_norm_g"].partition_broadcast(128), [], rfgb, "fgb")
            for t in range(NT):
                b = PSA.get(4)
                for k in range(KC):
                    tr(ps[:, b * 512 + k * 128: b * 512 + (k + 1) * 128], xT[:, k, t * 128:(t + 1) * 128], ident_f,
                       [rxT[k], rcf], [psr[b + k // 4]], inc=(k % 4 == 3))
                pr4 = [psr[b + q] for q in range(4)]
                S.op("act", lambda e, b=b: e.activation(out=xin, in_=ps[:, b * 512:(b + 4) * 512], func=AF.Square,
                                                        accum_out=sm[:, 36:37]), pr4, rxin + [rsmf])
                act(sm[:, 37:38], sm[:, 36:37], AF.Sqrt, [rsmf, rcst], [rsmf], scale=1.0 / D, bias=EPS6)
                dve("reciprocal", [rsmf], [rsmf], out=sm[:, 37:38], in_=sm[:, 37:38])
                dve("scalar_tensor_tensor", pr4 + [rsmf] + rfgb, rxin, out=xin, in0=ps[:, b * 512:(b + 4) * 512],
                    scalar=sm[:, 37:38], in1=fgb, op0=ALU.mult, op1=ALU.mult)
                dma("sp", out[s, tok0 + t * 128: tok0 + (t + 1) * 128, :], xin, rxin, [], "xin")

    with nc.allow_non_contiguous_dma(reason="small strided parameter loads"):
        S.emit(final_waits=["xin"])
    es.close()
    return nc


def make_consts():
    c = np.zeros((128, 640), np.float32)
    i = np.arange(128)
    c[:, C_ID:C_ID + 128] = np.eye(128, dtype=np.float32)
    u = (i[:, None] <= i[None, :]).astype(np.float32)
    c[:, C_U:C_U + 128] = u
    c[:, C_NU:C_NU + 128] = -u
    c[:, C_SL:C_SL + 128] = 1.0 - u
    c[:, C_ONE:C_ONE + 128] = 1.0
    return c


_NC_CACHE = {}


def kernel(**inputs):
    n = 8
    key = "full"
    if key not in _NC_CACHE:
        _NC_CACHE[key] = build()
    nc = _NC_CACHE[key]
    x = np.ascontiguousarray(inputs["x"], dtype=np.float32)
    c = np.ascontiguousarray(inputs["c"], dtype=np.float32)
    consts = make_consts()
    params = {k: np.ascontiguousarray(inputs[k], dtype=np.float32) for k in PARAM_NAMES}
    in_maps = []
    for i in range(n):
        m = {"x": x[2 * i:2 * i + 2], "c": c[2 * i:2 * i + 2], "consts": consts}
        m.update(params)
        in_maps.append(m)
    res = run_bass_kernel_spmd(nc, in_maps, core_ids=list(range(n)))
    return np.concatenate([r["out"] for r in res.results], axis=0)
```

```python
import contextlib
import numpy as np
import concourse.bass as bass
import concourse.mybir as mybir
from concourse.bass_utils import run_bass_kernel_spmd

F32 = mybir.dt.float32
BF16 = mybir.dt.bfloat16
AF = mybir.ActivationFunctionType
ALU = mybir.AluOpType

ENGS = ("pe", "act", "dve", "pool", "sp")


class Reg:
    __slots__ = ("name", "w", "rd")

    def __init__(self, name):
        self.name = name
        self.w = None
        self.rd = {}


class Rec:
    __slots__ = ("eng", "fn", "inc", "deps", "dma_sem", "cnt", "idx", "dma_waits")

    def __init__(self, eng, fn, inc, dma_sem):
        self.eng = eng
        self.fn = fn
        self.inc = inc
        self.deps = []
        self.dma_waits = {}
        self.dma_sem = dma_sem
        self.cnt = None
        self.idx = None


class Sched:
    def __init__(self, nc):
        self.nc = nc
        self.recs = {e: [] for e in ENGS}
        self.dma_cnt = {}
        self.n = 0

    def _dep(self, w, d):
        if d is None or d is w:
            return
        if d.dma_sem is not None:
            k = d.dma_sem
            w.dma_waits[k] = max(w.dma_waits.get(k, 0), self.dma_cnt[k])
            return
        if d.eng == "pe" and w.eng == "pe" and w.dma_sem is None:
            return
        if not d.inc:
            lst = self.recs[d.eng]
            tgt = None
            for j in range(d.idx + 1, len(lst)):
                if lst[j].inc and lst[j].dma_sem is None:
                    tgt = lst[j]
                    break
            if tgt is None:
                d.inc = True
                tgt = d
            d = tgt
        w.deps.append(d)

    def op(self, eng, fn, reads=(), writes=(), inc=True, dma=None):
        r = Rec(eng, fn, inc if dma is None else False, dma)
        r.idx = len(self.recs[eng])
        for x in reads:
            self._dep(r, x.w)
        for x in writes:
            self._dep(r, x.w)
            for rr in x.rd.values():
                self._dep(r, rr)
        if dma is not None:
            self.dma_cnt[dma] = self.dma_cnt.get(dma, 0) + 16
            r.cnt = self.dma_cnt[dma]
        self.recs[eng].append(r)
        for x in writes:
            x.w = r
            x.rd = {}
        for x in reads:
            x.rd[eng if dma is None else (dma, r.cnt)] = r
        self.n += 1
        return r

    def emit(self, final_waits=()):
        nc = self.nc
        with contextlib.ExitStack() as es:
            sems = {}
            for e in ENGS:
                sems[e] = es.enter_context(nc.semaphore("s_" + e))
            for k in self.dma_cnt:
                sems[("dma", k)] = es.enter_context(nc.semaphore("d_" + k))
            for e in ENGS:
                c = 0
                for r in self.recs[e]:
                    if r.dma_sem is None and r.inc:
                        c += 1
                        r.cnt = c
            block = es.enter_context(nc.Block())

            def run(e, eng):
                waited = {}
                for r in self.recs[e]:
                    need = {}
                    for d in r.deps:
                        need[d.eng] = max(need.get(d.eng, 0), d.cnt)
                    for k, v in r.dma_waits.items():
                        need[("dma", k)] = max(need.get(("dma", k), 0), v)
                    for k, v in need.items():
                        if waited.get(k, 0) < v:
                            eng.wait_ge(sems[k], v)
                            waited[k] = v
                    ins = r.fn(eng)
                    if r.dma_sem is not None:
                        ins.then_inc(sems[("dma", r.dma_sem)], 16)
                    elif r.inc:
                        ins.then_inc(sems[e], 1)
                if e == "sp":
                    for k in final_waits:
                        eng.wait_ge(sems[("dma", k)], self.dma_cnt[k])

            @block.tensor
            def _(eng):
                run("pe", eng)

            @block.scalar
            def _(eng):
                run("act", eng)

            @block.vector
            def _(eng):
                run("dve", eng)

            @block.gpsimd
            def _(eng):
                run("pool", eng)

            @block.sync
            def _(eng):
                run("sp", eng)


def vw(ap, dims):
    return bass.AP(ap.tensor, ap.offset, [list(ap.ap[0])] + [list(d) for d in dims])


D = 2048
KC = 16
TB = 512
NT = 4
SEQ = 2048
NW = 2
O_U, O_V, O_Q, O_K, O_VV, O_Z, O_X, O_DT = 0, 512, 1024, 1536, 1664, 1792, 2816, 4352
C_ID, C_U, C_NU, C_SL, C_ONE = 0, 128, 256, 384, 512

PARAM_NAMES = ["ada_w", "ada_b", "norm1_g", "w_in", "gm_ln_g", "gm_ln_b", "gm_ws", "gm_bs",
               "gm_norm_g", "attn_sinks", "attn_norm_g", "conv_w", "conv_b", "dt_bias", "a_log",
               "d_skip", "ssm_norm_g", "w_out", "norm2_g", "w_mlp1", "w_mlp2", "final_norm_g"]
PARAM_SHAPES = {
    "ada_w": [2, 2048, 12288], "ada_b": [2, 12288], "norm1_g": [2, 2048], "w_in": [2, 2048, 4368],
    "gm_ln_g": [2, 4, 128], "gm_ln_b": [2, 4, 128], "gm_ws": [2, 4, 128, 128], "gm_bs": [2, 4, 128],
    "gm_norm_g": [2, 512], "attn_sinks": [2, 8], "attn_norm_g": [2, 512], "conv_w": [2, 4, 1536],
    "conv_b": [2, 1536], "dt_bias": [2, 16], "a_log": [2, 16], "d_skip": [2, 16],
    "ssm_norm_g": [2, 1024], "w_out": [2, 2048, 2048], "norm2_g": [2, 2048],
    "w_mlp1": [2, 2048, 8192], "w_mlp2": [2, 8192, 2048], "final_norm_g": [2048],
}


def slim_shapes(nlayer, do_mlp):
    sh = {k: list(v) for k, v in PARAM_SHAPES.items()}
    for k in ("ada_w", "w_in", "w_out", "w_mlp1", "w_mlp2"):
        sh[k][0] = nlayer
    if not do_mlp:
        sh["w_mlp1"] = [1, 128, 128]
        sh["w_mlp2"] = [1, 128, 128]
    return sh


def build(nseq=2, nblk=4, nlayer=2, do_a=True, do_b=True, do_c=True, do_mlp=True, slim=False):
    nc = bass.Bass("TRN2", target_bir_lowering=False)
    din = lambda n, s: nc.dram_tensor(n, s, F32, kind="ExternalInput").ap()
    x = din("x", [2, SEQ, D])
    c_in = din("c", [2, D])
    shp = slim_shapes(nlayer, do_mlp) if slim else PARAM_SHAPES
    prm = {n: din(n, shp[n]) for n in PARAM_NAMES}
    consts = din("consts", [128, 640])
    out = nc.dram_tensor("out", [2, SEQ, D], F32, kind="ExternalOutput").ap()
    mod_d = nc.dram_tensor("mod_d", [2, 2, 12288], F32).ap()
    NTILE = 48
    wscr = nc.dram_tensor("wscr", [2, NTILE, 128, 8192], BF16).ap()

    es = contextlib.ExitStack()
    S = Sched(nc)

    def sb(name, shape, dt=F32):
        return es.enter_context(nc.sbuf_tensor(name, shape, dt))

    xT = sb("xT", [128, KC, TB]); rxT = [Reg(f"xT{k}") for k in range(KC)]
    hT = sb("hT", [128, KC, TB], BF16); rhT = [Reg(f"hT{k}") for k in range(KC)]
    mixT = sb("mixT", [128, KC, TB], BF16); rmix = [Reg(f"mix{k}") for k in range(KC)]
    wbuf = [sb(f"wbuf{i}", [128, 8192], BF16) for i in range(NW)]
    rw = [Reg(f"w{i}") for i in range(NW)]
    TT = sb("TT", [128, 6 * 1024]); rT = [Reg(f"T{i}") for i in range(6)]
    T = [TT[:, i * 1024:(i + 1) * 1024] for i in range(6)]
    xin = TT[:, 2048:4096]; rxin = [rT[2], rT[3]]
    fgb = TT[:, 0:2048]; rfgb = [rT[0], rT[1]]
    cf = sb("cf", [128, 640]); rcf = Reg("cf")
    cb = sb("cb", [128, 640], BF16); rcb = Reg("cb")
    cst = sb("cst", [128, 4]); rcst = Reg("cst")
    vrow = sb("vrow", [128, 128]); rvrow = Reg("vrow")
    vT = [sb(f"vT{l}", [128, 112]) for l in range(2)]; rvT = [Reg(f"vT{l}") for l in range(2)]
    vTf = sb("vTf", [128, 16]); rvTf = Reg("vTf")
    modT = sb("modT", [128, 4 * 96]); rmod = Reg("modT")
    gsc = sb("gsc", [128, 4 * 32]); rgsc = Reg("gsc")
    lnG = sb("lnG", [128, 512]); lnB = sb("lnB", [128, 512]); rln = Reg("ln")
    WsT = [sb(f"WsT{l}", [128, 4, 128], BF16) for l in range(2)]; rws = [Reg(f"ws{l}") for l in range(2)]
    esink = [sb(f"esink{l}", [128, 8]) for l in range(2)]
    dtb = [sb(f"dtb{l}", [128, 16]) for l in range(2)]
    aneg = [sb(f"aneg{l}", [128, 16]) for l in range(2)]
    dsk = [sb(f"dsk{l}", [128, 16]) for l in range(2)]
    rsp = Reg("smallparams")
    rstd_b = sb("rstd_b", [128, TB]); rrstd = Reg("rstd")
    tmps = [sb(f"tmp{i}", [128, TB]) for i in range(2)]; rtmp = [Reg(f"tmp{i}") for i in range(2)]
    AB = sb("AB", [128, 4096], BF16)
    qT = AB[:, 0:2048].rearrange("p (j t) -> p j t", j=4); rqT = Reg("qT")
    e_cur = AB[:, 2048:3072]; recur = Reg("ecur")
    e_prev = AB[:, 3072:4096]; reprev = Reg("eprev")
    zs = AB[:, :].rearrange("p (t f) -> p t f", t=4); rzs = [rqT, recur, reprev]
    kT = [sb(f"kT{l}", [128, 2, 640], BF16) for l in range(2)]; rkT = [Reg(f"kT{l}") for l in range(2)]
    vaug = [sb(f"vaug{l}", [128, 5, 2, 72], BF16) for l in range(2)]
    rva = [[Reg(f"va{l}_{i}") for i in range(5)] for l in range(2)]
    xs_tok = sb("xs_tok", [128, 4, 1024], BF16); rxs = Reg("xs_tok")
    bcT = sb("bcT", [128, 4, TB], BF16); rbc = Reg("bcT")
    bm_tok = sb("bm_tok", [128, 4, 2, 128], BF16); rbmt = Reg("bm_tok")
    xraw = sb("xraw", [128, 2, 515]); rxraw = [Reg("xraw0"), Reg("xraw1")]
    hist = [sb(f"hist{l}", [128, 12, 3]) for l in range(2)]; rhist = [Reg(f"hist{l}") for l in range(2)]
    Sst = [sb(f"Sst{l}", [128, 1024]) for l in range(2)]; rS = [Reg(f"S{l}") for l in range(2)]
    Sbf = sb("Sbf", [128, 1024], BF16); rSbf = Reg("Sbf")
    M_bf2 = [sb(f"M_bf{i}", [128, 2, 8, 128], BF16) for i in range(2)]; rM2 = [[Reg(f"M{i}{g}") for g in range(2)] for i in range(2)]
    xdtb = [sb(f"xdt{i}", [128, 16, 64], BF16) for i in range(2)]; rxdtb = [Reg(f"xdt{i}") for i in range(2)]
    xdt2b = [sb(f"xdt2{i}", [128, 16, 64], BF16) for i in range(2)]; rxdt2b = [Reg(f"xdt2{i}") for i in range(2)]
    vn = sb("vn", [128, 512], BF16); rvn = Reg("vn")
    obf = sb("obf", [128, 1024], BF16); robf = Reg("obf")
    sm = sb("sm", [128, 256]); rsm = Reg("sm")
    rsmc = [Reg("smc0"), Reg("smc1")]; rsmg = Reg("smg"); rsmf = Reg("smf")
    dtall = sb("dtall", [128, 4, 16]); daall = sb("daall", [128, 4, 16]); rdt = Reg("dt")
    cbm = sb("cbm", [128, 2, 128]); rcbm = Reg("cbm")
    cTf = sb("cTf", [128, KC, 2]); cTb = sb("cTb", [128, KC, 2], BF16); rcT = Reg("cT")
    abt = TT[0:2, 4096:4608]; rabt = rT[4]
    mrow = TT[0:2, 5120:5632]; rmrow = rT[5]

    ps = es.enter_context(nc.psum_tensor("ps", [128, 4096], F32))
    psr = [Reg(f"ps{b}") for b in range(8)]

    class PSA:
        i = 0

        @staticmethod
        def get(n=1):
            while PSA.i % n:
                PSA.i += 1
            b = PSA.i % 8
            PSA.i += n
            return b

    def psb(b, n=512, off=0):
        return ps[:, b * 512 + off: b * 512 + off + n]

    def mm(o, lhsT, rhs, start, stop, rd, wr, inc=None):
        S.op("pe", lambda e: e.matmul(o, lhsT, rhs, start=start, stop=stop), rd, wr,
             inc=(stop if inc is None else inc))

    def tr(o, i, ident, rd, wr, inc=True):
        S.op("pe", lambda e: e.transpose(o, i, ident), rd, wr, inc=inc)

    def act(o, i, func, rd, wr, **kw):
        S.op("act", lambda e: e.activation(out=o, in_=i, func=func, **kw), rd, wr)

    def dve(meth, rd, wr, **kw):
        S.op("dve", lambda e: getattr(e, meth)(**kw), rd, wr)

    def pool(meth, rd, wr, **kw):
        S.op("pool", lambda e: getattr(e, meth)(**kw), rd, wr)

    def dma(q, o, i, rd, wr, key):
        S.op(q, lambda e: e.dma_start(out=o, in_=i), rd, wr, dma=key)

    class WL:
        i = 0

    def wslot():
        i = WL.i % NW
        WL.i += 1
        return i

    class WS:
        first = True
        l = 0
        ti = 0
    rwscr = [Reg("wscr0"), Reg("wscr1")]

    def load_w(src3, k, n, cache=True):
        i = wslot()
        v = wbuf[i][:, 0:k * n].rearrange("p (k n) -> p k n", k=k)
        if not cache:
            dma("pool", v, src3, [], [rw[i]], f"w{i}")
            return v, rw[i]
        l_, ti = WS.l, WS.ti
        WS.ti += 1
        flat = wbuf[i][:, 0:k * n]
        if WS.first:
            dma("pool", v, src3, [], [rw[i]], f"w{i}")
            dma("sp", wscr[l_, ti, :, 0:k * n], flat, [rw[i]], [rwscr[l_]], f"ws{l_}")
        else:
            dma("sp", flat, wscr[l_, ti, :, 0:k * n], [rwscr[l_]], [rw[i]], f"wh{i}")
        return v, rw[i]

    ident_f = cf[:, C_ID:C_ID + 128]
    U_f = cf[:, C_U:C_U + 128]
    negU_f = cf[:, C_NU:C_NU + 128]
    ones_f = cf[:, C_ONE:C_ONE + 128]
    ident_b = cb[:, C_ID:C_ID + 128]
    U_b = cb[:, C_U:C_U + 128]
    SL_b = cb[:, C_SL:C_SL + 128]
    ones_b = cb[:, C_ONE:C_ONE + 128]
    EPS6 = cst[:, 0:1]
    EPS5 = cst[:, 1:2]
    ONE = cst[:, 2:3]

    dma("sp", cf[:], consts, [], [rcf], "const")
    dma("pool", cb[:], consts, [], [rcb], "constb")
    dve("memset", [], [rcst], ap=cst[:, 0:1], constant=1e-6)
    dve("memset", [], [rcst], ap=cst[:, 1:2], constant=1e-5)
    dve("memset", [], [rcst], ap=cst[:, 2:3], constant=1.0)
    dve("memset", [], [rcst], ap=cst[:, 3:4], constant=0.0)
    for l in range(2):
        for i in range(5):
            dve("memset", [], [rva[l][i]], ap=vaug[l][:, i, :, 64:72], constant=0.0)
            dve("memset", [], [rva[l][i]], ap=vaug[l][:, i, :, 64:65], constant=1.0)

    def vec_rows(l):
        rows = []
        rows.append((0, 16, prm["norm1_g"][l].rearrange("(k p) -> k p", p=128)))
        rows.append((16, 16, prm["norm2_g"][l].rearrange("(k p) -> k p", p=128)))
        rows.append((32, 4, prm["gm_norm_g"][l].rearrange("(k p) -> k p", p=128)))
        rows.append((36, 4, prm["attn_norm_g"][l].rearrange("(k p) -> k p", p=128)))
        rows.append((40, 8, prm["ssm_norm_g"][l].rearrange("(k p) -> k p", p=128)))
        rows.append((48, 12, prm["conv_b"][l].rearrange("(k p) -> k p", p=128)))
        rows.append((60, 48, prm["conv_w"][l].rearrange("t (k p) -> (t k) p", p=128)))
        rows.append((108, 4, prm["gm_bs"][l]))
        return rows

    for l in range(2):
        for (r0, n, src) in vec_rows(l):
            dma("sp", vrow[r0:r0 + n, :], src, [], [rvrow], "vrow")
        b = PSA.get()
        tr(psb(b, 112), vrow[0:112, :], ident_f[0:112, 0:112], [rvrow, rcf], [psr[b]])
        act(vT[l][:], psb(b, 112), AF.Copy, [psr[b]], [rvT[l]])
    dma("sp", vrow[0:16, :], prm["final_norm_g"].rearrange("(k p) -> k p", p=128), [], [rvrow], "vrow")
    b = PSA.get()
    tr(psb(b, 16), vrow[0:16, :], ident_f[0:16, 0:16], [rvrow, rcf], [psr[b]])
    act(vTf[:], psb(b, 16), AF.Copy, [psr[b]], [rvTf])

    for l in range(2):
        dma("sp", esink[l][:], prm["attn_sinks"][l].partition_broadcast(128), [], [rsp], "sp")
        dma("sp", dtb[l][:], prm["dt_bias"][l].partition_broadcast(128), [], [rsp], "sp")
        dma("sp", aneg[l][:], prm["a_log"][l].partition_broadcast(128), [], [rsp], "sp")
        dma("sp", dsk[l][:], prm["d_skip"][l].partition_broadcast(128), [], [rsp], "sp")
    for l in range(2):
        act(esink[l][:], esink[l][:], AF.Exp, [rsp], [rsp])
        act(aneg[l][:], aneg[l][:], AF.Exp, [rsp], [rsp])
        dve("tensor_scalar", [rsp], [rsp], out=aneg[l][:], in0=aneg[l][:], scalar1=-1.0, scalar2=None,
            op0=ALU.mult)
    for l in range(2):
        wsf = T[0].rearrange("p (h s) -> p h s", h=8)[:, 0:4, :]
        dma("sp", wsf, prm["gm_ws"][l].rearrange("h t s -> t h s"), [], [rT[0]], "wsf")
        b = PSA.get()
        for h in range(4):
            tr(psb(b, 128, h * 128), wsf[:, h, :], ident_f, [rT[0], rcf], [psr[b]], inc=(h == 3))
        dve("tensor_tensor", [psr[b], rcf], [rws[l]], out=WsT[l][:],
            in0=psb(b).rearrange("p (h t) -> p h t", h=4), in1=vw(U_f, [[0, 4], [1, 128]]), op=ALU.mult)

    for s_ in range(2):
        dma("sp", cTf[:, :, s_], c_in[s_].rearrange("(k p) -> p k", p=128), [], [rcT], "cT")
    act(cTb[:], cTf[:], AF.Silu, [rcT], [rcT])
    for l in range(nlayer):
        aw3 = prm["ada_w"][l].rearrange("(k p) n -> p k n", p=128)
        for n in range(24):
            w, rwi = load_w(aw3[:, :, n * 512:(n + 1) * 512], 16, 512, cache=False)
            dma("sp", abt, prm["ada_b"][l][n * 512:(n + 1) * 512].partition_broadcast(2), [], [rabt], "abt")
            b = PSA.get()
            for k in range(KC):
                mm(ps[0:2, b * 512:(b + 1) * 512], cTb[:, k, :], w[:, k, :], k == 0, k == KC - 1,
                   [rcT, rwi], [psr[b]])
            dve("tensor_tensor", [psr[b], rabt], [rmrow], out=mrow, in0=ps[0:2, b * 512:(b + 1) * 512],
                in1=abt, op=ALU.add)
            dma("sp", mod_d[l, :, n * 512:(n + 1) * 512], mrow, [rmrow], [], "mrow")
    rmodd = Reg("mod_d")
    for l in range(nlayer):
        for s in range(2):
            i = l * 2 + s
            S.op("sp", lambda e, l=l, s=s: e.dma_start(out=vrow[0:96, :],
                                                       in_=mod_d[l, s].rearrange("(r p) -> r p", p=128)),
                 [], [rmrow, rvrow], dma="mrow")
            b = PSA.get()
            tr(psb(b, 96), vrow[0:96, :], ident_f[0:96, 0:96], [rvrow, rcf], [psr[b]])
            act(modT[:, i * 96:(i + 1) * 96], psb(b, 96), AF.Copy, [psr[b]], [rmod])
            dve("scalar_tensor_tensor", [rmod, rvT[l]], [rgsc], out=gsc[:, i * 32:i * 32 + 16],
                in0=modT[:, i * 96 + 16:i * 96 + 32], scalar=1.0, in1=vT[l][:, 0:16], op0=ALU.add, op1=ALU.mult)
            dve("scalar_tensor_tensor", [rmod, rvT[l]], [rgsc], out=gsc[:, i * 32 + 16:i * 32 + 32],
                in0=modT[:, i * 96 + 64:i * 96 + 80], scalar=1.0, in1=vT[l][:, 16:32], op0=ALU.add, op1=ALU.mult)

    def norm_to_hT(gs, sh):
        for k in range(KC):
            act(hT[:, k, :], xT[:, k, :], AF.Square, [rxT[k]], [rhT[k]])
        b = PSA.get()
        for k in range(KC):
            mm(psb(b), ones_b, hT[:, k, :], k == 0, k == KC - 1, [rhT[k], rcb], [psr[b]])
        act(rstd_b[:], psb(b), AF.Sqrt, [psr[b], rcst], [rrstd], scale=1.0 / D, bias=EPS6)
        dve("reciprocal", [rrstd], [rrstd], out=rstd_b[:], in_=rstd_b[:])
        for k in range(KC):
            t_ = tmps[k % 2]
            dve("scalar_tensor_tensor", [rxT[k], rrstd, rgsc], [rtmp[k % 2]], out=t_[:], in0=xT[:, k, :],
                scalar=gs[:, k:k + 1], in1=rstd_b[:], op0=ALU.mult, op1=ALU.mult)
            act(hT[:, k, :], t_[:], AF.Identity, [rtmp[k % 2], rmod], [rhT[k]], bias=sh[:, k:k + 1], scale=1.0)

    def rms_scale(src, n, ss_col):
        act(T[3][:, 0:n], src, AF.Square, [rsm], [rT[3], rsm], accum_out=sm[:, ss_col:ss_col + 1])
        act(sm[:, ss_col + 1:ss_col + 2], sm[:, ss_col:ss_col + 1], AF.Sqrt, [rsm, rcst], [rsm],
            scale=1.0 / n, bias=EPS6)
        dve("reciprocal", [rsm], [rsm], out=sm[:, ss_col + 1:ss_col + 2], in_=sm[:, ss_col + 1:ss_col + 2])

    def to_mixT(src_bf, nchunk, k0, t, l, gcol, rsrc):
        b = PSA.get()
        pT = psb(b).bitcast(BF16)
        for j in range(nchunk):
            tr(pT[:, j * 128:(j + 1) * 128], src_bf[:, j * 128:(j + 1) * 128], ident_b, [rsrc, rcb], [psr[b]],
               inc=(j == nchunk - 1))
        for j in range(nchunk):
            act(mixT[:, k0 + j, t * 128:(t + 1) * 128], pT[:, j * 128:(j + 1) * 128], AF.Copy,
                [psr[b], rvT[l]], [rmix[k0 + j]], scale=vT[l][:, gcol + j:gcol + j + 1])

    def mixer_a(l, win3):
        wu, ru = load_w(win3[:, :, O_U:O_U + 512], 16, 512)
        wv, rv = load_w(win3[:, :, O_V:O_V + 512], 16, 512)
        def proj(t):
            ts = slice(t * 128, (t + 1) * 128)
            b = PSA.get(2)
            for k in range(KC):
                mm(psb(b), hT[:, k, ts], wu[:, k, :], k == 0, k == KC - 1, [rhT[k], ru], [psr[b]])
            for k in range(KC):
                mm(psb(b + 1), hT[:, k, ts], wv[:, k, :], k == 0, k == KC - 1, [rhT[k], rv], [psr[b + 1]])
            return b

        bnext = proj(0)
        for t in range(NT):
            b = bnext
            if t + 1 < NT:
                bnext = proj(t + 1)
            act(T[0], ps[:, b * 512:(b + 2) * 512], AF.Gelu_apprx_tanh, [psr[b], psr[b + 1]], [rT[0]])
            ug = T[0][:, 0:512]
            vg = T[0][:, 512:1024]
            st6 = sm[:, 0:24].rearrange("p (h s) -> p h s", h=4)
            mv = sm[:, 24:32].rearrange("p (h s) -> p h s", h=4)
            for h in range(4):
                dve("bn_stats", [rT[0]], [rsm], out=st6[:, h, :], in_=vg[:, h * 128:(h + 1) * 128])
            for h in range(4):
                dve("bn_aggr", [rsm], [rsm], out=mv[:, h, :], in_=st6[:, h, :])
            act(sm[:, 32:36], mv[:, :, 1], AF.Sqrt, [rsm, rcst], [rsm], bias=EPS5, scale=1.0)
            dve("reciprocal", [rsm], [rsm], out=sm[:, 32:36], in_=sm[:, 32:36])
            for h in range(4):
                dve("tensor_scalar", [rT[0], rsm], [rT[1]], out=T[1][:, h * 128:(h + 1) * 128],
                    in0=vg[:, h * 128:(h + 1) * 128], scalar1=mv[:, h, 0:1], scalar2=sm[:, 32 + h:33 + h],
                    op0=ALU.subtract, op1=ALU.mult)
            dve("tensor_tensor", [rT[1], rln], [rT[1]], out=T[1][:, 0:512], in0=T[1][:, 0:512], in1=lnG[:],
                op=ALU.mult)
            dve("tensor_tensor", [rT[1], rln], [rvn], out=vn[:], in0=T[1][:, 0:512], in1=lnB[:], op=ALU.add)
            b2 = PSA.get()
            for h in range(4):
                mm(psb(b2, 128, h * 128), WsT[l][:, h, :], vn[:, h * 128:(h + 1) * 128], True, True,
                   [rws[l], rvn], [psr[b2]], inc=(h == 3))
            for h in range(4):
                dve("scalar_tensor_tensor", [psr[b2], rT[0], rvT[l]], [rT[2]], out=T[2][:, h * 128:(h + 1) * 128],
                    in0=psb(b2, 128, h * 128), scalar=vT[l][:, 108 + h:109 + h], in1=ug[:, h * 128:(h + 1) * 128],
                    op0=ALU.add, op1=ALU.mult)
            S.op("act", lambda e: e.activation(out=T[3][:, 0:512], in_=T[2][:, 0:512], func=AF.Square,
                                               accum_out=sm[:, 40:41]), [rT[2]], [rT[3], rsm])
            act(sm[:, 41:42], sm[:, 40:41], AF.Sqrt, [rsm, rcst], [rsm], scale=1.0 / 512, bias=EPS6)
            dve("reciprocal", [rsm], [rsm], out=sm[:, 41:42], in_=sm[:, 41:42])
            act(obf[:, 0:512], T[2][:, 0:512], AF.Copy, [rT[2], rsm], [robf], scale=sm[:, 41:42])
            to_mixT(obf, 4, 0, t, l, 32, robf)

    def mixer_b(l, win3, g0):
        wq, rq = load_w(win3[:, :, O_Q:O_Q + 512], 16, 512)
        wkv, rk = load_w(win3[:, :, O_K:O_K + 256], 16, 256)
        wvv = wkv[:, :, 128:256]
        for h in range(8):
            b = PSA.get()
            for k in range(KC):
                mm(ps[0:64, b * 512:(b + 1) * 512], wq[:, k, h * 64:(h + 1) * 64], hT[:, k, :], k == 0, k == KC - 1,
                   [rhT[k], rq], [psr[b]])
            act(mixT[0:64, 8 + h, :], ps[0:64, b * 512:(b + 1) * 512], AF.Copy, [psr[b]], [rmix[8 + h]])
        for a in range(2):
            b = PSA.get()
            for k in range(KC):
                mm(ps[0:64, b * 512:(b + 1) * 512], wkv[:, k, a * 64:(a + 1) * 64], hT[:, k, :], k == 0, k == KC - 1,
                   [rhT[k], rk], [psr[b]])
            act(kT[l][0:64, a, 128:640], ps[0:64, b * 512:(b + 1) * 512], AF.Copy, [psr[b]], [rkT[l]])
        for t in range(NT):
            ts = slice(t * 128, (t + 1) * 128)
            slot = (g0 + t) % 5
            b = PSA.get()
            for k in range(KC):
                mm(psb(b, 128), hT[:, k, ts], wvv[:, k, :], k == 0, k == KC - 1, [rhT[k], rk], [psr[b]])
            act(vaug[l][:, slot, :, 0:64], psb(b, 128).rearrange("p (a d) -> p a d", a=2), AF.Copy,
                [psr[b]], [rva[l][slot]])
        for t in range(NT):
            ts = slice(t * 128, (t + 1) * 128)
            g = g0 + t
            slot = g % 5
            pslot = (g - 1) % 5
            blocks = [(e_cur, recur, 128 + t * 128, U_b, slot)]
            if g > 0:
                blocks.append((e_prev, reprev, t * 128, SL_b, pslot))
            for (ebuf, reb, kc0, mask, _) in blocks:
                b = PSA.get(2)
                for h in range(8):
                    a = h // 4
                    mm(ps[:, b * 512 + h * 128: b * 512 + (h + 1) * 128], kT[l][0:64, a, kc0:kc0 + 128],
                       mixT[0:64, 8 + h, ts], True, True, [rkT[l], rmix[8 + h]], [psr[b + h // 4]], inc=(h % 4 == 3))
                act(ebuf, ps[:, b * 512:(b + 2) * 512], AF.Exp, [psr[b], psr[b + 1]], [reb], scale=0.125)
                dve("tensor_tensor", [reb, rcb], [reb], out=ebuf.rearrange("p (h q) -> p h q", h=8),
                    in0=ebuf.rearrange("p (h q) -> p h q", h=8), in1=vw(mask, [[0, 8], [1, 128]]), op=ALU.mult)
            bo = PSA.get(2)
            for h in range(8):
                a = h // 4
                dst = ps[:, (bo + h // 4) * 512 + (h % 4) * 128:(bo + h // 4) * 512 + (h % 4) * 128 + 66]
                if g > 0:
                    mm(dst, e_prev[:, h * 128:(h + 1) * 128], vaug[l][:, pslot, a, 0:66], True, False,
                       [reprev, rva[l][pslot]], [psr[bo + h // 4]], inc=False)
                mm(dst, e_cur[:, h * 128:(h + 1) * 128], vaug[l][:, slot, a, 0:66], g == 0, True,
                   [recur, rva[l][slot]], [psr[bo + h // 4]], inc=(h % 4 == 3))
            for i2 in range(2):
                pv = psb(bo + i2, 512).rearrange("p (h c) -> p h c", c=128)
                dve("tensor_tensor", [psr[bo + i2], rsp], [rsm], out=sm[:, 48 + i2 * 4:52 + i2 * 4],
                    in0=pv[:, :, 64], in1=esink[l][:, i2 * 4:(i2 + 1) * 4], op=ALU.add)
            dve("reciprocal", [rsm], [rsm], out=sm[:, 48:56], in_=sm[:, 48:56])
            for i2 in range(2):
                pv = psb(bo + i2, 512).rearrange("p (h c) -> p h c", c=128)
                dve("tensor_tensor", [psr[bo + i2], rsm], [rT[2]],
                    out=T[2][:, i2 * 256:(i2 + 1) * 256].rearrange("p (h d) -> p h d", h=4),
                    in0=pv[:, :, 0:64], in1=vw(sm[:, 48 + i2 * 4:52 + i2 * 4], [[1, 4], [0, 64]]), op=ALU.mult)
            S.op("act", lambda e: e.activation(out=T[3][:, 0:512], in_=T[2][:, 0:512], func=AF.Square,
                                               accum_out=sm[:, 60:61]), [rT[2]], [rT[3], rsm])
            act(sm[:, 61:62], sm[:, 60:61], AF.Ln, [rsm, rcst], [rsm], scale=1.0 / 512, bias=EPS6)
            act(sm[:, 61:62], sm[:, 61:62], AF.Exp, [rsm], [rsm], scale=-0.5)
            act(obf[:, 0:512], T[2][:, 0:512], AF.Copy, [rT[2], rsm], [robf], scale=sm[:, 61:62])
            to_mixT(obf, 4, 4, t, l, 36, robf)
        act(kT[l][0:64, :, 0:128], kT[l][0:64, :, 512:640], AF.Copy, [rkT[l]], [rkT[l]])

    def mixer_c(l, win3):
        wz0, rz0 = load_w(win3[:, :, O_Z:O_Z + 512], 16, 512)
        wz1, rz1 = load_w(win3[:, :, O_Z + 512:O_Z + 1024], 16, 512)
        for t in range(NT):
            ts = slice(t * 128, (t + 1) * 128)
            b = PSA.get(2)
            for k in range(KC):
                mm(psb(b), hT[:, k, ts], wz0[:, k, :], k == 0, k == KC - 1, [rhT[k], rz0], [psr[b]])
            for k in range(KC):
                mm(psb(b + 1), hT[:, k, ts], wz1[:, k, :], k == 0, k == KC - 1, [rhT[k], rz1], [psr[b + 1]])
            act(zs[:, t, :], ps[:, b * 512:(b + 2) * 512], AF.Silu, [psr[b], psr[b + 1]], rzs)
        wx = [None, None, None]

        def xproj(c):
            i3, cc = c // 4, c % 4
            if cc == 0:
                wx[i3] = load_w(win3[:, :, O_X + i3 * 512:O_X + (i3 + 1) * 512], 16, 512)
            w_, rw_ = wx[i3]
            b = PSA.get()
            for k in range(KC):
                mm(psb(b), w_[:, k, cc * 128:(cc + 1) * 128], hT[:, k, :], k == 0, k == KC - 1,
                   [rhT[k], rw_], [psr[b]])
            return b

        bnx = xproj(0)
        for c in range(12):
            b = bnx
            xr = xraw[:, c % 2, :]
            rxr = rxraw[c % 2]
            dve("tensor_copy", [rhist[l]], [rxr], out=xr[:, 0:3], in_=hist[l][:, c, :])
            act(xr[:, 3:515], psb(b), AF.Copy, [psr[b]], [rxr])
            if c + 1 < 12:
                bnx = xproj(c + 1)
            dve("tensor_copy", [rxr], [rhist[l]], out=hist[l][:, c, :], in_=xr[:, 512:515])
            ta = tmps[0]
            dve("tensor_scalar", [rxr, rvT[l]], [rtmp[0]], out=ta[:], in0=xr[:, 0:512],
                scalar1=vT[l][:, 60 + c:61 + c], scalar2=vT[l][:, 48 + c:49 + c], op0=ALU.mult, op1=ALU.add)
            for tap in range(1, 4):
                dve("scalar_tensor_tensor", [rxr, rvT[l], rtmp[0]], [rtmp[0]], out=ta[:], in0=xr[:, tap:tap + 512],
                    scalar=vT[l][:, 60 + tap * 12 + c:61 + tap * 12 + c], in1=ta[:], op0=ALU.mult, op1=ALU.add)
            if c < 8:
                act(obf[:, 0:512], ta[:], AF.Silu, [rtmp[0]], [robf])
                bt = PSA.get()
                pTx = psb(bt).bitcast(BF16)
                for t in range(NT):
                    tr(pTx[:, t * 128:(t + 1) * 128], obf[:, t * 128:(t + 1) * 128], ident_b, [robf, rcb],
                       [psr[bt]], inc=(t == NT - 1))
                act(xs_tok[:, :, c * 128:(c + 1) * 128], pTx[:, 0:512].rearrange("p (t f) -> p t f", t=4), AF.Copy,
                    [psr[bt]], [rxs])
            else:
                act(bcT[:, c - 8, :], ta[:], AF.Silu, [rtmp[0]], [rbc])
        b = PSA.get()
        pT = psb(b).bitcast(BF16)
        for t in range(NT):
            for g in range(2):
                tr(pT[:, (t * 2 + g) * 128:(t * 2 + g + 1) * 128], bcT[:, g, t * 128:(t + 1) * 128], ident_b,
                   [rbc, rcb], [psr[b]], inc=(t == NT - 1 and g == 1))
        act(bm_tok[:].rearrange("p t g n -> p (t g n)"), pT, AF.Copy, [psr[b]], [rbmt])
        i = wslot()
        wdt = wbuf[i][:, 0:256].rearrange("p (k n) -> p k n", k=16)
        dma("pool", wdt, win3[:, :, O_DT:O_DT + 16], [], [rw[i]], f"w{i}")
        for t in range(NT):
            ts = slice(t * 128, (t + 1) * 128)
            b = PSA.get()
            for k in range(KC):
                mm(psb(b, 16), hT[:, k, ts], wdt[:, k, :], k == 0, k == KC - 1, [rhT[k], rw[i]], [psr[b]])
            dve("tensor_tensor", [psr[b], rsp], [rdt], out=dtall[:, t, :], in0=psb(b, 16), in1=dtb[l][:], op=ALU.add)
        act(dtall[:], dtall[:], AF.Exp, [rdt], [rdt])
        act(dtall[:], dtall[:], AF.Ln, [rdt, rcst], [rdt], bias=ONE, scale=1.0)
        dve("tensor_tensor", [rdt, rsp], [rdt], out=daall[:], in0=dtall[:], in1=vw(aneg[l][:], [[0, 4], [1, 16]]),
            op=ALU.mult)
        act(Sbf[:], Sst[l][:], AF.Copy, [rS[l]], [rSbf])
        def P1(t):
            par = t % 2
            ts = slice(t * 128, (t + 1) * 128)
            q0 = 64 + par * 96
            rs = rsmc[par]
            xdt, rxdt, xdt2, rxdt2, M_bf, rM = xdtb[par], rxdtb[par], xdt2b[par], rxdt2b[par], M_bf2[par], rM2[par]
            da = daall[:, t, :]
            dt_ = dtall[:, t, :]
            b = PSA.get()
            mm(psb(b, 16), U_f, da, True, True, [rcf, rdt], [psr[b]], inc=False)
            mm(psb(b, 16, 16), ones_f, da, True, True, [rcf, rdt], [psr[b]], inc=True)
            act(sm[:, q0:q0 + 32], psb(b, 32), AF.Copy, [psr[b]], [rs])
            ea = sm[:, q0 + 32:q0 + 48]
            dsd = sm[:, q0 + 48:q0 + 64]
            cd = sm[:, q0 + 64:q0 + 80]
            w2 = sm[:, q0 + 80:q0 + 96]
            act(ea, sm[:, q0:q0 + 16], AF.Exp, [rs], [rs])
            act(cd, sm[:, q0 + 16:q0 + 32], AF.Exp, [rs], [rs])
            dve("tensor_tensor", [rs], [rs], out=dsd, in0=sm[:, q0 + 16:q0 + 32], in1=sm[:, q0:q0 + 16], op=ALU.subtract)
            act(dsd, dsd, AF.Exp, [rs], [rs])
            dve("tensor_tensor", [rs, rdt], [rs], out=w2, in0=dsd, in1=dt_, op=ALU.mult)
            xs3 = xs_tok[:, t, :].rearrange("p (h d) -> p h d", h=16)
            pool("tensor_tensor", [rxs, rdt], [rxdt], out=xdt[:], in0=xs3, in1=vw(dt_, [[1, 16], [0, 64]]), op=ALU.mult)
            pool("tensor_tensor", [rxs, rs], [rxdt2], out=xdt2[:], in0=xs3, in1=vw(w2, [[1, 16], [0, 64]]), op=ALU.mult)
            b = PSA.get()
            for g in range(2):
                mm(psb(b, 128, g * 128), bcT[:, g, ts], bcT[:, 2 + g, ts], True, True, [rbc], [psr[b]], inc=(g == 1))
            dve("tensor_tensor", [psr[b], rcf], [rcbm], out=cbm[:], in0=psb(b, 256).rearrange("p (g l) -> p g l", g=2),
                in1=vw(U_f, [[0, 2], [1, 128]]), op=ALU.mult)
            yield
            for g in range(2):
                dag = daall[:, t, g * 8:(g + 1) * 8]
                pool("tensor_tensor", [rdt, rcf], [rT[0]], out=T[0].rearrange("p (h l) -> p h l", h=8),
                     in0=vw(dag, [[1, 8], [0, 128]]), in1=vw(U_f, [[0, 8], [1, 128]]), op=ALU.mult)
                pool("tensor_copy", [rdt], [rT[1]], out=T[1].rearrange("p (h l) -> p h l", h=8),
                     in_=vw(dag, [[1, 8], [0, 128]]))
                b = PSA.get(2)
                for hf in range(2):
                    mm(psb(b + hf), ones_f, T[0][:, hf * 512:(hf + 1) * 512], True, False, [rcf, rT[0]], [psr[b + hf]],
                       inc=False)
                    mm(psb(b + hf), negU_f, T[1][:, hf * 512:(hf + 1) * 512], False, True, [rcf, rT[1]], [psr[b + hf]],
                       inc=True)
                yield
                dve("tensor_scalar", [psr[b], psr[b + 1]], [rT[2]], out=T[2], in0=ps[:, b * 512:(b + 2) * 512],
                    scalar1=0.0, scalar2=None, op0=ALU.min)
                act(T[3], T[2], AF.Exp, [rT[2]], [rT[3]])
                dve("tensor_tensor", [rT[3], rcbm], [rM[g]], out=M_bf[:, g, :, :],
                    in0=T[3].rearrange("p (h l) -> p h l", h=8), in1=vw(cbm[:, g, :], [[0, 8], [1, 128]]), op=ALU.mult)
                yield

        def P2(t):
            par = t % 2
            ts = slice(t * 128, (t + 1) * 128)
            q0 = 64 + par * 96
            rs = rsmc[par]
            xdt, rxdt, xdt2, rxdt2, M_bf, rM = xdtb[par], rxdtb[par], xdt2b[par], rxdt2b[par], M_bf2[par], rM2[par]
            ea = sm[:, q0 + 32:q0 + 48]
            cd = sm[:, q0 + 64:q0 + 80]
            xs3 = xs_tok[:, t, :].rearrange("p (h d) -> p h d", h=16)
            b = PSA.get(2)
            for h in range(16):
                mm(ps[:, b * 512 + h * 64: b * 512 + (h + 1) * 64], M_bf[:, h // 8, h % 8, :], xdt[:, h, :], True, True,
                   [rM[h // 8], rxdt], [psr[b + h // 8]], inc=(h % 8 == 7))
            b2 = PSA.get(2)
            for g in range(2):
                mm(psb(b2 + g), bcT[:, 2 + g, ts], Sbf[:, g * 512:(g + 1) * 512], True, True, [rbc, rSbf], [psr[b2 + g]])
            b3 = PSA.get(2)
            for g in range(2):
                mm(psb(b3 + g), bm_tok[:, t, g, :], xdt2[:, g * 8:(g + 1) * 8, :].rearrange("p h d -> p (h d)"), True,
                   True, [rbmt, rxdt2], [psr[b3 + g]])
            yield
            dve("tensor_tensor", [psr[b2], psr[b2 + 1], rs], [rT[4]], out=T[4].rearrange("p (h d) -> p h d", h=16),
                in0=ps[:, b2 * 512:(b2 + 2) * 512].rearrange("p (h d) -> p h d", h=16),
                in1=vw(ea, [[1, 16], [0, 64]]), op=ALU.mult)
            dve("tensor_tensor", [psr[b], psr[b + 1], rT[4]], [rT[4]], out=T[4], in0=ps[:, b * 512:(b + 2) * 512],
                in1=T[4], op=ALU.add)
            pool("tensor_tensor", [rxs, rsp], [rT[5]], out=T[5].rearrange("p (h d) -> p h d", h=16), in0=xs3,
                 in1=vw(dsk[l][:], [[1, 16], [0, 64]]), op=ALU.mult)
            dve("tensor_tensor", [rT[4], rT[5]], [rT[4]], out=T[4], in0=T[4], in1=T[5], op=ALU.add)
            dve("tensor_tensor", [rS[l], rs], [rS[l]], out=Sst[l][:].rearrange("p (h d) -> p h d", h=16),
                in0=Sst[l][:].rearrange("p (h d) -> p h d", h=16), in1=vw(cd, [[1, 16], [0, 64]]), op=ALU.mult)
            dve("tensor_tensor", [psr[b3], psr[b3 + 1], rS[l]], [rS[l]], out=Sst[l][:], in0=ps[:, b3 * 512:(b3 + 2) * 512],
                in1=Sst[l][:], op=ALU.add)
            act(Sbf[:], Sst[l][:], AF.Copy, [rS[l]], [rSbf])
            yield
            dve("tensor_tensor", [rT[4]] + rzs, [rT[5]], out=T[5], in0=T[4], in1=zs[:, t, :], op=ALU.mult)
            for g in range(2):
                S.op("act", lambda e, g=g: e.activation(out=obf[:, g * 512:(g + 1) * 512],
                                                        in_=T[5][:, g * 512:(g + 1) * 512], func=AF.Square,
                                                        accum_out=sm[:, 44 + g:45 + g]), [rT[5]], [robf, rsmg])
            act(sm[:, 46:48], sm[:, 44:46], AF.Ln, [rsmg, rcst], [rsmg], scale=1.0 / 512, bias=EPS6)
            act(sm[:, 46:48], sm[:, 46:48], AF.Exp, [rsmg], [rsmg], scale=-0.5)
            for g in range(2):
                act(obf[:, g * 512:(g + 1) * 512], T[5][:, g * 512:(g + 1) * 512], AF.Copy, [rT[5], rsmg], [robf],
                    scale=sm[:, 46 + g:47 + g])
            yield
            to_mixT(obf, 8, 8, t, l, 40, robf)
            yield

        def drain(*gens):
            gens = [g for g in gens if g is not None]
            while gens:
                for g in list(gens):
                    try:
                        next(g)
                    except StopIteration:
                        gens.remove(g)

        drain(P1(0))
        for t in range(NT):
            drain(P1(t + 1) if t + 1 < NT else None, P2(t))

    def out_proj(l, gate):
        wo3 = prm["w_out"][l].rearrange("(k p) n -> p k n", p=128)
        for gi in range(4):
            wo, ro = load_w(wo3[:, :, gi * 512:(gi + 1) * 512], 16, 512)
            for mi in range(4):
                m = gi * 4 + mi
                b = PSA.get()
                for k in range(KC):
                    mm(psb(b), wo[:, k, mi * 128:(mi + 1) * 128], mixT[:, k, :], k == 0, k == KC - 1, [rmix[k], ro],
                       [psr[b]])
                dve("scalar_tensor_tensor", [psr[b], rxT[m], rmod], [rxT[m]], out=xT[:, m, :], in0=psb(b),
                    scalar=gate[:, m:m + 1], in1=xT[:, m, :], op0=ALU.mult, op1=ALU.add)

    def mlp(l, gate):
        w13 = prm["w_mlp1"][l].rearrange("(k p) n -> p k n", p=128)
        for f in range(16):
            w1, r1 = load_w(w13[:, :, f * 512:(f + 1) * 512], 16, 512)
            w2, r2 = load_w(prm["w_mlp2"][l][f * 512:(f + 1) * 512, :].rearrange("(c p) n -> p c n", p=128), 4, 2048)
            hb = (f % 2) * 4
            for cc in range(4):
                b = PSA.get()
                for k in range(KC):
                    mm(psb(b), w1[:, k, cc * 128:(cc + 1) * 128], hT[:, k, :], k == 0, k == KC - 1, [rhT[k], r1],
                       [psr[b]])
                act(tmps[cc % 2][:], psb(b), AF.Relu, [psr[b]], [rtmp[cc % 2]])
                dve("tensor_tensor", [rtmp[cc % 2]], [rmix[hb + cc]], out=mixT[:, hb + cc, :], in0=tmps[cc % 2][:],
                    in1=tmps[cc % 2][:], op=ALU.mult)
            for m in range(KC):
                b = PSA.get()
                for cc in range(4):
                    mm(psb(b), w2[:, cc, m * 128:(m + 1) * 128], mixT[:, hb + cc, :], cc == 0, cc == 3,
                       [rmix[hb + cc], r2], [psr[b]])
                dve("scalar_tensor_tensor", [psr[b], rxT[m], rmod], [rxT[m]], out=xT[:, m, :], in0=psb(b),
                    scalar=gate[:, m:m + 1], in1=xT[:, m, :], op0=ALU.mult, op1=ALU.add)

    for s in range(nseq):
        for l in range(2):
            dve("memset", [], [rS[l]], ap=Sst[l][:], constant=0.0)
            dve("memset", [], [rhist[l]], ap=hist[l][:], constant=0.0)
        for blk in range(nblk):
            tok0 = blk * TB
            for t in range(NT):
                dma("sp", xin, x[s, tok0 + t * 128: tok0 + (t + 1) * 128, :], [], rxin, "xin")
                for j in range(4):
                    b = PSA.get()
                    for i4 in range(4):
                        k = j * 4 + i4
                        tr(psb(b, 128, i4 * 128), xin[:, k * 128:(k + 1) * 128], ident_f, rxin + [rcf], [psr[b]],
                           inc=(i4 == 3))
                    act(xT[:, j * 4:(j + 1) * 4, t * 128:(t + 1) * 128], psb(b).rearrange("p (i f) -> p i f", i=4),
                        AF.Copy, [psr[b]], [rxT[j * 4 + i4] for i4 in range(4)])
            for l in range(nlayer):
                i = l * 2 + s
                WS.l = l
                WS.ti = 0
                WS.first = (s == 0 and blk == 0)
                mo = modT[:, i * 96:(i + 1) * 96]
                win3 = prm["w_in"][l].rearrange("(k p) n -> p k n", p=128)
                dma("sp", lnG[:], prm["gm_ln_g"][l].rearrange("h e -> (h e)").partition_broadcast(128), [], [rln], "ln")
                dma("sp", lnB[:], prm["gm_ln_b"][l].rearrange("h e -> (h e)").partition_broadcast(128), [], [rln], "ln")
                norm_to_hT(gsc[:, i * 32:i * 32 + 16], mo[:, 0:16])
                if do_a:
                    mixer_a(l, win3)
                else:
                    for k in range(0, 4):
                        dve("memset", [], [rmix[k]], ap=mixT[:, k, :], constant=0.0)
                if do_b:
                    mixer_b(l, win3, blk * NT)
                else:
                    for k in range(4, 8):
                        dve("memset", [], [rmix[k]], ap=mixT[:, k, :], constant=0.0)
                if do_c:
                    mixer_c(l, win3)
                else:
                    for k in range(8, 16):
                        dve("memset", [], [rmix[k]], ap=mixT[:, k, :], constant=0.0)
                out_proj(l, mo[:, 32:48])
                if do_mlp:
                    norm_to_hT(gsc[:, i * 32 + 16:i * 32 + 32], mo[:, 48:64])
                    mlp(l, mo[:, 80:96])
            dma("sp", fgb, prm["final_norm_g"].partition_broadcast(128), [], rfgb, "fgb")
            for t in range(NT):
                b = PSA.get(4)
                for k in range(KC):
                    tr(ps[:, b * 512 + k * 128: b * 512 + (k + 1) * 128], xT[:, k, t * 128:(t + 1) * 128], ident_f,
                       [rxT[k], rcf], [psr[b + k // 4]], inc=(k % 4 == 3))
                pr4 = [psr[b + q] for q in range(4)]
                S.op("act", lambda e, b=b: e.activation(out=xin, in_=ps[:, b * 512:(b + 4) * 512], func=AF.Square,
                                                        accum_out=sm[:, 36:37]), pr4, rxin + [rsmf])
                act(sm[:, 37:38], sm[:, 36:37], AF.Sqrt, [rsmf, rcst], [rsmf], scale=1.0 / D, bias=EPS6)
                dve("reciprocal", [rsmf], [rsmf], out=sm[:, 37:38], in_=sm[:, 37:38])
                dve("scalar_tensor_tensor", pr4 + [rsmf] + rfgb, rxin, out=xin, in0=ps[:, b * 512:(b + 4) * 512],
                    scalar=sm[:, 37:38], in1=fgb, op0=ALU.mult, op1=ALU.mult)
                dma("sp", out[s, tok0 + t * 128: tok0 + (t + 1) * 128, :], xin, rxin, [], "xin")

    with nc.allow_non_contiguous_dma(reason="small strided parameter loads"):
        S.emit(final_waits=["xin"])
    es.close()
    return nc


def make_consts():
    c = np.zeros((128, 640), np.float32)
    i = np.arange(128)
    c[:, C_ID:C_ID + 128] = np.eye(128, dtype=np.float32)
    u = (i[:, None] <= i[None, :]).astype(np.float32)
    c[:, C_U:C_U + 128] = u
    c[:, C_NU:C_NU + 128] = -u
    c[:, C_SL:C_SL + 128] = 1.0 - u
    c[:, C_ONE:C_ONE + 128] = 1.0
    return c


_NC_CACHE = {}


def kernel(**inputs):
    n = 8
    key = "full"
    if key not in _NC_CACHE:
        _NC_CACHE[key] = build()
    nc = _NC_CACHE[key]
    x = np.ascontiguousarray(inputs["x"], dtype=np.float32)
    c = np.ascontiguousarray(inputs["c"], dtype=np.float32)
    consts = make_consts()
    params = {k: np.ascontiguousarray(inputs[k], dtype=np.float32) for k in PARAM_NAMES}
    in_maps = []
    for i in range(n):
        m = {"x": x[2 * i:2 * i + 2], "c": c[2 * i:2 * i + 2], "consts": consts}
        m.update(params)
        in_maps.append(m)
    res = run_bass_kernel_spmd(nc, in_maps, core_ids=list(range(n)))
    return np.concatenate([r["out"] for r in res.results], axis=0)
```

```python
import contextlib
import numpy as np
import concourse.bass as bass
import concourse.mybir as mybir
from concourse.bass_utils import run_bass_kernel_spmd

F32 = mybir.dt.float32
BF16 = mybir.dt.bfloat16
AF = mybir.ActivationFunctionType
ALU = mybir.AluOpType

ENGS = ("pe", "act", "dve", "pool", "sp")


class Reg:
    __slots__ = ("name", "w", "rd")

    def __init__(self, name):
        self.name = name
        self.w = None
        self.rd = {}


class Rec:
    __slots__ = ("eng", "fn", "inc", "deps", "dma_sem", "cnt", "idx", "dma_waits")

    def __init__(self, eng, fn, inc, dma_sem):
        self.eng = eng
        self.fn = fn
        self.inc = inc
        self.deps = []
        self.dma_waits = {}
        self.dma_sem = dma_sem
        self.cnt = None
        self.idx = None


class Sched:
    def __init__(self, nc):
        self.nc = nc
        self.recs = {e: [] for e in ENGS}
        self.dma_cnt = {}
        self.n = 0

    def _dep(self, w, d):
        if d is None or d is w:
            return
        if d.dma_sem is not None:
            k = d.dma_sem
            w.dma_waits[k] = max(w.dma_waits.get(k, 0), self.dma_cnt[k])
            return
        if d.eng == "pe" and w.eng == "pe" and w.dma_sem is None:
            return
        if not d.inc:
            lst = self.recs[d.eng]
            tgt = None
            for j in range(d.idx + 1, len(lst)):
                if lst[j].inc and lst[j].dma_sem is None:
                    tgt = lst[j]
                    break
            if tgt is None:
                d.inc = True
                tgt = d
            d = tgt
        w.deps.append(d)

    def op(self, eng, fn, reads=(), writes=(), inc=True, dma=None):
        r = Rec(eng, fn, inc if dma is None else False, dma)
        r.idx = len(self.recs[eng])
        for x in reads:
            self._dep(r, x.w)
        for x in writes:
            self._dep(r, x.w)
            for rr in x.rd.values():
                self._dep(r, rr)
        if dma is not None:
            self.dma_cnt[dma] = self.dma_cnt.get(dma, 0) + 16
            r.cnt = self.dma_cnt[dma]
        self.recs[eng].append(r)
        for x in writes:
            x.w = r
            x.rd = {}
        for x in reads:
            x.rd[eng if dma is None else (dma, r.cnt)] = r
        self.n += 1
        return r

    def emit(self, final_waits=()):
        nc = self.nc
        with contextlib.ExitStack() as es:
            sems = {}
            for e in ENGS:
                sems[e] = es.enter_context(nc.semaphore("s_" + e))
            for k in self.dma_cnt:
                sems[("dma", k)] = es.enter_context(nc.semaphore("d_" + k))
            for e in ENGS:
                c = 0
                for r in self.recs[e]:
                    if r.dma_sem is None and r.inc:
                        c += 1
                        r.cnt = c
            block = es.enter_context(nc.Block())

            def run(e, eng):
                waited = {}
                for r in self.recs[e]:
                    need = {}
                    for d in r.deps:
                        need[d.eng] = max(need.get(d.eng, 0), d.cnt)
                    for k, v in r.dma_waits.items():
                        need[("dma", k)] = max(need.get(("dma", k), 0), v)
                    for k, v in need.items():
                        if waited.get(k, 0) < v:
                            eng.wait_ge(sems[k], v)
                            waited[k] = v
                    ins = r.fn(eng)
                    if r.dma_sem is not None:
                        ins.then_inc(sems[("dma", r.dma_sem)], 16)
                    elif r.inc:
                        ins.then_inc(sems[e], 1)
                if e == "sp":
                    for k in final_waits:
                        eng.wait_ge(sems[("dma", k)], self.dma_cnt[k])

            @block.tensor
            def _(eng):
                run("pe", eng)

            @block.scalar
            def _(eng):
                run("act", eng)

            @block.vector
            def _(eng):
                run("dve", eng)

            @block.gpsimd
            def _(eng):
                run("pool", eng)

            @block.sync
            def _(eng):
                run("sp", eng)


def vw(ap, dims):
    return bass.AP(ap.tensor, ap.offset, [list(ap.ap[0])] + [list(d) for d in dims])


D = 2048
KC = 16
TB = 512
NT = 4
SEQ = 2048
NW = 2
O_U, O_V, O_Q, O_K, O_VV, O_Z, O_X, O_DT = 0, 512, 1024, 1536, 1664, 1792, 2816, 4352
C_ID, C_U, C_NU, C_SL, C_ONE = 0, 128, 256, 384, 512

PARAM_NAMES = ["ada_w", "ada_b", "norm1_g", "w_in", "gm_ln_g", "gm_ln_b", "gm_ws", "gm_bs",
               "gm_norm_g", "attn_sinks", "attn_norm_g", "conv_w", "conv_b", "dt_bias", "a_log",
               "d_skip", "ssm_norm_g", "w_out", "norm2_g", "w_mlp1", "w_mlp2", "final_norm_g"]
PARAM_SHAPES = {
    "ada_w": [2, 2048, 12288], "ada_b": [2, 12288], "norm1_g": [2, 2048], "w_in": [2, 2048, 4368],
    "gm_ln_g": [2, 4, 128], "gm_ln_b": [2, 4, 128], "gm_ws": [2, 4, 128, 128], "gm_bs": [2, 4, 128],
    "gm_norm_g": [2, 512], "attn_sinks": [2, 8], "attn_norm_g": [2, 512], "conv_w": [2, 4, 1536],
    "conv_b": [2, 1536], "dt_bias": [2, 16], "a_log": [2, 16], "d_skip": [2, 16],
    "ssm_norm_g": [2, 1024], "w_out": [2, 2048, 2048], "norm2_g": [2, 2048],
    "w_mlp1": [2, 2048, 8192], "w_mlp2": [2, 8192, 2048], "final_norm_g": [2048],
}


def slim_shapes(nlayer, do_mlp):
    sh = {k: list(v) for k, v in PARAM_SHAPES.items()}
    for k in ("ada_w", "w_in", "w_out", "w_mlp1", "w_mlp2"):
        sh[k][0] = nlayer
    if not do_mlp:
        sh["w_mlp1"] = [1, 128, 128]
        sh["w_mlp2"] = [1, 128, 128]
    return sh


def build(nseq=2, nblk=4, nlayer=2, do_a=True, do_b=True, do_c=True, do_mlp=True, slim=False):
    nc = bass.Bass("TRN2", target_bir_lowering=False)
    din = lambda n, s: nc.dram_tensor(n, s, F32, kind="ExternalInput").ap()
    x = din("x", [2, SEQ, D])
    c_in = din("c", [2, D])
    shp = slim_shapes(nlayer, do_mlp) if slim else PARAM_SHAPES
    prm = {n: din(n, shp[n]) for n in PARAM_NAMES}
    consts = din("consts", [128, 640])
    out = nc.dram_tensor("out", [2, SEQ, D], F32, kind="ExternalOutput").ap()
    mod_d = nc.dram_tensor("mod_d", [2, 2, 12288], F32).ap()
    NTILE = 48
    wscr = nc.dram_tensor("wscr", [2, NTILE, 128, 8192], BF16).ap()

    es = contextlib.ExitStack()
    S = Sched(nc)

    def sb(name, shape, dt=F32):
        return es.enter_context(nc.sbuf_tensor(name, shape, dt))

    xT = sb("xT", [128, KC, TB]); rxT = [Reg(f"xT{k}") for k in range(KC)]
    hT = sb("hT", [128, KC, TB], BF16); rhT = [Reg(f"hT{k}") for k in range(KC)]
    mixT = sb("mixT", [128, KC, TB], BF16); rmix = [Reg(f"mix{k}") for k in range(KC)]
    wbuf = [sb(f"wbuf{i}", [128, 8192], BF16) for i in range(NW)]
    rw = [Reg(f"w{i}") for i in range(NW)]
    TT = sb("TT", [128, 6 * 1024]); rT = [Reg(f"T{i}") for i in range(6)]
    T = [TT[:, i * 1024:(i + 1) * 1024] for i in range(6)]
    xin = TT[:, 2048:4096]; rxin = [rT[2], rT[3]]
    fgb = TT[:, 0:2048]; rfgb = [rT[0], rT[1]]
    cf = sb("cf", [128, 640]); rcf = Reg("cf")
    cb = sb("cb", [128, 640], BF16); rcb = Reg("cb")
    cst = sb("cst", [128, 4]); rcst = Reg("cst")
    vrow = sb("vrow", [128, 128]); rvrow = Reg("vrow")
    vT = [sb(f"vT{l}", [128, 112]) for l in range(2)]; rvT = [Reg(f"vT{l}") for l in range(2)]
    vTf = sb("vTf", [128, 16]); rvTf = Reg("vTf")
    modT = sb("modT", [128, 4 * 96]); rmod = Reg("modT")
    gsc = sb("gsc", [128, 4 * 32]); rgsc = Reg("gsc")
    lnG = sb("lnG", [128, 512]); lnB = sb("lnB", [128, 512]); rln = Reg("ln")
    WsT = [sb(f"WsT{l}", [128, 4, 128], BF16) for l in range(2)]; rws = [Reg(f"ws{l}") for l in range(2)]
    esink = [sb(f"esink{l}", [128, 8]) for l in range(2)]
    dtb = [sb(f"dtb{l}", [128, 16]) for l in range(2)]
    aneg = [sb(f"aneg{l}", [128, 16]) for l in range(2)]
    dsk = [sb(f"dsk{l}", [128, 16]) for l in range(2)]
    rsp = Reg("smallparams")
    rstd_b = sb("rstd_b", [128, TB]); rrstd = Reg("rstd")
    tmps = [sb(f"tmp{i}", [128, TB]) for i in range(2)]; rtmp = [Reg(f"tmp{i}") for i in range(2)]
    AB = sb("AB", [128, 4096], BF16)
    qT = AB[:, 0:2048].rearrange("p (j t) -> p j t", j=4); rqT = Reg("qT")
    e_cur = AB[:, 2048:3072]; recur = Reg("ecur")
    e_prev = AB[:, 3072:4096]; reprev = Reg("eprev")
    zs = AB[:, :].rearrange("p (t f) -> p t f", t=4); rzs = [rqT, recur, reprev]
    kT = [sb(f"kT{l}", [128, 2, 640], BF16) for l in range(2)]; rkT = [Reg(f"kT{l}") for l in range(2)]
    vaug = [sb(f"vaug{l}", [128, 5, 2, 72], BF16) for l in range(2)]
    rva = [[Reg(f"va{l}_{i}") for i in range(5)] for l in range(2)]
    xs_tok = sb("xs_tok", [128, 4, 1024], BF16); rxs = Reg("xs_tok")
    bcT = sb("bcT", [128, 4, TB], BF16); rbc = Reg("bcT")
    bm_tok = sb("bm_tok", [128, 4, 2, 128], BF16); rbmt = Reg("bm_tok")
    xraw = sb("xraw", [128, 2, 515]); rxraw = [Reg("xraw0"), Reg("xraw1")]
    hist = [sb(f"hist{l}", [128, 12, 3]) for l in range(2)]; rhist = [Reg(f"hist{l}") for l in range(2)]
    Sst = [sb(f"Sst{l}", [128, 1024]) for l in range(2)]; rS = [Reg(f"S{l}") for l in range(2)]
    Sbf = sb("Sbf", [128, 1024], BF16); rSbf = Reg("Sbf")
    M_bf2 = [sb(f"M_bf{i}", [128, 2, 8, 128], BF16) for i in range(2)]; rM2 = [[Reg(f"M{i}{g}") for g in range(2)] for i in range(2)]
    xdtb = [sb(f"xdt{i}", [128, 16, 64], BF16) for i in range(2)]; rxdtb = [Reg(f"xdt{i}") for i in range(2)]
    xdt2b = [sb(f"xdt2{i}", [128, 16, 64], BF16) for i in range(2)]; rxdt2b = [Reg(f"xdt2{i}") for i in range(2)]
    vn = sb("vn", [128, 512], BF16); rvn = Reg("vn")
    obf = sb("obf", [128, 1024], BF16); robf = Reg("obf")
    sm = sb("sm", [128, 256]); rsm = Reg("sm")
    rsmc = [Reg("smc0"), Reg("smc1")]; rsmg = Reg("smg"); rsmf = Reg("smf")
    dtall = sb("dtall", [128, 4, 16]); daall = sb("daall", [128, 4, 16]); rdt = Reg("dt")
    cbm = sb("cbm", [128, 2, 128]); rcbm = Reg("cbm")
    cTf = sb("cTf", [128, KC, 2]); cTb = sb("cTb", [128, KC, 2], BF16); rcT = Reg("cT")
    abt = TT[0:2, 4096:4608]; rabt = rT[4]
    mrow = TT[0:2, 5120:5632]; rmrow = rT[5]

    ps = es.enter_context(nc.psum_tensor("ps", [128, 4096], F32))
    psr = [Reg(f"ps{b}") for b in range(8)]

    class PSA:
        i = 0

        @staticmethod
        def get(n=1):
            while PSA.i % n:
                PSA.i += 1
            b = PSA.i % 8
            PSA.i += n
            return b

    def psb(b, n=512, off=0):
        return ps[:, b * 512 + off: b * 512 + off + n]

    def mm(o, lhsT, rhs, start, stop, rd, wr, inc=None):
        S.op("pe", lambda e: e.matmul(o, lhsT, rhs, start=start, stop=stop), rd, wr,
             inc=(stop if inc is None else inc))

    def tr(o, i, ident, rd, wr, inc=True):
        S.op("pe", lambda e: e.transpose(o, i, ident), rd, wr, inc=inc)

    def act(o, i, func, rd, wr, **kw):
        S.op("act", lambda e: e.activation(out=o, in_=i, func=func, **kw), rd, wr)

    def dve(meth, rd, wr, **kw):
        S.op("dve", lambda e: getattr(e, meth)(**kw), rd, wr)

    def pool(meth, rd, wr, **kw):
        S.op("pool", lambda e: getattr(e, meth)(**kw), rd, wr)

    def dma(q, o, i, rd, wr, key):
        S.op(q, lambda e: e.dma_start(out=o, in_=i), rd, wr, dma=key)

    class WL:
        i = 0

    def wslot():
        i = WL.i % NW
        WL.i += 1
        return i

    class WS:
        first = True
        l = 0
        ti = 0
    rwscr = [Reg("wscr0"), Reg("wscr1")]

    def load_w(src3, k, n, cache=True):
        i = wslot()
        v = wbuf[i][:, 0:k * n].rearrange("p (k n) -> p k n", k=k)
        if not cache:
            dma("pool", v, src3, [], [rw[i]], f"w{i}")
            return v, rw[i]
        l_, ti = WS.l, WS.ti
        WS.ti += 1
        flat = wbuf[i][:, 0:k * n]
        if WS.first:
            dma("pool", v, src3, [], [rw[i]], f"w{i}")
            dma("sp", wscr[l_, ti, :, 0:k * n], flat, [rw[i]], [rwscr[l_]], f"ws{l_}")
        else:
            dma("sp", flat, wscr[l_, ti, :, 0:k * n], [rwscr[l_]], [rw[i]], f"wh{i}")
        return v, rw[i]

    ident_f = cf[:, C_ID:C_ID + 128]
    U_f = cf[:, C_U:C_U + 128]
    negU_f = cf[:, C_NU:C_NU + 128]
    ones_f = cf[:, C_ONE:C_ONE + 128]
    ident_b = cb[:, C_ID:C_ID + 128]
    U_b = cb[:, C_U:C_U + 128]
    SL_b = cb[:, C_SL:C_SL + 128]
    ones_b = cb[:, C_ONE:C_ONE + 128]
    EPS6 = cst[:, 0:1]
    EPS5 = cst[:, 1:2]
    ONE = cst[:, 2:3]

    dma("sp", cf[:], consts, [], [rcf], "const")
    dma("pool", cb[:], consts, [], [rcb], "constb")
    dve("memset", [], [rcst], ap=cst[:, 0:1], constant=1e-6)
    dve("memset", [], [rcst], ap=cst[:, 1:2], constant=1e-5)
    dve("memset", [], [rcst], ap=cst[:, 2:3], constant=1.0)
    dve("memset", [], [rcst], ap=cst[:, 3:4], constant=0.0)
    for l in range(2):
        for i in range(5):
            dve("memset", [], [rva[l][i]], ap=vaug[l][:, i, :, 64:72], constant=0.0)
            dve("memset", [], [rva[l][i]], ap=vaug[l][:, i, :, 64:65], constant=1.0)

    def vec_rows(l):
        rows = []
        rows.append((0, 16, prm["norm1_g"][l].rearrange("(k p) -> k p", p=128)))
        rows.append((16, 16, prm["norm2_g"][l].rearrange("(k p) -> k p", p=128)))
        rows.append((32, 4, prm["gm_norm_g"][l].rearrange("(k p) -> k p", p=128)))
        rows.append((36, 4, prm["attn_norm_g"][l].rearrange("(k p) -> k p", p=128)))
        rows.append((40, 8, prm["ssm_norm_g"][l].rearrange("(k p) -> k p", p=128)))
        rows.append((48, 12, prm["conv_b"][l].rearrange("(k p) -> k p", p=128)))
        rows.append((60, 48, prm["conv_w"][l].rearrange("t (k p) -> (t k) p", p=128)))
        rows.append((108, 4, prm["gm_bs"][l]))
        return rows

    for l in range(2):
        for (r0, n, src) in vec_rows(l):
            dma("sp", vrow[r0:r0 + n, :], src, [], [rvrow], "vrow")
        b = PSA.get()
        tr(psb(b, 112), vrow[0:112, :], ident_f[0:112, 0:112], [rvrow, rcf], [psr[b]])
        act(vT[l][:], psb(b, 112), AF.Copy, [psr[b]], [rvT[l]])
    dma("sp", vrow[0:16, :], prm["final_norm_g"].rearrange("(k p) -> k p", p=128), [], [rvrow], "vrow")
    b = PSA.get()
    tr(psb(b, 16), vrow[0:16, :], ident_f[0:16, 0:16], [rvrow, rcf], [psr[b]])
    act(vTf[:], psb(b, 16), AF.Copy, [psr[b]], [rvTf])

    for l in range(2):
        dma("sp", esink[l][:], prm["attn_sinks"][l].partition_broadcast(128), [], [rsp], "sp")
        dma("sp", dtb[l][:], prm["dt_bias"][l].partition_broadcast(128), [], [rsp], "sp")
        dma("sp", aneg[l][:], prm["a_log"][l].partition_broadcast(128), [], [rsp], "sp")
        dma("sp", dsk[l][:], prm["d_skip"][l].partition_broadcast(128), [], [rsp], "sp")
    for l in range(2):
        act(esink[l][:], esink[l][:], AF.Exp, [rsp], [rsp])
        act(aneg[l][:], aneg[l][:], AF.Exp, [rsp], [rsp])
        dve("tensor_scalar", [rsp], [rsp], out=aneg[l][:], in0=aneg[l][:], scalar1=-1.0, scalar2=None,
            op0=ALU.mult)
    for l in range(2):
        wsf = T[0].rearrange("p (h s) -> p h s", h=8)[:, 0:4, :]
        dma("sp", wsf, prm["gm_ws"][l].rearrange("h t s -> t h s"), [], [rT[0]], "wsf")
        b = PSA.get()
        for h in range(4):
            tr(psb(b, 128, h * 128), wsf[:, h, :], ident_f, [rT[0], rcf], [psr[b]], inc=(h == 3))
        dve("tensor_tensor", [psr[b], rcf], [rws[l]], out=WsT[l][:],
            in0=psb(b).rearrange("p (h t) -> p h t", h=4), in1=vw(U_f, [[0, 4], [1, 128]]), op=ALU.mult)

    for s_ in range(2):
        dma("sp", cTf[:, :, s_], c_in[s_].rearrange("(k p) -> p k", p=128), [], [rcT], "cT")
    act(cTb[:], cTf[:], AF.Silu, [rcT], [rcT])
    for l in range(nlayer):
        aw3 = prm["ada_w"][l].rearrange("(k p) n -> p k n", p=128)
        for n in range(24):
            w, rwi = load_w(aw3[:, :, n * 512:(n + 1) * 512], 16, 512, cache=False)
            dma("sp", abt, prm["ada_b"][l][n * 512:(n + 1) * 512].partition_broadcast(2), [], [rabt], "abt")
            b = PSA.get()
            for k in range(KC):
                mm(ps[0:2, b * 512:(b + 1) * 512], cTb[:, k, :], w[:, k, :], k == 0, k == KC - 1,
                   [rcT, rwi], [psr[b]])
            dve("tensor_tensor", [psr[b], rabt], [rmrow], out=mrow, in0=ps[0:2, b * 512:(b + 1) * 512],
                in1=abt, op=ALU.add)
            dma("sp", mod_d[l, :, n * 512:(n + 1) * 512], mrow, [rmrow], [], "mrow")
    rmodd = Reg("mod_d")
    for l in range(nlayer):
        for s in range(2):
            i = l * 2 + s
            S.op("sp", lambda e, l=l, s=s: e.dma_start(out=vrow[0:96, :],
                                                       in_=mod_d[l, s].rearrange("(r p) -> r p", p=128)),
                 [], [rmrow, rvrow], dma="mrow")
            b = PSA.get()
            tr(psb(b, 96), vrow[0:96, :], ident_f[0:96, 0:96], [rvrow, rcf], [psr[b]])
            act(modT[:, i * 96:(i + 1) * 96], psb(b, 96), AF.Copy, [psr[b]], [rmod])
            dve("scalar_tensor_tensor", [rmod, rvT[l]], [rgsc], out=gsc[:, i * 32:i * 32 + 16],
                in0=modT[:, i * 96 + 16:i * 96 + 32], scalar=1.0, in1=vT[l][:, 0:16], op0=ALU.add, op1=ALU.mult)
            dve("scalar_tensor_tensor", [rmod, rvT[l]], [rgsc], out=gsc[:, i * 32 + 16:i * 32 + 32],
                in0=modT[:, i * 96 + 64:i * 96 + 80], scalar=1.0, in1=vT[l][:, 16:32], op0=ALU.add, op1=ALU.mult)

    def norm_to_hT(gs, sh):
        for k in range(KC):
            act(hT[:, k, :], xT[:, k, :], AF.Square, [rxT[k]], [rhT[k]])
        b = PSA.get()
        for k in range(KC):
            mm(psb(b), ones_b, hT[:, k, :], k == 0, k == KC - 1, [rhT[k], rcb], [psr[b]])
        act(rstd_b[:], psb(b), AF.Sqrt, [psr[b], rcst], [rrstd], scale=1.0 / D, bias=EPS6)
        dve("reciprocal", [rrstd], [rrstd], out=rstd_b[:], in_=rstd_b[:])
        for k in range(KC):
            t_ = tmps[k % 2]
            dve("scalar_tensor_tensor", [rxT[k], rrstd, rgsc], [rtmp[k % 2]], out=t_[:], in0=xT[:, k, :],
                scalar=gs[:, k:k + 1], in1=rstd_b[:], op0=ALU.mult, op1=ALU.mult)
            act(hT[:, k, :], t_[:], AF.Identity, [rtmp[k % 2], rmod], [rhT[k]], bias=sh[:, k:k + 1], scale=1.0)

    def rms_scale(src, n, ss_col):
        act(T[3][:, 0:n], src, AF.Square, [rsm], [rT[3], rsm], accum_out=sm[:, ss_col:ss_col + 1])
        act(sm[:, ss_col + 1:ss_col + 2], sm[:, ss_col:ss_col + 1], AF.Sqrt, [rsm, rcst], [rsm],
            scale=1.0 / n, bias=EPS6)
        dve("reciprocal", [rsm], [rsm], out=sm[:, ss_col + 1:ss_col + 2], in_=sm[:, ss_col + 1:ss_col + 2])

    def to_mixT(src_bf, nchunk, k0, t, l, gcol, rsrc):
        b = PSA.get()
        pT = psb(b).bitcast(BF16)
        for j in range(nchunk):
            tr(pT[:, j * 128:(j + 1) * 128], src_bf[:, j * 128:(j + 1) * 128], ident_b, [rsrc, rcb], [psr[b]],
               inc=(j == nchunk - 1))
        for j in range(nchunk):
            act(mixT[:, k0 + j, t * 128:(t + 1) * 128], pT[:, j * 128:(j + 1) * 128], AF.Copy,
                [psr[b], rvT[l]], [rmix[k0 + j]], scale=vT[l][:, gcol + j:gcol + j + 1])

    def mixer_a(l, win3):
        wu, ru = load_w(win3[:, :, O_U:O_U + 512], 16, 512)
        wv, rv = load_w(win3[:, :, O_V:O_V + 512], 16, 512)
        def proj(t):
            ts = slice(t * 128, (t + 1) * 128)
            b = PSA.get(2)
            for k in range(KC):
                mm(psb(b), hT[:, k, ts], wu[:, k, :], k == 0, k == KC - 1, [rhT[k], ru], [psr[b]])
            for k in range(KC):
                mm(psb(b + 1), hT[:, k, ts], wv[:, k, :], k == 0, k == KC - 1, [rhT[k], rv], [psr[b + 1]])
            return b

        bnext = proj(0)
        for t in range(NT):
            b = bnext
            if t + 1 < NT:
                bnext = proj(t + 1)
            act(T[0], ps[:, b * 512:(b + 2) * 512], AF.Gelu_apprx_tanh, [psr[b], psr[b + 1]], [rT[0]])
            ug = T[0][:, 0:512]
            vg = T[0][:, 512:1024]
            st6 = sm[:, 0:24].rearrange("p (h s) -> p h s", h=4)
            mv = sm[:, 24:32].rearrange("p (h s) -> p h s", h=4)
            for h in range(4):
                dve("bn_stats", [rT[0]], [rsm], out=st6[:, h, :], in_=vg[:, h * 128:(h + 1) * 128])
            for h in range(4):
                dve("bn_aggr", [rsm], [rsm], out=mv[:, h, :], in_=st6[:, h, :])
            act(sm[:, 32:36], mv[:, :, 1], AF.Ln, [rsm, rcst], [rsm], bias=EPS5, scale=1.0)
            act(sm[:, 32:36], sm[:, 32:36], AF.Exp, [rsm], [rsm], scale=-0.5)
            for h in range(4):
                dve("tensor_scalar", [rT[0], rsm], [rT[1]], out=T[1][:, h * 128:(h + 1) * 128],
                    in0=vg[:, h * 128:(h + 1) * 128], scalar1=mv[:, h, 0:1], scalar2=sm[:, 32 + h:33 + h],
                    op0=ALU.subtract, op1=ALU.mult)
            dve("tensor_tensor", [rT[1], rln], [rT[1]], out=T[1][:, 0:512], in0=T[1][:, 0:512], in1=lnG[:],
                op=ALU.mult)
            dve("tensor_tensor", [rT[1], rln], [rvn], out=vn[:], in0=T[1][:, 0:512], in1=lnB[:], op=ALU.add)
            b2 = PSA.get()
            for h in range(4):
                mm(psb(b2, 128, h * 128), WsT[l][:, h, :], vn[:, h * 128:(h + 1) * 128], True, True,
                   [rws[l], rvn], [psr[b2]], inc=(h == 3))
            for h in range(4):
                dve("scalar_tensor_tensor", [psr[b2], rT[0], rvT[l]], [rT[2]], out=T[2][:, h * 128:(h + 1) * 128],
                    in0=psb(b2, 128, h * 128), scalar=vT[l][:, 108 + h:109 + h], in1=ug[:, h * 128:(h + 1) * 128],
                    op0=ALU.add, op1=ALU.mult)
            S.op("act", lambda e: e.activation(out=T[3][:, 0:512], in_=T[2][:, 0:512], func=AF.Square,
                                               accum_out=sm[:, 40:41]), [rT[2]], [rT[3], rsm])
            act(sm[:, 41:42], sm[:, 40:41], AF.Ln, [rsm, rcst], [rsm], scale=1.0 / 512, bias=EPS6)
            act(sm[:, 41:42], sm[:, 41:42], AF.Exp, [rsm], [rsm], scale=-0.5)
            act(obf[:, 0:512], T[2][:, 0:512], AF.Copy, [rT[2], rsm], [robf], scale=sm[:, 41:42])
            to_mixT(obf, 4, 0, t, l, 32, robf)

    def mixer_b(l, win3, g0):
        wq, rq = load_w(win3[:, :, O_Q:O_Q + 512], 16, 512)
        wkv, rk = load_w(win3[:, :, O_K:O_K + 256], 16, 256)
        wvv = wkv[:, :, 128:256]
        for h in range(8):
            b = PSA.get()
            for k in range(KC):
                mm(ps[0:64, b * 512:(b + 1) * 512], wq[:, k, h * 64:(h + 1) * 64], hT[:, k, :], k == 0, k == KC - 1,
                   [rhT[k], rq], [psr[b]])
            act(mixT[0:64, 8 + h, :], ps[0:64, b * 512:(b + 1) * 512], AF.Copy, [psr[b]], [rmix[8 + h]])
        for a in range(2):
            b = PSA.get()
            for k in range(KC):
                mm(ps[0:64, b * 512:(b + 1) * 512], wkv[:, k, a * 64:(a + 1) * 64], hT[:, k, :], k == 0, k == KC - 1,
                   [rhT[k], rk], [psr[b]])
            act(kT[l][0:64, a, 128:640], ps[0:64, b * 512:(b + 1) * 512], AF.Copy, [psr[b]], [rkT[l]])
        for t in range(NT):
            ts = slice(t * 128, (t + 1) * 128)
            slot = (g0 + t) % 5
            b = PSA.get()
            for k in range(KC):
                mm(psb(b, 128), hT[:, k, ts], wvv[:, k, :], k == 0, k == KC - 1, [rhT[k], rk], [psr[b]])
            act(vaug[l][:, slot, :, 0:64], psb(b, 128).rearrange("p (a d) -> p a d", a=2), AF.Copy,
                [psr[b]], [rva[l][slot]])
        for t in range(NT):
            ts = slice(t * 128, (t + 1) * 128)
            g = g0 + t
            slot = g % 5
            pslot = (g - 1) % 5
            blocks = [(e_cur, recur, 128 + t * 128, U_b, slot)]
            if g > 0:
                blocks.append((e_prev, reprev, t * 128, SL_b, pslot))
            for (ebuf, reb, kc0, mask, _) in blocks:
                b = PSA.get(2)
                for h in range(8):
                    a = h // 4
                    mm(ps[:, b * 512 + h * 128: b * 512 + (h + 1) * 128], kT[l][0:64, a, kc0:kc0 + 128],
                       mixT[0:64, 8 + h, ts], True, True, [rkT[l], rmix[8 + h]], [psr[b + h // 4]], inc=(h % 4 == 3))
                act(ebuf, ps[:, b * 512:(b + 2) * 512], AF.Exp, [psr[b], psr[b + 1]], [reb], scale=0.125)
                dve("tensor_tensor", [reb, rcb], [reb], out=ebuf.rearrange("p (h q) -> p h q", h=8),
                    in0=ebuf.rearrange("p (h q) -> p h q", h=8), in1=vw(mask, [[0, 8], [1, 128]]), op=ALU.mult)
            bo = PSA.get(2)
            for h in range(8):
                a = h // 4
                dst = ps[:, (bo + h // 4) * 512 + (h % 4) * 128:(bo + h // 4) * 512 + (h % 4) * 128 + 66]
                if g > 0:
                    mm(dst, e_prev[:, h * 128:(h + 1) * 128], vaug[l][:, pslot, a, 0:66], True, False,
                       [reprev, rva[l][pslot]], [psr[bo + h // 4]], inc=False)
                mm(dst, e_cur[:, h * 128:(h + 1) * 128], vaug[l][:, slot, a, 0:66], g == 0, True,
                   [recur, rva[l][slot]], [psr[bo + h // 4]], inc=(h % 4 == 3))
            for i2 in range(2):
                pv = psb(bo + i2, 512).rearrange("p (h c) -> p h c", c=128)
                dve("tensor_tensor", [psr[bo + i2], rsp], [rsm], out=sm[:, 48 + i2 * 4:52 + i2 * 4],
                    in0=pv[:, :, 64], in1=esink[l][:, i2 * 4:(i2 + 1) * 4], op=ALU.add)
            dve("reciprocal", [rsm], [rsm], out=sm[:, 48:56], in_=sm[:, 48:56])
            for i2 in range(2):
                pv = psb(bo + i2, 512).rearrange("p (h c) -> p h c", c=128)
                dve("tensor_tensor", [psr[bo + i2], rsm], [rT[2]],
                    out=T[2][:, i2 * 256:(i2 + 1) * 256].rearrange("p (h d) -> p h d", h=4),
                    in0=pv[:, :, 0:64], in1=vw(sm[:, 48 + i2 * 4:52 + i2 * 4], [[1, 4], [0, 64]]), op=ALU.mult)
            S.op("act", lambda e: e.activation(out=T[3][:, 0:512], in_=T[2][:, 0:512], func=AF.Square,
                                               accum_out=sm[:, 60:61]), [rT[2]], [rT[3], rsm])
            act(sm[:, 61:62], sm[:, 60:61], AF.Ln, [rsm, rcst], [rsm], scale=1.0 / 512, bias=EPS6)
            act(sm[:, 61:62], sm[:, 61:62], AF.Exp, [rsm], [rsm], scale=-0.5)
            act(obf[:, 0:512], T[2][:, 0:512], AF.Copy, [rT[2], rsm], [robf], scale=sm[:, 61:62])
            to_mixT(obf, 4, 4, t, l, 36, robf)
        act(kT[l][0:64, :, 0:128], kT[l][0:64, :, 512:640], AF.Copy, [rkT[l]], [rkT[l]])

    def mixer_c(l, win3):
        wx = [None, None, None]

        def xproj(c):
            i3, cc = c // 4, c % 4
            if cc == 0:
                wx[i3] = load_w(win3[:, :, O_X + i3 * 512:O_X + (i3 + 1) * 512], 16, 512)
            w_, rw_ = wx[i3]
            b = PSA.get()
            for k in range(KC):
                mm(psb(b), w_[:, k, cc * 128:(cc + 1) * 128], hT[:, k, :], k == 0, k == KC - 1,
                   [rhT[k], rw_], [psr[b]])
            return b

        def copy_in(c, b):
            xr = xraw[:, c % 2, :]
            rxr = rxraw[c % 2]
            pool("tensor_copy", [rhist[l]], [rxr], out=xr[:, 0:3], in_=hist[l][:, c, :])
            act(xr[:, 3:515], psb(b), AF.Copy, [psr[b]], [rxr])
            pool("tensor_copy", [rxr], [rhist[l]], out=hist[l][:, c, :], in_=xr[:, 512:515])

        copy_in(0, xproj(0))
        for c in range(12):
            if c + 1 < 12:
                copy_in(c + 1, xproj(c + 1))
            xr = xraw[:, c % 2, :]
            rxr = rxraw[c % 2]
            ta = tmps[c % 2]
            rta = rtmp[c % 2]
            dve("tensor_scalar", [rxr, rvT[l]], [rta], out=ta[:], in0=xr[:, 0:512],
                scalar1=vT[l][:, 60 + c:61 + c], scalar2=vT[l][:, 48 + c:49 + c], op0=ALU.mult, op1=ALU.add)
            for tap in range(1, 4):
                dve("scalar_tensor_tensor", [rxr, rvT[l], rta], [rta], out=ta[:], in0=xr[:, tap:tap + 512],
                    scalar=vT[l][:, 60 + tap * 12 + c:61 + tap * 12 + c], in1=ta[:], op0=ALU.mult, op1=ALU.add)
            if c < 8:
                act(obf[:, 0:512], ta[:], AF.Silu, [rta], [robf])
                bt = PSA.get()
                pTx = psb(bt).bitcast(BF16)
                for t in range(NT):
                    tr(pTx[:, t * 128:(t + 1) * 128], obf[:, t * 128:(t + 1) * 128], ident_b, [robf, rcb],
                       [psr[bt]], inc=(t == NT - 1))
                act(xs_tok[:, :, c * 128:(c + 1) * 128], pTx[:, 0:512].rearrange("p (t f) -> p t f", t=4), AF.Copy,
                    [psr[bt]], [rxs])
            else:
                act(bcT[:, c - 8, :], ta[:], AF.Silu, [rta], [rbc])
        b = PSA.get()
        pT = psb(b).bitcast(BF16)
        for t in range(NT):
            for g in range(2):
                tr(pT[:, (t * 2 + g) * 128:(t * 2 + g + 1) * 128], bcT[:, g, t * 128:(t + 1) * 128], ident_b,
                   [rbc, rcb], [psr[b]], inc=(t == NT - 1 and g == 1))
        act(bm_tok[:].rearrange("p t g n -> p (t g n)"), pT, AF.Copy, [psr[b]], [rbmt])
        i = wslot()
        wdt = wbuf[i][:, 0:256].rearrange("p (k n) -> p k n", k=16)
        dma("pool", wdt, win3[:, :, O_DT:O_DT + 16], [], [rw[i]], f"w{i}")
        for t in range(NT):
            ts = slice(t * 128, (t + 1) * 128)
            b = PSA.get()
            for k in range(KC):
                mm(psb(b, 16), hT[:, k, ts], wdt[:, k, :], k == 0, k == KC - 1, [rhT[k], rw[i]], [psr[b]])
            dve("tensor_tensor", [psr[b], rsp], [rdt], out=dtall[:, t, :], in0=psb(b, 16), in1=dtb[l][:], op=ALU.add)
        act(dtall[:], dtall[:], AF.Exp, [rdt], [rdt])
        act(dtall[:], dtall[:], AF.Ln, [rdt, rcst], [rdt], bias=ONE, scale=1.0)
        dve("tensor_tensor", [rdt, rsp], [rdt], out=daall[:], in0=dtall[:], in1=vw(aneg[l][:], [[0, 4], [1, 16]]),
            op=ALU.mult)
        act(Sbf[:], Sst[l][:], AF.Copy, [rS[l]], [rSbf])
        def P1(t):
            par = t % 2
            ts = slice(t * 128, (t + 1) * 128)
            q0 = 64 + par * 96
            rs = rsmc[par]
            xdt, rxdt, xdt2, rxdt2, M_bf, rM = xdtb[par], rxdtb[par], xdt2b[par], rxdt2b[par], M_bf2[par], rM2[par]
            da = daall[:, t, :]
            dt_ = dtall[:, t, :]
            b = PSA.get()
            mm(psb(b, 16), U_f, da, True, True, [rcf, rdt], [psr[b]], inc=False)
            mm(psb(b, 16, 16), ones_f, da, True, True, [rcf, rdt], [psr[b]], inc=True)
            act(sm[:, q0:q0 + 32], psb(b, 32), AF.Copy, [psr[b]], [rs])
            ea = sm[:, q0 + 32:q0 + 48]
            dsd = sm[:, q0 + 48:q0 + 64]
            cd = sm[:, q0 + 64:q0 + 80]
            w2 = sm[:, q0 + 80:q0 + 96]
            act(ea, sm[:, q0:q0 + 16], AF.Exp, [rs], [rs])
            act(cd, sm[:, q0 + 16:q0 + 32], AF.Exp, [rs], [rs])
            dve("tensor_tensor", [rs], [rs], out=dsd, in0=sm[:, q0 + 16:q0 + 32], in1=sm[:, q0:q0 + 16], op=ALU.subtract)
            act(dsd, dsd, AF.Exp, [rs], [rs])
            dve("tensor_tensor", [rs, rdt], [rs], out=w2, in0=dsd, in1=dt_, op=ALU.mult)
            xs3 = xs_tok[:, t, :].rearrange("p (h d) -> p h d", h=16)
            pool("tensor_tensor", [rxs, rdt], [rxdt], out=xdt[:], in0=xs3, in1=vw(dt_, [[1, 16], [0, 64]]), op=ALU.mult)
            pool("tensor_tensor", [rxs, rs], [rxdt2], out=xdt2[:], in0=xs3, in1=vw(w2, [[1, 16], [0, 64]]), op=ALU.mult)
            b = PSA.get()
            for g in range(2):
                mm(psb(b, 128, g * 128), bcT[:, g, ts], bcT[:, 2 + g, ts], True, True, [rbc], [psr[b]], inc=(g == 1))
            dve("tensor_tensor", [psr[b], rcf], [rcbm], out=cbm[:], in0=psb(b, 256).rearrange("p (g l) -> p g l", g=2),
                in1=vw(U_f, [[0, 2], [1, 128]]), op=ALU.mult)
            yield
            for g in range(2):
                dag = daall[:, t, g * 8:(g + 1) * 8]
                pool("tensor_tensor", [rdt, rcf], [rT[0]], out=T[0].rearrange("p (h l) -> p h l", h=8),
                     in0=vw(dag, [[1, 8], [0, 128]]), in1=vw(U_f, [[0, 8], [1, 128]]), op=ALU.mult)
                pool("tensor_copy", [rdt], [rT[1]], out=T[1].rearrange("p (h l) -> p h l", h=8),
                     in_=vw(dag, [[1, 8], [0, 128]]))
                b = PSA.get(2)
                for hf in range(2):
                    mm(psb(b + hf), ones_f, T[0][:, hf * 512:(hf + 1) * 512], True, False, [rcf, rT[0]], [psr[b + hf]],
                       inc=False)
                    mm(psb(b + hf), negU_f, T[1][:, hf * 512:(hf + 1) * 512], False, True, [rcf, rT[1]], [psr[b + hf]],
                       inc=True)
                yield
                dve("tensor_scalar", [psr[b], psr[b + 1]], [rT[2]], out=T[2], in0=ps[:, b * 512:(b + 2) * 512],
                    scalar1=0.0, scalar2=None, op0=ALU.min)
                act(T[3], T[2], AF.Exp, [rT[2]], [rT[3]])
                dve("tensor_tensor", [rT[3], rcbm], [rM[g]], out=M_bf[:, g, :, :],
                    in0=T[3].rearrange("p (h l) -> p h l", h=8), in1=vw(cbm[:, g, :], [[0, 8], [1, 128]]), op=ALU.mult)
                yield

        def P2(t):
            par = t % 2
            ts = slice(t * 128, (t + 1) * 128)
            q0 = 64 + par * 96
            rs = rsmc[par]
            xdt, rxdt, xdt2, rxdt2, M_bf, rM = xdtb[par], rxdtb[par], xdt2b[par], rxdt2b[par], M_bf2[par], rM2[par]
            ea = sm[:, q0 + 32:q0 + 48]
            cd = sm[:, q0 + 64:q0 + 80]
            xs3 = xs_tok[:, t, :].rearrange("p (h d) -> p h d", h=16)
            b = PSA.get(2)
            for h in range(16):
                mm(ps[:, b * 512 + h * 64: b * 512 + (h + 1) * 64], M_bf[:, h // 8, h % 8, :], xdt[:, h, :], True, True,
                   [rM[h // 8], rxdt], [psr[b + h // 8]], inc=(h % 8 == 7))
            b2 = PSA.get(2)
            for g in range(2):
                mm(psb(b2 + g), bcT[:, 2 + g, ts], Sbf[:, g * 512:(g + 1) * 512], True, True, [rbc, rSbf], [psr[b2 + g]])
            b3 = PSA.get(2)
            for g in range(2):
                mm(psb(b3 + g), bm_tok[:, t, g, :], xdt2[:, g * 8:(g + 1) * 8, :].rearrange("p h d -> p (h d)"), True,
                   True, [rbmt, rxdt2], [psr[b3 + g]])
            yield
            dve("tensor_tensor", [psr[b2], psr[b2 + 1], rs], [rT[4]], out=T[4].rearrange("p (h d) -> p h d", h=16),
                in0=ps[:, b2 * 512:(b2 + 2) * 512].rearrange("p (h d) -> p h d", h=16),
                in1=vw(ea, [[1, 16], [0, 64]]), op=ALU.mult)
            dve("tensor_tensor", [psr[b], psr[b + 1], rT[4]], [rT[4]], out=T[4], in0=ps[:, b * 512:(b + 2) * 512],
                in1=T[4], op=ALU.add)
            pool("tensor_tensor", [rxs, rsp], [rT[5]], out=T[5].rearrange("p (h d) -> p h d", h=16), in0=xs3,
                 in1=vw(dsk[l][:], [[1, 16], [0, 64]]), op=ALU.mult)
            dve("tensor_tensor", [rT[4], rT[5]], [rT[4]], out=T[4], in0=T[4], in1=T[5], op=ALU.add)
            dve("tensor_tensor", [rS[l], rs], [rS[l]], out=Sst[l][:].rearrange("p (h d) -> p h d", h=16),
                in0=Sst[l][:].rearrange("p (h d) -> p h d", h=16), in1=vw(cd, [[1, 16], [0, 64]]), op=ALU.mult)
            dve("tensor_tensor", [psr[b3], psr[b3 + 1], rS[l]], [rS[l]], out=Sst[l][:], in0=ps[:, b3 * 512:(b3 + 2) * 512],
                in1=Sst[l][:], op=ALU.add)
            act(Sbf[:], Sst[l][:], AF.Copy, [rS[l]], [rSbf])
            yield
            dve("tensor_tensor", [rT[4]] + rzs, [rT[5]], out=T[5], in0=T[4], in1=zs[:, t, :], op=ALU.mult)
            for g in range(2):
                S.op("act", lambda e, g=g: e.activation(out=obf[:, g * 512:(g + 1) * 512],
                                                        in_=T[5][:, g * 512:(g + 1) * 512], func=AF.Square,
                                                        accum_out=sm[:, 44 + g:45 + g]), [rT[5]], [robf, rsmg])
            act(sm[:, 46:48], sm[:, 44:46], AF.Ln, [rsmg, rcst], [rsmg], scale=1.0 / 512, bias=EPS6)
            act(sm[:, 46:48], sm[:, 46:48], AF.Exp, [rsmg], [rsmg], scale=-0.5)
            for g in range(2):
                act(obf[:, g * 512:(g + 1) * 512], T[5][:, g * 512:(g + 1) * 512], AF.Copy, [rT[5], rsmg], [robf],
                    scale=sm[:, 46 + g:47 + g])
            yield
            to_mixT(obf, 8, 8, t, l, 40, robf)
            yield

        def Z(t):
            ts = slice(t * 128, (t + 1) * 128)
            b = PSA.get(2)
            for (w_, r_, bb) in ((wz0, rz0, b), (wz1, rz1, b + 1)):
                for k in range(KC):
                    mm(psb(bb), hT[:, k, ts], w_[:, k, :], k == 0, k == KC - 1, [rhT[k], r_], [psr[bb]])
            act(zs[:, t, :], ps[:, b * 512:(b + 2) * 512], AF.Silu, [psr[b], psr[b + 1]], rzs)

        def drain(*gens):
            gens = [g for g in gens if g is not None]
            while gens:
                for g in list(gens):
                    try:
                        next(g)
                    except StopIteration:
                        gens.remove(g)

        wz0, rz0 = load_w(win3[:, :, O_Z:O_Z + 512], 16, 512)
        wz1, rz1 = load_w(win3[:, :, O_Z + 512:O_Z + 1024], 16, 512)
        Z(0)
        drain(P1(0))
        for t in range(NT):
            if t + 1 < NT:
                Z(t + 1)
            drain(P1(t + 1) if t + 1 < NT else None, P2(t))

    def out_proj(l, gate):
        wo3 = prm["w_out"][l].rearrange("(k p) n -> p k n", p=128)
        for gi in range(4):
            wo, ro = load_w(wo3[:, :, gi * 512:(gi + 1) * 512], 16, 512)
            for mi in range(4):
                m = gi * 4 + mi
                b = PSA.get()
                for k in range(KC):
                    mm(psb(b), wo[:, k, mi * 128:(mi + 1) * 128], mixT[:, k, :], k == 0, k == KC - 1, [rmix[k], ro],
                       [psr[b]])
                dve("scalar_tensor_tensor", [psr[b], rxT[m], rmod], [rxT[m]], out=xT[:, m, :], in0=psb(b),
                    scalar=gate[:, m:m + 1], in1=xT[:, m, :], op0=ALU.mult, op1=ALU.add)

    def mlp(l, gate):
        w13 = prm["w_mlp1"][l].rearrange("(k p) n -> p k n", p=128)
        for f in range(16):
            w1, r1 = load_w(w13[:, :, f * 512:(f + 1) * 512], 16, 512)
            w2, r2 = load_w(prm["w_mlp2"][l][f * 512:(f + 1) * 512, :].rearrange("(c p) n -> p c n", p=128), 4, 2048)
            hb = (f % 2) * 4
            for cc in range(4):
                b = PSA.get()
                for k in range(KC):
                    mm(psb(b), w1[:, k, cc * 128:(cc + 1) * 128], hT[:, k, :], k == 0, k == KC - 1, [rhT[k], r1],
                       [psr[b]])
                act(tmps[cc % 2][:], psb(b), AF.Relu, [psr[b]], [rtmp[cc % 2]])
                dve("tensor_tensor", [rtmp[cc % 2]], [rmix[hb + cc]], out=mixT[:, hb + cc, :], in0=tmps[cc % 2][:],
                    in1=tmps[cc % 2][:], op=ALU.mult)
            for m in range(KC):
                b = PSA.get()
                for cc in range(4):
                    mm(psb(b), w2[:, cc, m * 128:(m + 1) * 128], mixT[:, hb + cc, :], cc == 0, cc == 3,
                       [rmix[hb + cc], r2], [psr[b]])
                dve("scalar_tensor_tensor", [psr[b], rxT[m], rmod], [rxT[m]], out=xT[:, m, :], in0=psb(b),
                    scalar=gate[:, m:m + 1], in1=xT[:, m, :], op0=ALU.mult, op1=ALU.add)

    for s in range(nseq):
        for l in range(2):
            dve("memset", [], [rS[l]], ap=Sst[l][:], constant=0.0)
            dve("memset", [], [rhist[l]], ap=hist[l][:], constant=0.0)
        for blk in range(nblk):
            tok0 = blk * TB
            for t in range(NT):
                dma("sp", xin, x[s, tok0 + t * 128: tok0 + (t + 1) * 128, :], [], rxin, "xin")
                for j in range(4):
                    b = PSA.get()
                    for i4 in range(4):
                        k = j * 4 + i4
                        tr(psb(b, 128, i4 * 128), xin[:, k * 128:(k + 1) * 128], ident_f, rxin + [rcf], [psr[b]],
                           inc=(i4 == 3))
                    act(xT[:, j * 4:(j + 1) * 4, t * 128:(t + 1) * 128], psb(b).rearrange("p (i f) -> p i f", i=4),
                        AF.Copy, [psr[b]], [rxT[j * 4 + i4] for i4 in range(4)])
            for l in range(nlayer):
                i = l * 2 + s
                WS.l = l
                WS.ti = 0
                WS.first = (s == 0 and blk == 0)
                mo = modT[:, i * 96:(i + 1) * 96]
                win3 = prm["w_in"][l].rearrange("(k p) n -> p k n", p=128)
                dma("sp", lnG[:], prm["gm_ln_g"][l].rearrange("h e -> (h e)").partition_broadcast(128), [], [rln], "ln")
                dma("sp", lnB[:], prm["gm_ln_b"][l].rearrange("h e -> (h e)").partition_broadcast(128), [], [rln], "ln")
                norm_to_hT(gsc[:, i * 32:i * 32 + 16], mo[:, 0:16])
                if do_a:
                    mixer_a(l, win3)
                else:
                    for k in range(0, 4):
                        dve("memset", [], [rmix[k]], ap=mixT[:, k, :], constant=0.0)
                if do_b:
                    mixer_b(l, win3, blk * NT)
                else:
                    for k in range(4, 8):
                        dve("memset", [], [rmix[k]], ap=mixT[:, k, :], constant=0.0)
                if do_c:
                    mixer_c(l, win3)
                else:
                    for k in range(8, 16):
                        dve("memset", [], [rmix[k]], ap=mixT[:, k, :], constant=0.0)
                out_proj(l, mo[:, 32:48])
                if do_mlp:
                    norm_to_hT(gsc[:, i * 32 + 16:i * 32 + 32], mo[:, 48:64])
                    mlp(l, mo[:, 80:96])
            dma("sp", fgb, prm["final_norm_g"].partition_broadcast(128), [], rfgb, "fgb")
            for t in range(NT):
                b = PSA.get(4)
                for k in range(KC):
                    tr(ps[:, b * 512 + k * 128: b * 512 + (k + 1) * 128], xT[:, k, t * 128:(t + 1) * 128], ident_f,
                       [rxT[k], rcf], [psr[b + k // 4]], inc=(k % 4 == 3))
                pr4 = [psr[b + q] for q in range(4)]
                S.op("act", lambda e, b=b: e.activation(out=xin, in_=ps[:, b * 512:(b + 4) * 512], func=AF.Square,
                                                        accum_out=sm[:, 36:37]), pr4, rxin + [rsmf])
                act(sm[:, 37:38], sm[:, 36:37], AF.Sqrt, [rsmf, rcst], [rsmf], scale=1.0 / D, bias=EPS6)
                dve("reciprocal", [rsmf], [rsmf], out=sm[:, 37:38], in_=sm[:, 37:38])
                dve("scalar_tensor_tensor", pr4 + [rsmf] + rfgb, rxin, out=xin, in0=ps[:, b * 512:(b + 4) * 512],
                    scalar=sm[:, 37:38], in1=fgb, op0=ALU.mult, op1=ALU.mult)
                dma("sp", out[s, tok0 + t * 128: tok0 + (t + 1) * 128, :], xin, rxin, [], "xin")

    with nc.allow_non_contiguous_dma(reason="small strided parameter loads"):
        S.emit(final_waits=["xin"])
    es.close()
    return nc


def make_consts():
    c = np.zeros((128, 640), np.float32)
    i = np.arange(128)
    c[:, C_ID:C_ID + 128] = np.eye(128, dtype=np.float32)
    u = (i[:, None] <= i[None, :]).astype(np.float32)
    c[:, C_U:C_U + 128] = u
    c[:, C_NU:C_NU + 128] = -u
    c[:, C_SL:C_SL + 128] = 1.0 - u
    c[:, C_ONE:C_ONE + 128] = 1.0
    return c


_NC_CACHE = {}


def kernel(**inputs):
    n = 8
    key = "full"
    if key not in _NC_CACHE:
        _NC_CACHE[key] = build()
    nc = _NC_CACHE[key]
    x = np.ascontiguousarray(inputs["x"], dtype=np.float32)
    c = np.ascontiguousarray(inputs["c"], dtype=np.float32)
    consts = make_consts()
    params = {k: np.ascontiguousarray(inputs[k], dtype=np.float32) for k in PARAM_NAMES}
    in_maps = []
    for i in range(n):
        m = {"x": x[2 * i:2 * i + 2], "c": c[2 * i:2 * i + 2], "consts": consts}
        m.update(params)
        in_maps.append(m)
    res = run_bass_kernel_spmd(nc, in_maps, core_ids=list(range(n)))
    return np.concatenate([r["out"] for r in res.results], axis=0)
```

```python
import contextlib
import numpy as np
import concourse.bass as bass
import concourse.mybir as mybir
from concourse.bass_utils import run_bass_kernel_spmd

F32 = mybir.dt.float32
BF16 = mybir.dt.bfloat16
AF = mybir.ActivationFunctionType
ALU = mybir.AluOpType

ENGS = ("pe", "act", "dve", "pool", "sp")


class Reg:
    __slots__ = ("name", "w", "rd")

    def __init__(self, name):
        self.name = name
        self.w = None
        self.rd = {}


class Rec:
    __slots__ = ("eng", "fn", "inc", "deps", "dma_sem", "cnt", "idx", "dma_waits")

    def __init__(self, eng, fn, inc, dma_sem):
        self.eng = eng
        self.fn = fn
        self.inc = inc
        self.deps = []
        self.dma_waits = {}
        self.dma_sem = dma_sem
        self.cnt = None
        self.idx = None


class Sched:
    def __init__(self, nc):
        self.nc = nc
        self.recs = {e: [] for e in ENGS}
        self.dma_cnt = {}
        self.n = 0

    def _dep(self, w, d):
        if d is None or d is w:
            return
        if d.dma_sem is not None:
            k = d.dma_sem
            w.dma_waits[k] = max(w.dma_waits.get(k, 0), self.dma_cnt[k])
            return
        if d.eng == "pe" and w.eng == "pe" and w.dma_sem is None:
            return
        if not d.inc:
            lst = self.recs[d.eng]
            tgt = None
            for j in range(d.idx + 1, len(lst)):
                if lst[j].inc and lst[j].dma_sem is None:
                    tgt = lst[j]
                    break
            if tgt is None:
                d.inc = True
                tgt = d
            d = tgt
        w.deps.append(d)

    def op(self, eng, fn, reads=(), writes=(), inc=True, dma=None):
        r = Rec(eng, fn, inc if dma is None else False, dma)
        r.idx = len(self.recs[eng])
        for x in reads:
            self._dep(r, x.w)
        for x in writes:
            self._dep(r, x.w)
            for rr in x.rd.values():
                self._dep(r, rr)
        if dma is not None:
            self.dma_cnt[dma] = self.dma_cnt.get(dma, 0) + 16
            r.cnt = self.dma_cnt[dma]
        self.recs[eng].append(r)
        for x in writes:
            x.w = r
            x.rd = {}
        for x in reads:
            x.rd[eng if dma is None else (dma, r.cnt)] = r
        self.n += 1
        return r

    def emit(self, final_waits=()):
        nc = self.nc
        with contextlib.ExitStack() as es:
            sems = {}
            for e in ENGS:
                sems[e] = es.enter_context(nc.semaphore("s_" + e))
            for k in self.dma_cnt:
                sems[("dma", k)] = es.enter_context(nc.semaphore("d_" + k))
            for e in ENGS:
                c = 0
                for r in self.recs[e]:
                    if r.dma_sem is None and r.inc:
                        c += 1
                        r.cnt = c
            block = es.enter_context(nc.Block())

            def run(e, eng):
                waited = {}
                for r in self.recs[e]:
                    need = {}
                    for d in r.deps:
                        need[d.eng] = max(need.get(d.eng, 0), d.cnt)
                    for k, v in r.dma_waits.items():
                        need[("dma", k)] = max(need.get(("dma", k), 0), v)
                    for k, v in need.items():
                        if waited.get(k, 0) < v:
                            eng.wait_ge(sems[k], v)
                            waited[k] = v
                    ins = r.fn(eng)
                    if r.dma_sem is not None:
                        ins.then_inc(sems[("dma", r.dma_sem)], 16)
                    elif r.inc:
                        ins.then_inc(sems[e], 1)
                if e == "sp":
                    for k in final_waits:
                        eng.wait_ge(sems[("dma", k)], self.dma_cnt[k])

            @block.tensor
            def _(eng):
                run("pe", eng)

            @block.scalar
            def _(eng):
                run("act", eng)

            @block.vector
            def _(eng):
                run("dve", eng)

            @block.gpsimd
            def _(eng):
                run("pool", eng)

            @block.sync
            def _(eng):
                run("sp", eng)


def vw(ap, dims):
    return bass.AP(ap.tensor, ap.offset, [list(ap.ap[0])] + [list(d) for d in dims])


D = 2048
KC = 16
TB = 512
NT = 4
SEQ = 2048
NW = 2
O_U, O_V, O_Q, O_K, O_VV, O_Z, O_X, O_DT = 0, 512, 1024, 1536, 1664, 1792, 2816, 4352
C_ID, C_U, C_NU, C_SL, C_ONE = 0, 128, 256, 384, 512

PARAM_NAMES = ["ada_w", "ada_b", "norm1_g", "w_in", "gm_ln_g", "gm_ln_b", "gm_ws", "gm_bs",
               "gm_norm_g", "attn_sinks", "attn_norm_g", "conv_w", "conv_b", "dt_bias", "a_log",
               "d_skip", "ssm_norm_g", "w_out", "norm2_g", "w_mlp1", "w_mlp2", "final_norm_g"]
PARAM_SHAPES = {
    "ada_w": [2, 2048, 12288], "ada_b": [2, 12288], "norm1_g": [2, 2048], "w_in": [2, 2048, 4368],
    "gm_ln_g": [2, 4, 128], "gm_ln_b": [2, 4, 128], "gm_ws": [2, 4, 128, 128], "gm_bs": [2, 4, 128],
    "gm_norm_g": [2, 512], "attn_sinks": [2, 8], "attn_norm_g": [2, 512], "conv_w": [2, 4, 1536],
    "conv_b": [2, 1536], "dt_bias": [2, 16], "a_log": [2, 16], "d_skip": [2, 16],
    "ssm_norm_g": [2, 1024], "w_out": [2, 2048, 2048], "norm2_g": [2, 2048],
    "w_mlp1": [2, 2048, 8192], "w_mlp2": [2, 8192, 2048], "final_norm_g": [2048],
}


def slim_shapes(nlayer, do_mlp):
    sh = {k: list(v) for k, v in PARAM_SHAPES.items()}
    for k in ("ada_w", "w_in", "w_out", "w_mlp1", "w_mlp2"):
        sh[k][0] = nlayer
    if not do_mlp:
        sh["w_mlp1"] = [1, 128, 128]
        sh["w_mlp2"] = [1, 128, 128]
    return sh


def build(nseq=2, nblk=4, nlayer=2, do_a=True, do_b=True, do_c=True, do_mlp=True, slim=False):
    nc = bass.Bass("TRN2", target_bir_lowering=False)
    din = lambda n, s: nc.dram_tensor(n, s, F32, kind="ExternalInput").ap()
    x = din("x", [2, SEQ, D])
    c_in = din("c", [2, D])
    shp = slim_shapes(nlayer, do_mlp) if slim else PARAM_SHAPES
    prm = {n: din(n, shp[n]) for n in PARAM_NAMES}
    consts = din("consts", [128, 640])
    out = nc.dram_tensor("out", [2, SEQ, D], F32, kind="ExternalOutput").ap()
    mod_d = nc.dram_tensor("mod_d", [2, 2, 12288], F32).ap()
    NTILE = 48
    wscr = nc.dram_tensor("wscr", [2, NTILE, 128, 8192], BF16).ap()

    es = contextlib.ExitStack()
    S = Sched(nc)

    def sb(name, shape, dt=F32):
        return es.enter_context(nc.sbuf_tensor(name, shape, dt))

    xT = sb("xT", [128, KC, TB]); rxT = [Reg(f"xT{k}") for k in range(KC)]
    hT = sb("hT", [128, KC, TB], BF16); rhT = [Reg(f"hT{k}") for k in range(KC)]
    mixT = sb("mixT", [128, KC, TB], BF16); rmix = [Reg(f"mix{k}") for k in range(KC)]
    wbuf = [sb(f"wbuf{i}", [128, 8192], BF16) for i in range(NW)]
    rw = [Reg(f"w{i}") for i in range(NW)]
    TT = sb("TT", [128, 6 * 1024]); rT = [Reg(f"T{i}") for i in range(6)]
    T = [TT[:, i * 1024:(i + 1) * 1024] for i in range(6)]
    xin = TT[:, 2048:4096]; rxin = [rT[2], rT[3]]
    fgb = TT[:, 0:2048]; rfgb = [rT[0], rT[1]]
    cf = sb("cf", [128, 640]); rcf = Reg("cf")
    cb = sb("cb", [128, 640], BF16); rcb = Reg("cb")
    cst = sb("cst", [128, 4]); rcst = Reg("cst")
    vrow = sb("vrow", [128, 128]); rvrow = Reg("vrow")
    vT = [sb(f"vT{l}", [128, 112]) for l in range(2)]; rvT = [Reg(f"vT{l}") for l in range(2)]
    vTf = sb("vTf", [128, 16]); rvTf = Reg("vTf")
    modT = sb("modT", [128, 4 * 96]); rmod = Reg("modT")
    gsc = sb("gsc", [128, 4 * 32]); rgsc = Reg("gsc")
    lnG = sb("lnG", [128, 512]); lnB = sb("lnB", [128, 512]); rln = Reg("ln")
    WsT = [sb(f"WsT{l}", [128, 4, 128], BF16) for l in range(2)]; rws = [Reg(f"ws{l}") for l in range(2)]
    esink = [sb(f"esink{l}", [128, 8]) for l in range(2)]
    dtb = [sb(f"dtb{l}", [128, 16]) for l in range(2)]
    aneg = [sb(f"aneg{l}", [128, 16]) for l in range(2)]
    dsk = [sb(f"dsk{l}", [128, 16]) for l in range(2)]
    rsp = Reg("smallparams")
    rstd_b = sb("rstd_b", [128, TB]); rrstd = Reg("rstd")
    tmps = [sb(f"tmp{i}", [128, TB]) for i in range(2)]; rtmp = [Reg(f"tmp{i}") for i in range(2)]
    AB = sb("AB", [128, 4096], BF16)
    qT = AB[:, 0:2048].rearrange("p (j t) -> p j t", j=4); rqT = Reg("qT")
    e_cur = AB[:, 2048:3072]; recur = Reg("ecur")
    e_prev = AB[:, 3072:4096]; reprev = Reg("eprev")
    zs = AB[:, :].rearrange("p (t f) -> p t f", t=4); rzs = [rqT, recur, reprev]
    kT = [sb(f"kT{l}", [128, 2, 640], BF16) for l in range(2)]; rkT = [Reg(f"kT{l}") for l in range(2)]
    vaug = [sb(f"vaug{l}", [128, 5, 2, 72], BF16) for l in range(2)]
    rva = [[Reg(f"va{l}_{i}") for i in range(5)] for l in range(2)]
    xs_tok = sb("xs_tok", [128, 4, 1024], BF16); rxs = Reg("xs_tok")
    bcT = sb("bcT", [128, 4, TB], BF16); rbc = Reg("bcT")
    bm_tok = sb("bm_tok", [128, 4, 2, 128], BF16); rbmt = Reg("bm_tok")
    xraw = sb("xraw", [128, 2, 515]); rxraw = [Reg("xraw0"), Reg("xraw1")]
    hist = [sb(f"hist{l}", [128, 12, 3]) for l in range(2)]; rhist = [Reg(f"hist{l}") for l in range(2)]
    Sst = [sb(f"Sst{l}", [128, 1024]) for l in range(2)]; rS = [Reg(f"S{l}") for l in range(2)]
    Sbf = sb("Sbf", [128, 1024], BF16); rSbf = Reg("Sbf")
    M_bf2 = [sb(f"M_bf{i}", [128, 2, 8, 128], BF16) for i in range(2)]; rM2 = [[Reg(f"M{i}{g}") for g in range(2)] for i in range(2)]
    xdtb = [sb(f"xdt{i}", [128, 16, 64], BF16) for i in range(2)]; rxdtb = [Reg(f"xdt{i}") for i in range(2)]
    xdt2b = [sb(f"xdt2{i}", [128, 16, 64], BF16) for i in range(2)]; rxdt2b = [Reg(f"xdt2{i}") for i in range(2)]
    vn = sb("vn", [128, 512], BF16); rvn = Reg("vn")
    obf = sb("obf", [128, 1024], BF16); robf = Reg("obf")
    sm = sb("sm", [128, 256]); rsm = Reg("sm")
    rsmc = [Reg("smc0"), Reg("smc1")]; rsmg = Reg("smg"); rsmf = Reg("smf")
    dtall = sb("dtall", [128, 4, 16]); daall = sb("daall", [128, 4, 16]); rdt = Reg("dt")
    cbm = sb("cbm", [128, 2, 128]); rcbm = Reg("cbm")
    cTf = sb("cTf", [128, KC, 2]); cTb = sb("cTb", [128, KC, 2], BF16); rcT = Reg("cT")
    abt = TT[0:2, 4096:4608]; rabt = rT[4]
    mrow = TT[0:2, 5120:5632]; rmrow = rT[5]

    ps = es.enter_context(nc.psum_tensor("ps", [128, 4096], F32))
    psr = [Reg(f"ps{b}") for b in range(8)]

    class PSA:
        i = 0

        @staticmethod
        def get(n=1):
            while PSA.i % n:
                PSA.i += 1
            b = PSA.i % 8
            PSA.i += n
            return b

    def psb(b, n=512, off=0):
        return ps[:, b * 512 + off: b * 512 + off + n]

    def mm(o, lhsT, rhs, start, stop, rd, wr, inc=None):
        S.op("pe", lambda e: e.matmul(o, lhsT, rhs, start=start, stop=stop), rd, wr,
             inc=(stop if inc is None else inc))

    def tr(o, i, ident, rd, wr, inc=True):
        S.op("pe", lambda e: e.transpose(o, i, ident), rd, wr, inc=inc)

    def act(o, i, func, rd, wr, **kw):
        S.op("act", lambda e: e.activation(out=o, in_=i, func=func, **kw), rd, wr)

    def dve(meth, rd, wr, **kw):
        S.op("dve", lambda e: getattr(e, meth)(**kw), rd, wr)

    def pool(meth, rd, wr, **kw):
        S.op("pool", lambda e: getattr(e, meth)(**kw), rd, wr)

    def dma(q, o, i, rd, wr, key):
        S.op(q, lambda e: e.dma_start(out=o, in_=i), rd, wr, dma=key)

    class WL:
        i = 0

    def wslot():
        i = WL.i % NW
        WL.i += 1
        return i

    class WS:
        first = True
        l = 0
        ti = 0
    rwscr = [Reg("wscr0"), Reg("wscr1")]

    def load_w(src3, k, n, cache=True):
        i = wslot()
        v = wbuf[i][:, 0:k * n].rearrange("p (k n) -> p k n", k=k)
        if not cache:
            dma("pool", v, src3, [], [rw[i]], f"w{i}")
            return v, rw[i]
        l_, ti = WS.l, WS.ti
        WS.ti += 1
        flat = wbuf[i][:, 0:k * n]
        if WS.first:
            dma("pool", v, src3, [], [rw[i]], f"w{i}")
            dma("sp", wscr[l_, ti, :, 0:k * n], flat, [rw[i]], [rwscr[l_]], f"ws{l_}")
        else:
            dma("sp", flat, wscr[l_, ti, :, 0:k * n], [rwscr[l_]], [rw[i]], f"wh{i}")
        return v, rw[i]

    ident_f = cf[:, C_ID:C_ID + 128]
    U_f = cf[:, C_U:C_U + 128]
    negU_f = cf[:, C_NU:C_NU + 128]
    ones_f = cf[:, C_ONE:C_ONE + 128]
    ident_b = cb[:, C_ID:C_ID + 128]
    U_b = cb[:, C_U:C_U + 128]
    SL_b = cb[:, C_SL:C_SL + 128]
    ones_b = cb[:, C_ONE:C_ONE + 128]
    EPS6 = cst[:, 0:1]
    EPS5 = cst[:, 1:2]
    ONE = cst[:, 2:3]

    dma("sp", cf[:], consts, [], [rcf], "const")
    dma("pool", cb[:], consts, [], [rcb], "constb")
    dve("memset", [], [rcst], ap=cst[:, 0:1], constant=1e-6)
    dve("memset", [], [rcst], ap=cst[:, 1:2], constant=1e-5)
    dve("memset", [], [rcst], ap=cst[:, 2:3], constant=1.0)
    dve("memset", [], [rcst], ap=cst[:, 3:4], constant=0.0)
    for l in range(2):
        for i in range(5):
            dve("memset", [], [rva[l][i]], ap=vaug[l][:, i, :, 64:72], constant=0.0)
            dve("memset", [], [rva[l][i]], ap=vaug[l][:, i, :, 64:65], constant=1.0)

    def vec_rows(l):
        rows = []
        rows.append((0, 16, prm["norm1_g"][l].rearrange("(k p) -> k p", p=128)))
        rows.append((16, 16, prm["norm2_g"][l].rearrange("(k p) -> k p", p=128)))
        rows.append((32, 4, prm["gm_norm_g"][l].rearrange("(k p) -> k p", p=128)))
        rows.append((36, 4, prm["attn_norm_g"][l].rearrange("(k p) -> k p", p=128)))
        rows.append((40, 8, prm["ssm_norm_g"][l].rearrange("(k p) -> k p", p=128)))
        rows.append((48, 12, prm["conv_b"][l].rearrange("(k p) -> k p", p=128)))
        rows.append((60, 48, prm["conv_w"][l].rearrange("t (k p) -> (t k) p", p=128)))
        rows.append((108, 4, prm["gm_bs"][l]))
        return rows

    for l in range(2):
        for (r0, n, src) in vec_rows(l):
            dma("sp", vrow[r0:r0 + n, :], src, [], [rvrow], "vrow")
        b = PSA.get()
        tr(psb(b, 112), vrow[0:112, :], ident_f[0:112, 0:112], [rvrow, rcf], [psr[b]])
        act(vT[l][:], psb(b, 112), AF.Copy, [psr[b]], [rvT[l]])
    dma("sp", vrow[0:16, :], prm["final_norm_g"].rearrange("(k p) -> k p", p=128), [], [rvrow], "vrow")
    b = PSA.get()
    tr(psb(b, 16), vrow[0:16, :], ident_f[0:16, 0:16], [rvrow, rcf], [psr[b]])
    act(vTf[:], psb(b, 16), AF.Copy, [psr[b]], [rvTf])

    for l in range(2):
        dma("sp", esink[l][:], prm["attn_sinks"][l].partition_broadcast(128), [], [rsp], "sp")
        dma("sp", dtb[l][:], prm["dt_bias"][l].partition_broadcast(128), [], [rsp], "sp")
        dma("sp", aneg[l][:], prm["a_log"][l].partition_broadcast(128), [], [rsp], "sp")
        dma("sp", dsk[l][:], prm["d_skip"][l].partition_broadcast(128), [], [rsp], "sp")
    for l in range(2):
        act(esink[l][:], esink[l][:], AF.Exp, [rsp], [rsp])
        act(aneg[l][:], aneg[l][:], AF.Exp, [rsp], [rsp])
        dve("tensor_scalar", [rsp], [rsp], out=aneg[l][:], in0=aneg[l][:], scalar1=-1.0, scalar2=None,
            op0=ALU.mult)
    for l in range(2):
        wsf = T[0].rearrange("p (h s) -> p h s", h=8)[:, 0:4, :]
        dma("sp", wsf, prm["gm_ws"][l].rearrange("h t s -> t h s"), [], [rT[0]], "wsf")
        b = PSA.get()
        for h in range(4):
            tr(psb(b, 128, h * 128), wsf[:, h, :], ident_f, [rT[0], rcf], [psr[b]], inc=(h == 3))
        dve("tensor_tensor", [psr[b], rcf], [rws[l]], out=WsT[l][:],
            in0=psb(b).rearrange("p (h t) -> p h t", h=4), in1=vw(U_f, [[0, 4], [1, 128]]), op=ALU.mult)

    for s_ in range(2):
        dma("sp", cTf[:, :, s_], c_in[s_].rearrange("(k p) -> p k", p=128), [], [rcT], "cT")
    act(cTb[:], cTf[:], AF.Silu, [rcT], [rcT])
    for l in range(nlayer):
        aw3 = prm["ada_w"][l].rearrange("(k p) n -> p k n", p=128)
        for n in range(24):
            w, rwi = load_w(aw3[:, :, n * 512:(n + 1) * 512], 16, 512, cache=False)
            dma("sp", abt, prm["ada_b"][l][n * 512:(n + 1) * 512].partition_broadcast(2), [], [rabt], "abt")
            b = PSA.get()
            for k in range(KC):
                mm(ps[0:2, b * 512:(b + 1) * 512], cTb[:, k, :], w[:, k, :], k == 0, k == KC - 1,
                   [rcT, rwi], [psr[b]])
            dve("tensor_tensor", [psr[b], rabt], [rmrow], out=mrow, in0=ps[0:2, b * 512:(b + 1) * 512],
                in1=abt, op=ALU.add)
            dma("sp", mod_d[l, :, n * 512:(n + 1) * 512], mrow, [rmrow], [], "mrow")
    rmodd = Reg("mod_d")
    for l in range(nlayer):
        for s in range(2):
            i = l * 2 + s
            S.op("sp", lambda e, l=l, s=s: e.dma_start(out=vrow[0:96, :],
                                                       in_=mod_d[l, s].rearrange("(r p) -> r p", p=128)),
                 [], [rmrow, rvrow], dma="mrow")
            b = PSA.get()
            tr(psb(b, 96), vrow[0:96, :], ident_f[0:96, 0:96], [rvrow, rcf], [psr[b]])
            act(modT[:, i * 96:(i + 1) * 96], psb(b, 96), AF.Copy, [psr[b]], [rmod])
            dve("scalar_tensor_tensor", [rmod, rvT[l]], [rgsc], out=gsc[:, i * 32:i * 32 + 16],
                in0=modT[:, i * 96 + 16:i * 96 + 32], scalar=1.0, in1=vT[l][:, 0:16], op0=ALU.add, op1=ALU.mult)
            dve("scalar_tensor_tensor", [rmod, rvT[l]], [rgsc], out=gsc[:, i * 32 + 16:i * 32 + 32],
                in0=modT[:, i * 96 + 64:i * 96 + 80], scalar=1.0, in1=vT[l][:, 16:32], op0=ALU.add, op1=ALU.mult)

    def norm_to_hT(gs, sh):
        for k in range(KC):
            act(hT[:, k, :], xT[:, k, :], AF.Square, [rxT[k]], [rhT[k]])
        b = PSA.get()
        for k in range(KC):
            mm(psb(b), ones_b, hT[:, k, :], k == 0, k == KC - 1, [rhT[k], rcb], [psr[b]])
        act(rstd_b[:], psb(b), AF.Sqrt, [psr[b], rcst], [rrstd], scale=1.0 / D, bias=EPS6)
        dve("reciprocal", [rrstd], [rrstd], out=rstd_b[:], in_=rstd_b[:])
        for k in range(KC):
            t_ = tmps[k % 2]
            dve("scalar_tensor_tensor", [rxT[k], rrstd, rgsc], [rtmp[k % 2]], out=t_[:], in0=xT[:, k, :],
                scalar=gs[:, k:k + 1], in1=rstd_b[:], op0=ALU.mult, op1=ALU.mult)
            act(hT[:, k, :], t_[:], AF.Identity, [rtmp[k % 2], rmod], [rhT[k]], bias=sh[:, k:k + 1], scale=1.0)

    def rms_scale(src, n, ss_col):
        act(T[3][:, 0:n], src, AF.Square, [rsm], [rT[3], rsm], accum_out=sm[:, ss_col:ss_col + 1])
        act(sm[:, ss_col + 1:ss_col + 2], sm[:, ss_col:ss_col + 1], AF.Sqrt, [rsm, rcst], [rsm],
            scale=1.0 / n, bias=EPS6)
        dve("reciprocal", [rsm], [rsm], out=sm[:, ss_col + 1:ss_col + 2], in_=sm[:, ss_col + 1:ss_col + 2])

    def to_mixT(src_bf, nchunk, k0, t, l, gcol, rsrc):
        b = PSA.get()
        pT = psb(b).bitcast(BF16)
        for j in range(nchunk):
            tr(pT[:, j * 128:(j + 1) * 128], src_bf[:, j * 128:(j + 1) * 128], ident_b, [rsrc, rcb], [psr[b]],
               inc=(j == nchunk - 1))
        for j in range(nchunk):
            act(mixT[:, k0 + j, t * 128:(t + 1) * 128], pT[:, j * 128:(j + 1) * 128], AF.Copy,
                [psr[b], rvT[l]], [rmix[k0 + j]], scale=vT[l][:, gcol + j:gcol + j + 1])

    def mixer_a(l, win3):
        wu, ru = load_w(win3[:, :, O_U:O_U + 512], 16, 512)
        wv, rv = load_w(win3[:, :, O_V:O_V + 512], 16, 512)
        def proj(t):
            ts = slice(t * 128, (t + 1) * 128)
            b = PSA.get(2)
            for k in range(KC):
                mm(psb(b), hT[:, k, ts], wu[:, k, :], k == 0, k == KC - 1, [rhT[k], ru], [psr[b]])
            for k in range(KC):
                mm(psb(b + 1), hT[:, k, ts], wv[:, k, :], k == 0, k == KC - 1, [rhT[k], rv], [psr[b + 1]])
            return b

        bnext = proj(0)
        for t in range(NT):
            b = bnext
            if t + 1 < NT:
                bnext = proj(t + 1)
            act(T[0], ps[:, b * 512:(b + 2) * 512], AF.Gelu_apprx_tanh, [psr[b], psr[b + 1]], [rT[0]])
            ug = T[0][:, 0:512]
            vg = T[0][:, 512:1024]
            st6 = sm[:, 0:24].rearrange("p (h s) -> p h s", h=4)
            mv = sm[:, 24:32].rearrange("p (h s) -> p h s", h=4)
            for h in range(4):
                dve("bn_stats", [rT[0]], [rsm], out=st6[:, h, :], in_=vg[:, h * 128:(h + 1) * 128])
            for h in range(4):
                dve("bn_aggr", [rsm], [rsm], out=mv[:, h, :], in_=st6[:, h, :])
            act(sm[:, 32:36], mv[:, :, 1], AF.Sqrt, [rsm, rcst], [rsm], bias=EPS5, scale=1.0)
            dve("reciprocal", [rsm], [rsm], out=sm[:, 32:36], in_=sm[:, 32:36])
            for h in range(4):
                dve("tensor_scalar", [rT[0], rsm], [rT[1]], out=T[1][:, h * 128:(h + 1) * 128],
                    in0=vg[:, h * 128:(h + 1) * 128], scalar1=mv[:, h, 0:1], scalar2=sm[:, 32 + h:33 + h],
                    op0=ALU.subtract, op1=ALU.mult)
            dve("tensor_tensor", [rT[1], rln], [rT[1]], out=T[1][:, 0:512], in0=T[1][:, 0:512], in1=lnG[:],
                op=ALU.mult)
            dve("tensor_tensor", [rT[1], rln], [rvn], out=vn[:], in0=T[1][:, 0:512], in1=lnB[:], op=ALU.add)
            b2 = PSA.get()
            for h in range(4):
                mm(psb(b2, 128, h * 128), WsT[l][:, h, :], vn[:, h * 128:(h + 1) * 128], True, True,
                   [rws[l], rvn], [psr[b2]], inc=(h == 3))
            for h in range(4):
                dve("scalar_tensor_tensor", [psr[b2], rT[0], rvT[l]], [rT[2]], out=T[2][:, h * 128:(h + 1) * 128],
                    in0=psb(b2, 128, h * 128), scalar=vT[l][:, 108 + h:109 + h], in1=ug[:, h * 128:(h + 1) * 128],
                    op0=ALU.add, op1=ALU.mult)
            S.op("act", lambda e: e.activation(out=T[3][:, 0:512], in_=T[2][:, 0:512], func=AF.Square,
                                               accum_out=sm[:, 40:41]), [rT[2]], [rT[3], rsm])
            act(sm[:, 41:42], sm[:, 40:41], AF.Sqrt, [rsm, rcst], [rsm], scale=1.0 / 512, bias=EPS6)
            dve("reciprocal", [rsm], [rsm], out=sm[:, 41:42], in_=sm[:, 41:42])
            act(obf[:, 0:512], T[2][:, 0:512], AF.Copy, [rT[2], rsm], [robf], scale=sm[:, 41:42])
            to_mixT(obf, 4, 0, t, l, 32, robf)

    def mixer_b(l, win3, g0):
        wq, rq = load_w(win3[:, :, O_Q:O_Q + 512], 16, 512)
        wkv, rk = load_w(win3[:, :, O_K:O_K + 256], 16, 256)
        wvv = wkv[:, :, 128:256]
        for h in range(8):
            b = PSA.get()
            for k in range(KC):
                mm(ps[0:64, b * 512:(b + 1) * 512], wq[:, k, h * 64:(h + 1) * 64], hT[:, k, :], k == 0, k == KC - 1,
                   [rhT[k], rq], [psr[b]])
            act(mixT[0:64, 8 + h, :], ps[0:64, b * 512:(b + 1) * 512], AF.Copy, [psr[b]], [rmix[8 + h]])
        for a in range(2):
            b = PSA.get()
            for k in range(KC):
                mm(ps[0:64, b * 512:(b + 1) * 512], wkv[:, k, a * 64:(a + 1) * 64], hT[:, k, :], k == 0, k == KC - 1,
                   [rhT[k], rk], [psr[b]])
            act(kT[l][0:64, a, 128:640], ps[0:64, b * 512:(b + 1) * 512], AF.Copy, [psr[b]], [rkT[l]])
        for t in range(NT):
            ts = slice(t * 128, (t + 1) * 128)
            slot = (g0 + t) % 5
            b = PSA.get()
            for k in range(KC):
                mm(psb(b, 128), hT[:, k, ts], wvv[:, k, :], k == 0, k == KC - 1, [rhT[k], rk], [psr[b]])
            act(vaug[l][:, slot, :, 0:64], psb(b, 128).rearrange("p (a d) -> p a d", a=2), AF.Copy,
                [psr[b]], [rva[l][slot]])
        for t in range(NT):
            ts = slice(t * 128, (t + 1) * 128)
            g = g0 + t
            slot = g % 5
            pslot = (g - 1) % 5
            blocks = [(e_cur, recur, 128 + t * 128, U_b, slot)]
            if g > 0:
                blocks.append((e_prev, reprev, t * 128, SL_b, pslot))
            for (ebuf, reb, kc0, mask, _) in blocks:
                b = PSA.get(2)
                for h in range(8):
                    a = h // 4
                    mm(ps[:, b * 512 + h * 128: b * 512 + (h + 1) * 128], kT[l][0:64, a, kc0:kc0 + 128],
                       mixT[0:64, 8 + h, ts], True, True, [rkT[l], rmix[8 + h]], [psr[b + h // 4]], inc=(h % 4 == 3))
                act(ebuf, ps[:, b * 512:(b + 2) * 512], AF.Exp, [psr[b], psr[b + 1]], [reb], scale=0.125)
                dve("tensor_tensor", [reb, rcb], [reb], out=ebuf.rearrange("p (h q) -> p h q", h=8),
                    in0=ebuf.rearrange("p (h q) -> p h q", h=8), in1=vw(mask, [[0, 8], [1, 128]]), op=ALU.mult)
            bo = PSA.get(2)
            for h in range(8):
                a = h // 4
                dst = ps[:, (bo + h // 4) * 512 + (h % 4) * 128:(bo + h // 4) * 512 + (h % 4) * 128 + 66]
                if g > 0:
                    mm(dst, e_prev[:, h * 128:(h + 1) * 128], vaug[l][:, pslot, a, 0:66], True, False,
                       [reprev, rva[l][pslot]], [psr[bo + h // 4]], inc=False)
                mm(dst, e_cur[:, h * 128:(h + 1) * 128], vaug[l][:, slot, a, 0:66], g == 0, True,
                   [recur, rva[l][slot]], [psr[bo + h // 4]], inc=(h % 4 == 3))
            for i2 in range(2):
                pv = psb(bo + i2, 512).rearrange("p (h c) -> p h c", c=128)
                dve("tensor_tensor", [psr[bo + i2], rsp], [rsm], out=sm[:, 48 + i2 * 4:52 + i2 * 4],
                    in0=pv[:, :, 64], in1=esink[l][:, i2 * 4:(i2 + 1) * 4], op=ALU.add)
            dve("reciprocal", [rsm], [rsm], out=sm[:, 48:56], in_=sm[:, 48:56])
            for i2 in range(2):
                pv = psb(bo + i2, 512).rearrange("p (h c) -> p h c", c=128)
                dve("tensor_tensor", [psr[bo + i2], rsm], [rT[2]],
                    out=T[2][:, i2 * 256:(i2 + 1) * 256].rearrange("p (h d) -> p h d", h=4),
                    in0=pv[:, :, 0:64], in1=vw(sm[:, 48 + i2 * 4:52 + i2 * 4], [[1, 4], [0, 64]]), op=ALU.mult)
            S.op("act", lambda e: e.activation(out=T[3][:, 0:512], in_=T[2][:, 0:512], func=AF.Square,
                                               accum_out=sm[:, 60:61]), [rT[2]], [rT[3], rsm])
            act(sm[:, 61:62], sm[:, 60:61], AF.Ln, [rsm, rcst], [rsm], scale=1.0 / 512, bias=EPS6)
            act(sm[:, 61:62], sm[:, 61:62], AF.Exp, [rsm], [rsm], scale=-0.5)
            act(obf[:, 0:512], T[2][:, 0:512], AF.Copy, [rT[2], rsm], [robf], scale=sm[:, 61:62])
            to_mixT(obf, 4, 4, t, l, 36, robf)
        act(kT[l][0:64, :, 0:128], kT[l][0:64, :, 512:640], AF.Copy, [rkT[l]], [rkT[l]])

    def mixer_c(l, win3):
        wx = [None, None, None]

        def xproj(c):
            i3, cc = c // 4, c % 4
            if cc == 0:
                wx[i3] = load_w(win3[:, :, O_X + i3 * 512:O_X + (i3 + 1) * 512], 16, 512)
            w_, rw_ = wx[i3]
            b = PSA.get()
            for k in range(KC):
                mm(psb(b), w_[:, k, cc * 128:(cc + 1) * 128], hT[:, k, :], k == 0, k == KC - 1,
                   [rhT[k], rw_], [psr[b]])
            return b

        def copy_in(c, b):
            xr = xraw[:, c % 2, :]
            rxr = rxraw[c % 2]
            dve("tensor_copy", [rhist[l]], [rxr], out=xr[:, 0:3], in_=hist[l][:, c, :])
            act(xr[:, 3:515], psb(b), AF.Copy, [psr[b]], [rxr])

        def hist_upd(c):
            dve("tensor_copy", [rxraw[c % 2]], [rhist[l]], out=hist[l][:, c, :], in_=xraw[:, c % 2, 512:515])

        copy_in(0, xproj(0))
        hist_upd(0)
        for c in range(12):
            if c + 1 < 12:
                copy_in(c + 1, xproj(c + 1))
            xr = xraw[:, c % 2, :]
            rxr = rxraw[c % 2]
            ta = tmps[c % 2]
            rta = rtmp[c % 2]
            dve("tensor_scalar", [rxr, rvT[l]], [rta], out=ta[:], in0=xr[:, 0:512],
                scalar1=vT[l][:, 60 + c:61 + c], scalar2=vT[l][:, 48 + c:49 + c], op0=ALU.mult, op1=ALU.add)
            for tap in range(1, 4):
                dve("scalar_tensor_tensor", [rxr, rvT[l], rta], [rta], out=ta[:], in0=xr[:, tap:tap + 512],
                    scalar=vT[l][:, 60 + tap * 12 + c:61 + tap * 12 + c], in1=ta[:], op0=ALU.mult, op1=ALU.add)
            if c + 1 < 12:
                hist_upd(c + 1)
            if c < 8:
                act(obf[:, 0:512], ta[:], AF.Silu, [rta], [robf])
                bt = PSA.get()
                pTx = psb(bt).bitcast(BF16)
                for t in range(NT):
                    tr(pTx[:, t * 128:(t + 1) * 128], obf[:, t * 128:(t + 1) * 128], ident_b, [robf, rcb],
                       [psr[bt]], inc=(t == NT - 1))
                act(xs_tok[:, :, c * 128:(c + 1) * 128], pTx[:, 0:512].rearrange("p (t f) -> p t f", t=4), AF.Copy,
                    [psr[bt]], [rxs])
            else:
                act(bcT[:, c - 8, :], ta[:], AF.Silu, [rta], [rbc])
        b = PSA.get()
        pT = psb(b).bitcast(BF16)
        for t in range(NT):
            for g in range(2):
                tr(pT[:, (t * 2 + g) * 128:(t * 2 + g + 1) * 128], bcT[:, g, t * 128:(t + 1) * 128], ident_b,
                   [rbc, rcb], [psr[b]], inc=(t == NT - 1 and g == 1))
        act(bm_tok[:].rearrange("p t g n -> p (t g n)"), pT, AF.Copy, [psr[b]], [rbmt])
        i = wslot()
        wdt = wbuf[i][:, 0:256].rearrange("p (k n) -> p k n", k=16)
        dma("pool", wdt, win3[:, :, O_DT:O_DT + 16], [], [rw[i]], f"w{i}")
        for t in range(NT):
            ts = slice(t * 128, (t + 1) * 128)
            b = PSA.get()
            for k in range(KC):
                mm(psb(b, 16), hT[:, k, ts], wdt[:, k, :], k == 0, k == KC - 1, [rhT[k], rw[i]], [psr[b]])
            dve("tensor_tensor", [psr[b], rsp], [rdt], out=dtall[:, t, :], in0=psb(b, 16), in1=dtb[l][:], op=ALU.add)
        act(dtall[:], dtall[:], AF.Exp, [rdt], [rdt])
        act(dtall[:], dtall[:], AF.Ln, [rdt, rcst], [rdt], bias=ONE, scale=1.0)
        dve("tensor_tensor", [rdt, rsp], [rdt], out=daall[:], in0=dtall[:], in1=vw(aneg[l][:], [[0, 4], [1, 16]]),
            op=ALU.mult)
        act(Sbf[:], Sst[l][:], AF.Copy, [rS[l]], [rSbf])
        def P1(t):
            par = t % 2
            ts = slice(t * 128, (t + 1) * 128)
            q0 = 64 + par * 96
            rs = rsmc[par]
            xdt, rxdt, xdt2, rxdt2, M_bf, rM = xdtb[par], rxdtb[par], xdt2b[par], rxdt2b[par], M_bf2[par], rM2[par]
            da = daall[:, t, :]
            dt_ = dtall[:, t, :]
            b = PSA.get()
            mm(psb(b, 16), U_f, da, True, True, [rcf, rdt], [psr[b]], inc=False)
            mm(psb(b, 16, 16), ones_f, da, True, True, [rcf, rdt], [psr[b]], inc=True)
            act(sm[:, q0:q0 + 32], psb(b, 32), AF.Copy, [psr[b]], [rs])
            ea = sm[:, q0 + 32:q0 + 48]
            dsd = sm[:, q0 + 48:q0 + 64]
            cd = sm[:, q0 + 64:q0 + 80]
            w2 = sm[:, q0 + 80:q0 + 96]
            act(ea, sm[:, q0:q0 + 16], AF.Exp, [rs], [rs])
            act(cd, sm[:, q0 + 16:q0 + 32], AF.Exp, [rs], [rs])
            dve("tensor_tensor", [rs], [rs], out=dsd, in0=sm[:, q0 + 16:q0 + 32], in1=sm[:, q0:q0 + 16], op=ALU.subtract)
            act(dsd, dsd, AF.Exp, [rs], [rs])
            dve("tensor_tensor", [rs, rdt], [rs], out=w2, in0=dsd, in1=dt_, op=ALU.mult)
            xs3 = xs_tok[:, t, :].rearrange("p (h d) -> p h d", h=16)
            pool("tensor_tensor", [rxs, rdt], [rxdt], out=xdt[:], in0=xs3, in1=vw(dt_, [[1, 16], [0, 64]]), op=ALU.mult)
            pool("tensor_tensor", [rxs, rs], [rxdt2], out=xdt2[:], in0=xs3, in1=vw(w2, [[1, 16], [0, 64]]), op=ALU.mult)
            b = PSA.get()
            for g in range(2):
                mm(psb(b, 128, g * 128), bcT[:, g, ts], bcT[:, 2 + g, ts], True, True, [rbc], [psr[b]], inc=(g == 1))
            dve("tensor_tensor", [psr[b], rcf], [rcbm], out=cbm[:], in0=psb(b, 256).rearrange("p (g l) -> p g l", g=2),
                in1=vw(U_f, [[0, 2], [1, 128]]), op=ALU.mult)
            yield
            for g in range(2):
                dag = daall[:, t, g * 8:(g + 1) * 8]
                pool("tensor_tensor", [rdt, rcf], [rT[0]], out=T[0].rearrange("p (h l) -> p h l", h=8),
                     in0=vw(dag, [[1, 8], [0, 128]]), in1=vw(U_f, [[0, 8], [1, 128]]), op=ALU.mult)
                pool("tensor_copy", [rdt], [rT[1]], out=T[1].rearrange("p (h l) -> p h l", h=8),
                     in_=vw(dag, [[1, 8], [0, 128]]))
                b = PSA.get(2)
                for hf in range(2):
                    mm(psb(b + hf), ones_f, T[0][:, hf * 512:(hf + 1) * 512], True, False, [rcf, rT[0]], [psr[b + hf]],
                       inc=False)
                    mm(psb(b + hf), negU_f, T[1][:, hf * 512:(hf + 1) * 512], False, True, [rcf, rT[1]], [psr[b + hf]],
                       inc=True)
                yield
                dve("tensor_scalar", [psr[b], psr[b + 1]], [rT[2]], out=T[2], in0=ps[:, b * 512:(b + 2) * 512],
                    scalar1=0.0, scalar2=None, op0=ALU.min)
                act(T[3], T[2], AF.Exp, [rT[2]], [rT[3]])
                dve("tensor_tensor", [rT[3], rcbm], [rM[g]], out=M_bf[:, g, :, :],
                    in0=T[3].rearrange("p (h l) -> p h l", h=8), in1=vw(cbm[:, g, :], [[0, 8], [1, 128]]), op=ALU.mult)
                yield

        def P2(t):
            par = t % 2
            ts = slice(t * 128, (t + 1) * 128)
            q0 = 64 + par * 96
            rs = rsmc[par]
            xdt, rxdt, xdt2, rxdt2, M_bf, rM = xdtb[par], rxdtb[par], xdt2b[par], rxdt2b[par], M_bf2[par], rM2[par]
            ea = sm[:, q0 + 32:q0 + 48]
            cd = sm[:, q0 + 64:q0 + 80]
            xs3 = xs_tok[:, t, :].rearrange("p (h d) -> p h d", h=16)
            b = PSA.get(2)
            for h in range(16):
                mm(ps[:, b * 512 + h * 64: b * 512 + (h + 1) * 64], M_bf[:, h // 8, h % 8, :], xdt[:, h, :], True, True,
                   [rM[h // 8], rxdt], [psr[b + h // 8]], inc=(h % 8 == 7))
            b2 = PSA.get(2)
            for g in range(2):
                mm(psb(b2 + g), bcT[:, 2 + g, ts], Sbf[:, g * 512:(g + 1) * 512], True, True, [rbc, rSbf], [psr[b2 + g]])
            b3 = PSA.get(2)
            for g in range(2):
                mm(psb(b3 + g), bm_tok[:, t, g, :], xdt2[:, g * 8:(g + 1) * 8, :].rearrange("p h d -> p (h d)"), True,
                   True, [rbmt, rxdt2], [psr[b3 + g]])
            yield
            dve("tensor_tensor", [psr[b2], psr[b2 + 1], rs], [rT[4]], out=T[4].rearrange("p (h d) -> p h d", h=16),
                in0=ps[:, b2 * 512:(b2 + 2) * 512].rearrange("p (h d) -> p h d", h=16),
                in1=vw(ea, [[1, 16], [0, 64]]), op=ALU.mult)
            dve("tensor_tensor", [psr[b], psr[b + 1], rT[4]], [rT[4]], out=T[4], in0=ps[:, b * 512:(b + 2) * 512],
                in1=T[4], op=ALU.add)
            pool("tensor_tensor", [rxs, rsp], [rT[5]], out=T[5].rearrange("p (h d) -> p h d", h=16), in0=xs3,
                 in1=vw(dsk[l][:], [[1, 16], [0, 64]]), op=ALU.mult)
            dve("tensor_tensor", [rT[4], rT[5]], [rT[4]], out=T[4], in0=T[4], in1=T[5], op=ALU.add)
            dve("tensor_tensor", [rS[l], rs], [rS[l]], out=Sst[l][:].rearrange("p (h d) -> p h d", h=16),
                in0=Sst[l][:].rearrange("p (h d) -> p h d", h=16), in1=vw(cd, [[1, 16], [0, 64]]), op=ALU.mult)
            dve("tensor_tensor", [psr[b3], psr[b3 + 1], rS[l]], [rS[l]], out=Sst[l][:], in0=ps[:, b3 * 512:(b3 + 2) * 512],
                in1=Sst[l][:], op=ALU.add)
            act(Sbf[:], Sst[l][:], AF.Copy, [rS[l]], [rSbf])
            yield
            dve("tensor_tensor", [rT[4]] + rzs, [rT[5]], out=T[5], in0=T[4], in1=zs[:, t, :], op=ALU.mult)
            for g in range(2):
                S.op("act", lambda e, g=g: e.activation(out=obf[:, g * 512:(g + 1) * 512],
                                                        in_=T[5][:, g * 512:(g + 1) * 512], func=AF.Square,
                                                        accum_out=sm[:, 44 + g:45 + g]), [rT[5]], [robf, rsmg])
            act(sm[:, 46:48], sm[:, 44:46], AF.Ln, [rsmg, rcst], [rsmg], scale=1.0 / 512, bias=EPS6)
            act(sm[:, 46:48], sm[:, 46:48], AF.Exp, [rsmg], [rsmg], scale=-0.5)
            for g in range(2):
                act(obf[:, g * 512:(g + 1) * 512], T[5][:, g * 512:(g + 1) * 512], AF.Copy, [rT[5], rsmg], [robf],
                    scale=sm[:, 46 + g:47 + g])
            yield
            to_mixT(obf, 8, 8, t, l, 40, robf)
            yield

        def Z(t):
            ts = slice(t * 128, (t + 1) * 128)
            b = PSA.get(2)
            for (w_, r_, bb) in ((wz0, rz0, b), (wz1, rz1, b + 1)):
                for k in range(KC):
                    mm(psb(bb), hT[:, k, ts], w_[:, k, :], k == 0, k == KC - 1, [rhT[k], r_], [psr[bb]])
            act(zs[:, t, :], ps[:, b * 512:(b + 2) * 512], AF.Silu, [psr[b], psr[b + 1]], rzs)

        def drain(*gens):
            gens = [g for g in gens if g is not None]
            while gens:
                for g in list(gens):
                    try:
                        next(g)
                    except StopIteration:
                        gens.remove(g)

        wz0, rz0 = load_w(win3[:, :, O_Z:O_Z + 512], 16, 512)
        wz1, rz1 = load_w(win3[:, :, O_Z + 512:O_Z + 1024], 16, 512)
        Z(0)
        drain(P1(0))
        for t in range(NT):
            if t + 1 < NT:
                Z(t + 1)
            drain(P1(t + 1) if t + 1 < NT else None, P2(t))

    def out_proj(l, gate):
        wo3 = prm["w_out"][l].rearrange("(k p) n -> p k n", p=128)
        for gi in range(4):
            wo, ro = load_w(wo3[:, :, gi * 512:(gi + 1) * 512], 16, 512)
            for mi in range(4):
                m = gi * 4 + mi
                b = PSA.get()
                for k in range(KC):
                    mm(psb(b), wo[:, k, mi * 128:(mi + 1) * 128], mixT[:, k, :], k == 0, k == KC - 1, [rmix[k], ro],
                       [psr[b]])
                dve("scalar_tensor_tensor", [psr[b], rxT[m], rmod], [rxT[m]], out=xT[:, m, :], in0=psb(b),
                    scalar=gate[:, m:m + 1], in1=xT[:, m, :], op0=ALU.mult, op1=ALU.add)

    def mlp(l, gate):
        w13 = prm["w_mlp1"][l].rearrange("(k p) n -> p k n", p=128)
        for f in range(16):
            w1, r1 = load_w(w13[:, :, f * 512:(f + 1) * 512], 16, 512)
            w2, r2 = load_w(prm["w_mlp2"][l][f * 512:(f + 1) * 512, :].rearrange("(c p) n -> p c n", p=128), 4, 2048)
            hb = (f % 2) * 4
            for cc in range(4):
                b = PSA.get()
                for k in range(KC):
                    mm(psb(b), w1[:, k, cc * 128:(cc + 1) * 128], hT[:, k, :], k == 0, k == KC - 1, [rhT[k], r1],
                       [psr[b]])
                act(tmps[cc % 2][:], psb(b), AF.Relu, [psr[b]], [rtmp[cc % 2]])
                dve("tensor_tensor", [rtmp[cc % 2]], [rmix[hb + cc]], out=mixT[:, hb + cc, :], in0=tmps[cc % 2][:],
                    in1=tmps[cc % 2][:], op=ALU.mult)
            for m in range(KC):
                b = PSA.get()
                for cc in range(4):
                    mm(psb(b), w2[:, cc, m * 128:(m + 1) * 128], mixT[:, hb + cc, :], cc == 0, cc == 3,
                       [rmix[hb + cc], r2], [psr[b]])
                dve("scalar_tensor_tensor", [psr[b], rxT[m], rmod], [rxT[m]], out=xT[:, m, :], in0=psb(b),
                    scalar=gate[:, m:m + 1], in1=xT[:, m, :], op0=ALU.mult, op1=ALU.add)

    for s in range(nseq):
        for l in range(2):
            dve("memset", [], [rS[l]], ap=Sst[l][:], constant=0.0)
            dve("memset", [], [rhist[l]], ap=hist[l][:], constant=0.0)
        for blk in range(nblk):
            tok0 = blk * TB
            for t in range(NT):
                dma("sp", xin, x[s, tok0 + t * 128: tok0 + (t + 1) * 128, :], [], rxin, "xin")
                for j in range(4):
                    b = PSA.get()
                    for i4 in range(4):
                        k = j * 4 + i4
                        tr(psb(b, 128, i4 * 128), xin[:, k * 128:(k + 1) * 128], ident_f, rxin + [rcf], [psr[b]],
                           inc=(i4 == 3))
                    act(xT[:, j * 4:(j + 1) * 4, t * 128:(t + 1) * 128], psb(b).rearrange("p (i f) -> p i f", i=4),
                        AF.Copy, [psr[b]], [rxT[j * 4 + i4] for i4 in range(4)])
            for l in range(nlayer):
                i = l * 2 + s
                WS.l = l
                WS.ti = 0
                WS.first = (s == 0 and blk == 0)
                mo = modT[:, i * 96:(i + 1) * 96]
                win3 = prm["w_in"][l].rearrange("(k p) n -> p k n", p=128)
                dma("sp", lnG[:], prm["gm_ln_g"][l].rearrange("h e -> (h e)").partition_broadcast(128), [], [rln], "ln")
                dma("sp", lnB[:], prm["gm_ln_b"][l].rearrange("h e -> (h e)").partition_broadcast(128), [], [rln], "ln")
                norm_to_hT(gsc[:, i * 32:i * 32 + 16], mo[:, 0:16])
                if do_a:
                    mixer_a(l, win3)
                else:
                    for k in range(0, 4):
                        dve("memset", [], [rmix[k]], ap=mixT[:, k, :], constant=0.0)
                if do_b:
                    mixer_b(l, win3, blk * NT)
                else:
                    for k in range(4, 8):
                        dve("memset", [], [rmix[k]], ap=mixT[:, k, :], constant=0.0)
                if do_c:
                    mixer_c(l, win3)
                else:
                    for k in range(8, 16):
                        dve("memset", [], [rmix[k]], ap=mixT[:, k, :], constant=0.0)
                out_proj(l, mo[:, 32:48])
                if do_mlp:
                    norm_to_hT(gsc[:, i * 32 + 16:i * 32 + 32], mo[:, 48:64])
                    mlp(l, mo[:, 80:96])
            dma("sp", fgb, prm["final_norm_g"].partition_broadcast(128), [], rfgb, "fgb")
            for t in range(NT):
                b = PSA.get(4)
                for k in range(KC):
                    tr(ps[:, b * 512 + k * 128: b * 512 + (k + 1) * 128], xT[:, k, t * 128:(t + 1) * 128], ident_f,
                       [rxT[k], rcf], [psr[b + k // 4]], inc=(k % 4 == 3))
                pr4 = [psr[b + q] for q in range(4)]
                S.op("act", lambda e, b=b: e.activation(out=xin, in_=ps[:, b * 512:(b + 4) * 512], func=AF.Square,
                                                        accum_out=sm[:, 36:37]), pr4, rxin + [rsmf])
                act(sm[:, 37:38], sm[:, 36:37], AF.Sqrt, [rsmf, rcst], [rsmf], scale=1.0 / D, bias=EPS6)
                dve("reciprocal", [rsmf], [rsmf], out=sm[:, 37:38], in_=sm[:, 37:38])
                dve("scalar_tensor_tensor", pr4 + [rsmf] + rfgb, rxin, out=xin, in0=ps[:, b * 512:(b + 4) * 512],
                    scalar=sm[:, 37:38], in1=fgb, op0=ALU.mult, op1=ALU.mult)
                dma("sp", out[s, tok0 + t * 128: tok0 + (t + 1) * 128, :], xin, rxin, [], "xin")

    with nc.allow_non_contiguous_dma(reason="small strided parameter loads"):
        S.emit(final_waits=["xin"])
    es.close()
    return nc


def make_consts():
    c = np.zeros((128, 640), np.float32)
    i = np.arange(128)
    c[:, C_ID:C_ID + 128] = np.eye(128, dtype=np.float32)
    u = (i[:, None] <= i[None, :]).astype(np.float32)
    c[:, C_U:C_U + 128] = u
    c[:, C_NU:C_NU + 128] = -u
    c[:, C_SL:C_SL + 128] = 1.0 - u
    c[:, C_ONE:C_ONE + 128] = 1.0
    return c


_NC_CACHE = {}


def kernel(**inputs):
    n = 8
    key = "full"
    if key not in _NC_CACHE:
        _NC_CACHE[key] = build()
    nc = _NC_CACHE[key]
    x = np.ascontiguousarray(inputs["x"], dtype=np.float32)
    c = np.ascontiguousarray(inputs["c"], dtype=np.float32)
    consts = make_consts()
    params = {k: np.ascontiguousarray(inputs[k], dtype=np.float32) for k in PARAM_NAMES}
    in_maps = []
    for i in range(n):
        m = {"x": x[2 * i:2 * i + 2], "c": c[2 * i:2 * i + 2], "consts": consts}
        m.update(params)
        in_maps.append(m)
    res = run_bass_kernel_spmd(nc, in_maps, core_ids=list(range(n)))
    return np.concatenate([r["out"] for r in res.results], axis=0)
```
